# Optimizing a Trainium2 kernel written in Bass

```python
import jax, jax.numpy as jnp
from jax import lax
import numpy as np

D_MODEL = 1024
BATCH = 16
SEQ = 256
DEPTH = 4
DEC_BATCH = 8
DEC_SEQ = 2048
PAST_LEN = 256

GRID_W = 64
N_EVEN = (DEPTH + 1) // 2
N_ODD = DEPTH // 2
D_FF = 4 * D_MODEL
EPS = 1e-6
ROPE_BASE = 10000.0
CHUNK = 64
Q_BLOCK = 128
H_A = 4
DK_A = 64
DV_A = 128
GATE_RANK = 16
GATE_NORM = 16.0
H_B = 4
DK_B = 64
DV_B = 128
H_C = 8
Q_LORA = 256
KV_LORA = 256
QK_NOPE = 128
QK_ROPE = 64
V_HEAD_C = 128
A_QK = H_A * DK_A
A_V = H_A * DV_A
B_QK = H_B * DK_B
B_V = H_B * DV_B
EVEN_SPLITS = (A_QK, A_QK, A_V, A_V, 2 * GATE_RANK, B_QK, B_QK, B_V, B_V)
EVEN_IN = 2 * A_QK + 2 * A_V + 2 * GATE_RANK + 2 * B_QK + 2 * B_V
EVEN_OUT = A_V + B_V
ODD_SPLITS = (Q_LORA, KV_LORA, QK_ROPE)
ODD_IN = Q_LORA + KV_LORA + QK_ROPE

kernel_name = 'hybrid_gla_retnet_mla_diffusion_step'


def rms_norm(x, g=None):
    xf = x.astype(jnp.float32)
    y = xf * lax.rsqrt(jnp.mean(jnp.square(xf), axis=-1, keepdims=True) + EPS)
    if g is not None:
        y = y * g.astype(jnp.float32)
    return y.astype(x.dtype)


def split_cols(x, sizes):
    return jnp.split(x, np.cumsum(sizes)[:-1].tolist(), axis=-1)


def ada_mod(cond, w_ada, b_ada):
    m = jnp.einsum('bd,de->be', jax.nn.silu(cond), w_ada) + b_ada
    return jnp.split(m[:, None, :], 6, axis=-1)


def pre_norm_mod(x, g, shift, scale):
    return rms_norm(x, g) * (1.0 + scale) + shift


def post_norm_residual(x, y, g, gate):
    return x + gate * rms_norm(y, g)


def sq_relu_mlp(h, w1, w2):
    u = jax.nn.relu(jnp.einsum('bld,df->blf', h, w1))
    return jnp.einsum('blf,fd->bld', jnp.square(u), w2)


def axial_rope(rows, rot_dim):
    row = jnp.repeat(jnp.arange(rows), GRID_W).astype(jnp.float32)
    col = jnp.tile(jnp.arange(GRID_W), rows).astype(jnp.float32)
    n_freq = rot_dim // 4
    inv = ROPE_BASE ** (-jnp.arange(n_freq, dtype=jnp.float32) / n_freq)
    ang = jnp.concatenate([row[:, None] * inv, col[:, None] * inv], axis=-1)
    return jnp.cos(ang), jnp.sin(ang)


def apply_rope(x, cos, sin):
    half = x.shape[-1] // 2
    xf = x.astype(jnp.float32)
    x1, x2 = xf[..., :half], xf[..., half:]
    c, s = cos[None, :, None, :], sin[None, :, None, :]
    return jnp.concatenate([x1 * c - x2 * s, x1 * s + x2 * c], axis=-1).astype(x.dtype)


def chunked_gated_scan(q, k, v, log_a, s0):
    nb, L, H, _ = q.shape
    dv = v.shape[-1]
    n = L // CHUNK

    def to_chunks(t):
        t = t.astype(jnp.float32).reshape(nb, n, CHUNK, H, t.shape[-1])
        return jnp.transpose(t, (1, 0, 3, 2, 4))

    qc, kc, vc, gc = to_chunks(q), to_chunks(k), to_chunks(v), to_chunks(log_a)
    causal_in_chunk = jnp.tril(jnp.ones((CHUNK, CHUNK), dtype=bool))

    def step(S, inp):
        qi, ki, vi, gi = inp
        b = jnp.cumsum(gi, axis=2)
        b_last = b[:, :, -1:, :]
        q_dec = qi * jnp.exp(b)
        k_inv = ki * jnp.exp(-b)
        a = jnp.where(causal_in_chunk, jnp.einsum('bhik,bhjk->bhij', q_dec, k_inv), 0.0)
        o = jnp.einsum('bhik,bhkv->bhiv', q_dec, S) + jnp.einsum('bhij,bhjv->bhiv', a, vi)
        k_up = ki * jnp.exp(b_last - b)
        S_new = jnp.exp(b_last[:, :, 0, :])[..., None] * S + jnp.einsum('bhjk,bhjv->bhkv', k_up, vi)
        return S_new, o

    S_fin, oc = lax.scan(step, s0.astype(jnp.float32), (qc, kc, vc, gc))
    o = jnp.transpose(oc, (1, 0, 3, 2, 4)).reshape(nb, L, H, dv)
    return o.astype(q.dtype), S_fin


def bidir_scan(q, k, v, log_a_fwd, log_a_bwd, s_fwd, s_bwd):
    flip = lambda t: jnp.flip(t, axis=1)
    o_f, S_f = chunked_gated_scan(q, k, v, log_a_fwd, s_fwd)
    o_b, S_b = chunked_gated_scan(flip(q), flip(k), flip(v), flip(log_a_bwd), s_bwd)
    return o_f + flip(o_b), S_f, S_b


def even_mixer(h, w_in, w_gk2, b_gk2, gla_norm, ret_decay, w_out, s_gla, s_ret, rope):
    nb, L, _ = h.shape
    proj = jnp.einsum('bld,de->ble', h, w_in)
    qa, ka, va, ga, gk_lr, qb, kb, vb, gb = split_cols(proj, EVEN_SPLITS)
    qa = qa.reshape(nb, L, H_A, DK_A) * DK_A ** -0.5
    ka = ka.reshape(nb, L, H_A, DK_A)
    va = va.reshape(nb, L, H_A, DV_A)
    gate_pre = jnp.einsum('bldr,drk->bldk', gk_lr.reshape(nb, L, 2, GATE_RANK), w_gk2) + b_gk2
    log_a = (jax.nn.log_sigmoid(gate_pre.astype(jnp.float32)) / GATE_NORM).reshape(nb, L, 2, H_A, DK_A)
    o_a, sa_f, sa_b = bidir_scan(qa, ka, va, log_a[:, :, 0], log_a[:, :, 1], s_gla[:, 0], s_gla[:, 1])
    o_a = rms_norm(o_a, gla_norm).reshape(nb, L, A_V) * jax.nn.silu(ga)
    qb = qb.reshape(nb, L, H_B, DK_B)
    kb = kb.reshape(nb, L, H_B, DK_B) * DK_B ** -0.5
    vb = vb.reshape(nb, L, H_B, DV_B)
    if rope is not None:
        qb, kb = apply_rope(qb, *rope), apply_rope(kb, *rope)
    log_g = -jnp.exp(ret_decay.astype(jnp.float32))
    g_f = jnp.broadcast_to(log_g[0][:, None], (nb, L, H_B, DK_B))
    g_b = jnp.broadcast_to(log_g[1][:, None], (nb, L, H_B, DK_B))
    o_b, sb_f, sb_b = bidir_scan(qb, kb, vb, g_f, g_b, s_ret[:, 0], s_ret[:, 1])
    o_b = rms_norm(o_b).reshape(nb, L, B_V) * jax.nn.silu(gb)
    y = jnp.einsum('ble,ed->bld', jnp.concatenate([o_a, o_b], axis=-1), w_out)
    return y, jnp.stack([sa_f, sa_b], axis=1), jnp.stack([sb_f, sb_b], axis=1)


def blocked_attention(q, k, v):
    nb, Lq, H, dq = q.shape
    dv = v.shape[-1]
    scale = dq ** -0.5
    kf, vf = k.astype(jnp.float32), v.astype(jnp.float32)
    qblocks = jnp.moveaxis(q.reshape(nb, Lq // Q_BLOCK, Q_BLOCK, H, dq), 1, 0)

    def one_block(qi):
        s = jnp.einsum('bqhd,bkhd->bhqk', qi.astype(jnp.float32), kf) * scale
        p = jax.nn.softmax(s, axis=-1)
        return jnp.einsum('bhqk,bkhv->bqhv', p, vf).astype(q.dtype)

    out = lax.map(one_block, qblocks)
    return jnp.moveaxis(out, 0, 1).reshape(nb, Lq, H, dv)


def mla_expand(ckv, kpe, w_kv_b, rope):
    nb, L, _ = ckv.shape
    kv = jnp.einsum('blr,re->ble', ckv, w_kv_b).reshape(nb, L, H_C, QK_NOPE + V_HEAD_C)
    k_pe = kpe[:, :, None, :]
    if rope is not None:
        k_pe = apply_rope(k_pe, *rope)
    k = jnp.concatenate([kv[..., :QK_NOPE], jnp.broadcast_to(k_pe, (nb, L, H_C, QK_ROPE))], axis=-1)
    return k, kv[..., QK_NOPE:]


def mla_mixer(h, w_in, q_a_norm, w_q_b, kv_a_norm, w_kv_b, w_out, rope, ckv_ctx=None, kpe_ctx=None):
    nb, L, _ = h.shape
    q_lat, ckv, kpe = split_cols(jnp.einsum('bld,de->ble', h, w_in), ODD_SPLITS)
    q = jnp.einsum('blr,re->ble', rms_norm(q_lat, q_a_norm), w_q_b).reshape(nb, L, H_C, QK_NOPE + QK_ROPE)
    ckv = rms_norm(ckv, kv_a_norm)
    k, v = mla_expand(ckv, kpe, w_kv_b, rope)
    if rope is not None:
        q = jnp.concatenate([q[..., :QK_NOPE], apply_rope(q[..., QK_NOPE:], *rope)], axis=-1)
    if ckv_ctx is not None:
        k_c, v_c = mla_expand(ckv_ctx, kpe_ctx, w_kv_b, None)
        k = jnp.concatenate([k_c, k], axis=1)
        v = jnp.concatenate([v_c, v], axis=1)
    o = blocked_attention(q, k, v)
    y = jnp.einsum('ble,ed->bld', o.reshape(nb, L, H_C * V_HEAD_C), w_out)
    return y, ckv, kpe


def setup_inputs(seed: int = 0) -> dict:
    key = jax.random.key(seed)
    ks = iter(jax.random.split(key, 32))
    nrm = lambda shape, scale: jax.random.normal(next(ks), shape, jnp.float32) * scale
    gain = lambda shape: 1.0 + nrm(shape, 0.05)
    decay_init = jnp.log(-jnp.log1p(-(2.0 ** (-5.0 - jnp.arange(H_B, dtype=jnp.float32)))))
    return {
        'x_prompt': nrm((BATCH, SEQ, D_MODEL), 1.0),
        'x_sample': nrm((DEC_BATCH, DEC_SEQ, D_MODEL), 1.0),
        'cache_ckv': nrm((DEC_BATCH, N_ODD, PAST_LEN, KV_LORA), 1.0),
        'cache_kpe': nrm((DEC_BATCH, N_ODD, PAST_LEN, QK_ROPE), 1.0),
        'state_gla': nrm((DEC_BATCH, N_EVEN, 2, H_A, DK_A, DV_A), 1.0),
        'state_ret': nrm((DEC_BATCH, N_EVEN, 2, H_B, DK_B, DV_B), 1.0),
        'c': nrm((DEC_BATCH, D_MODEL), 1.0),
        'c_ctx': nrm((D_MODEL,), 1.0),
        'w_ada': nrm((DEPTH, D_MODEL, 6 * D_MODEL), 0.5 * D_MODEL ** -0.5),
        'b_ada': nrm((DEPTH, 6 * D_MODEL), 0.02),
        'norm_mix_pre': gain((DEPTH, D_MODEL)),
        'norm_mix_post': gain((DEPTH, D_MODEL)),
        'norm_mlp_pre': gain((DEPTH, D_MODEL)),
        'norm_mlp_post': gain((DEPTH, D_MODEL)),
        'w_in_even': nrm((N_EVEN, D_MODEL, EVEN_IN), D_MODEL ** -0.5),
        'w_gk2': nrm((N_EVEN, 2, GATE_RANK, A_QK), GATE_RANK ** -0.5),
        'b_gk2': nrm((N_EVEN, 2, A_QK), 0.1),
        'gla_norm': gain((N_EVEN, DV_A)),
        'ret_decay': decay_init + nrm((N_EVEN, 2, H_B), 0.05),
        'w_out_even': nrm((N_EVEN, EVEN_OUT, D_MODEL), EVEN_OUT ** -0.5),
        'w_in_odd': nrm((N_ODD, D_MODEL, ODD_IN), D_MODEL ** -0.5),
        'q_a_norm': gain((N_ODD, Q_LORA)),
        'w_q_b': nrm((N_ODD, Q_LORA, H_C * (QK_NOPE + QK_ROPE)), Q_LORA ** -0.5),
        'kv_a_norm': gain((N_ODD, KV_LORA)),
        'w_kv_b': nrm((N_ODD, KV_LORA, H_C * (QK_NOPE + V_HEAD_C)), KV_LORA ** -0.5),
        'w_out_odd': nrm((N_ODD, H_C * V_HEAD_C, D_MODEL), (H_C * V_HEAD_C) ** -0.5),
        'w_mlp1': nrm((DEPTH, D_MODEL, D_FF), D_MODEL ** -0.5),
        'w_mlp2': nrm((DEPTH, D_FF, D_MODEL), D_FF ** -0.5),
    }


def reference(x_prompt, x_sample, cache_ckv, cache_kpe, state_gla, state_ret, c, c_ctx,
              w_ada, b_ada, norm_mix_pre, norm_mix_post, norm_mlp_pre, norm_mlp_post,
              w_in_even, w_gk2, b_gk2, gla_norm, ret_decay, w_out_even,
              w_in_odd, q_a_norm, w_q_b, kv_a_norm, w_kv_b, w_out_odd,
              w_mlp1, w_mlp2):
    n_lat = x_sample.shape[1]
    rows = n_lat // GRID_W
    rope_ret = axial_rope(rows, DK_B)
    rope_mla = axial_rope(rows, QK_ROPE)
    nb_ctx = x_prompt.shape[0]
    zero_gla = jnp.zeros((nb_ctx, 2, H_A, DK_A, DV_A), jnp.float32)
    zero_ret = jnp.zeros((nb_ctx, 2, H_B, DK_B, DV_B), jnp.float32)

    xc, xl = x_prompt, x_sample
    new_ckv, new_kpe, new_gla, new_ret = [], [], [], []
    for l in range(DEPTH):
        i = l // 2
        sh1c, sc1c, gt1c, sh2c, sc2c, gt2c = ada_mod(c_ctx[None, :], w_ada[l], b_ada[l])
        sh1l, sc1l, gt1l, sh2l, sc2l, gt2l = ada_mod(c, w_ada[l], b_ada[l])
        hc = pre_norm_mod(xc, norm_mix_pre[l], sh1c, sc1c)
        hl = pre_norm_mod(xl, norm_mix_pre[l], sh1l, sc1l)
        if l % 2 == 0:
            ew = (w_in_even[i], w_gk2[i], b_gk2[i], gla_norm[i], ret_decay[i], w_out_even[i])
            yc, sg, sr = even_mixer(hc, *ew, zero_gla, zero_ret, None)
            yl, _, _ = even_mixer(hl, *ew, state_gla[:, i], state_ret[:, i], rope_ret)
            new_gla.append(sg)
            new_ret.append(sr)
        else:
            ow = (w_in_odd[i], q_a_norm[i], w_q_b[i], kv_a_norm[i], w_kv_b[i], w_out_odd[i])
            yc, ckv_c, kpe_c = mla_mixer(hc, *ow, None)
            yl, _, _ = mla_mixer(hl, *ow, rope_mla, cache_ckv[:, i], cache_kpe[:, i])
            new_ckv.append(ckv_c)
            new_kpe.append(kpe_c)
        xc = post_norm_residual(xc, yc, norm_mix_post[l], gt1c)
        xl = post_norm_residual(xl, yl, norm_mix_post[l], gt1l)
        hc = pre_norm_mod(xc, norm_mlp_pre[l], sh2c, sc2c)
        hl = pre_norm_mod(xl, norm_mlp_pre[l], sh2l, sc2l)
        xc = post_norm_residual(xc, sq_relu_mlp(hc, w_mlp1[l], w_mlp2[l]), norm_mlp_post[l], gt2c)
        xl = post_norm_residual(xl, sq_relu_mlp(hl, w_mlp1[l], w_mlp2[l]), norm_mlp_post[l], gt2l)

    return (xc, xl, jnp.stack(new_ckv, axis=1), jnp.stack(new_kpe, axis=1),
            jnp.stack(new_gla, axis=1), jnp.stack(new_ret, axis=1))
```

```python
import os
from contextlib import ExitStack
import numpy as np
import concourse.bass as bass
import concourse.mybir as mybir
from concourse.bass_utils import run_bass_kernel_spmd

F32 = mybir.dt.float32
BF16 = mybir.dt.bfloat16
AF = mybir.ActivationFunctionType
ALU = mybir.AluOpType

P = 128
D = 1024
KC = 8
TT = 512
NT = 5
NTOK = NT * TT
DEPTH = 4
DFF = 4096
EPS = 1e-6
LS = 2048
LP = 256
NCH = NTOK // 64

R_BADA = 0
R_NMIXPRE = 192
R_NMIXPOST = 224
R_NMLPPRE = 256
R_NMLPPOST = 288
R_COND = 320
R_QAN = 336
R_KVAN = 340
R_GLAN = 344
R_BGK = 346
R_TOT = 384


class Buf:
    __slots__ = ("name", "w", "r", "sem", "cnt")

    def __init__(self, name):
        self.name = name
        self.w = None
        self.r = {}
        self.sem = None
        self.cnt = 0


class Sched:
    ENG = ("pe", "act", "dve", "pool", "sp")

    def __init__(self, nc, stack):
        self.nc = nc
        self.stack = stack
        self.ops = {e: [] for e in self.ENG}
        self.cnt = {e: 0 for e in self.ENG}
        self.waited = {e: {} for e in self.ENG}
        self.sems = {}
        for e in self.ENG:
            self.sems[e] = stack.enter_context(nc.semaphore("s_" + e))
        self.ndsem = 0
        self.dma_bufs = []

    def _deps(self, eng, reads, writes, lhs=None):
        d = {}
        dr = {}
        for b in (reads if lhs is None else lhs):
            if b.w is not None:
                k, v = b.w
                if dr.get(k, 0) < v:
                    dr[k] = v
        for b in reads:
            if b.w is not None:
                k, v = b.w
                if d.get(k, 0) < v:
                    d[k] = v
        for b in writes:
            if b.w is not None:
                k, v = b.w
                if d.get(k, 0) < v:
                    d[k] = v
            for k, v in b.r.items():
                if d.get(k, 0) < v:
                    d[k] = v
        out = []
        wd = self.waited[eng]
        for k, v in d.items():
            if k == eng and eng == "pe":
                continue
            if wd.get(k, 0) >= v:
                continue
            wonly = dr.get(k, 0) <= wd.get(k, 0)
            wd[k] = v
            out.append((k, v, wonly))
        return out

    def op(self, eng, meth, args, kw, reads=(), writes=(), lhs=None):
        deps = self._deps(eng, reads, writes, lhs)
        self.cnt[eng] += 1
        idx = self.cnt[eng]
        self.ops[eng].append((deps, (meth, args, kw), (eng, 1)))
        for b in reads:
            b.r[eng] = idx
        for b in writes:
            b.w = (eng, idx)
            b.r = {}

    def dma(self, q, out, in_, reads, writes, sb):
        fn = ("dma_start", (), dict(out=out, in_=in_))
        deps = self._deps(q, reads, writes)
        if sb.sem is None:
            sb.sem = "d%d" % self.ndsem
            self.ndsem += 1
            self.sems[sb.sem] = self.stack.enter_context(self.nc.semaphore(sb.sem))
            self.dma_bufs.append(sb)
        sb.cnt += 16
        self.ops[q].append((deps, fn, (sb.sem, 16)))
        for b in reads:
            b.r[sb.sem] = sb.cnt
        for b in writes:
            b.w = (sb.sem, sb.cnt)
            b.r = {}

    def final_wait(self, eng, bufs):
        deps = self._deps(eng, bufs, bufs)
        self.ops[eng].append((deps, None, None))

    def barrier(self):
        toks = [(k, self.cnt[k]) for k in self.ENG if self.cnt[k] > 0]
        toks += [(b.sem, b.cnt) for b in self.dma_bufs]
        for e in self.ENG:
            wd = self.waited[e]
            deps = []
            for k, v in toks:
                if k == e:
                    continue
                if wd.get(k, 0) >= v:
                    continue
                wd[k] = v
                deps.append((k, v))
            if deps:
                self.ops[e].append((deps, None, None))

    def emit(self, eng, e):
        for deps, fn, inc in self.ops[eng]:
            if fn is None or fn[0] == "dma_start" or eng in ("sp", "pool"):
                for d_ in deps:
                    e.wait_ge(self.sems[d_[0]], d_[1])
                if fn is None:
                    continue
                meth, args, kw = fn
                ins = getattr(e, meth)(*args, **kw)
            else:
                fus = None
                for j in range(len(deps) - 1, -1, -1):
                    if eng != "pe" or (len(deps[j]) > 2 and deps[j][2]):
                        fus = j
                        break
                for j, d_ in enumerate(deps):
                    if j != fus:
                        e.wait_ge(self.sems[d_[0]], d_[1])
                meth, args, kw = fn
                ins = getattr(e, meth)(*args, **kw)
                if fus is not None:
                    ins._wait_ge(self.sems[deps[fus][0]], deps[fus][1])
            ins.then_inc(self.sems[inc[0]], inc[1])
def build(nl=DEPTH, mix="all"):
    nc = bass.Bass("TRN2", target_bir_lowering=False)
    stack = ExitStack()
    S = Sched(nc, stack)

    def din(name, shape, dt=F32):
        return nc.dram_tensor(name, list(shape), dt, kind="ExternalInput").ap()

    def dout(name, shape, dt=F32):
        return nc.dram_tensor(name, list(shape), dt, kind="ExternalOutput").ap()

    xin = din("xin", [NTOK, D])
    vecs = din("vecs", [R_TOT, P])
    identf = din("identf", [P, P])
    w_ada = din("w_ada", [DEPTH, D, 6 * D])
    w_mlp1 = din("w_mlp1", [DEPTH, D, DFF])
    w_mlp2 = din("w_mlp2", [DEPTH, DFF, D])
    w_in_odd = din("w_in_odd", [2, D, 576])
    w_q_b = din("w_q_b", [2, 256, 1536])
    w_kv_b = din("w_kv_b", [2, 256, 2048])
    w_out_odd = din("w_out_odd", [2, D, D])
    w_out_even = din("w_out_even", [2, D, D])
    cckv = din("cckv", [2, 256, 256])
    ckpe = din("ckpe", [2, 256, 64])
    ropeC_d = din("ropeC", [P, LS])
    ropeS_d = din("ropeS", [P, LS])
    w_in_even = din("w_in_even", [2, D, 3104])
    w_gk2 = din("w_gk2", [2, 2, 16, 256])
    sgla = din("sgla", [2, 2, 4, 64, 128])
    sret = din("sret", [2, 2, 4, 64, 128])
    rdec_d = din("rdec", [P, 8])
    iot_d = din("iot", [P, 4, 64])
    m4_d = din("m4", [P, 256])
    resetm_d = din("resetm", [P, TT])
    ogla = dout("ogla", [2, 2, 2, 4, 64, 128])
    oret = dout("oret", [2, 2, 2, 4, 64, 128])
    ys = dout("ys", [NTOK, D])
    ockv = dout("ockv", [2, 2, LP, 256])
    okpe = dout("okpe", [2, 2, LP, 64])
    xs = nc.dram_tensor("xs", [NT, P, KC, TT], F32).ap()
    xsb = [Buf("xs%d" % t) for t in range(NT)]

    def sb(name, shape, dt):
        return stack.enter_context(nc.sbuf_tensor(name, list(shape), dt))

    R0 = sb("R0", [P, KC, NTOK], BF16)
    R12 = sb("R12", [P, KC, NTOK], F32)
    WA = sb("WA", [P, 14336], F32)
    ident = sb("ident", [P, P], F32)
    identb = sb("identb", [P, P], BF16)
    onesb = sb("onesb", [P, P], BF16)
    onesf32 = sb("onesf32", [P, P], F32)
    B_onesf32 = Buf("onesf32")
    vT = sb("vT", [P, R_TOT], F32)
    vrow = sb("vrow", [P, 3, P], F32)
    sc = sb("sc", [P, KC, 2], F32)
    scb = sb("scb", [P, KC, 2], BF16)
    wad = sb("wad", [P, 2, KC, P], BF16)
    B_wad = [Buf("wad0"), Buf("wad1")]
    mod = sb("mod", [P, DEPTH, 48, 2], F32)
    gvec = sb("gvec", [P, DEPTH, 4, KC, 2], F32)
    rs1 = sb("rs1", [P, TT], F32)
    rstd = sb("rstd", [P, TT], F32)
    rstd2 = sb("rstd2", [P, TT], F32)
    epsb = sb("epsb", [P, 1], F32)
    ropeC = sb("ropeC_s", [P, LS], BF16)
    ropeS = sb("ropeS_s", [P, LS], BF16)
    rdec = sb("rdec_s", [P, 8], F32)
    lgn = sb("lgn", [P, 8], F32)
    lgm = sb("lgm", [P, 8], F32)
    iot = sb("iot_s", [P, 4, 64], F32)
    m4 = sb("m4_s", [P, 256], BF16)
    resetm = sb("resetm_s", [P, TT], BF16)
    onesf = sb("onesf", [P, 1], F32)
    negb = sb("negb", [P, 8], F32)
    Dch = sb("Dch", [P, 2, NCH], F32)
    rtab = sb("rtab", [P, 2, 3, 64], F32)
    Dret = sb("Dret", [P, 2], F32)
    B_ec = Buf("evenconst")
    sq2 = sb("sq2", [P, 2, TT], BF16)
    B_rope, B_sq2 = Buf("rope"), Buf("sq2")

    def wv(off, n, dt=F32):
        a = WA[:, off:off + n]
        return a.bitcast(BF16) if dt == BF16 else a

    def v3(a, k):
        return a.rearrange("p (k t) -> p k t", k=k)

    xa = [v3(wv(i * 4096, 4096), KC) for i in range(2)]
    tmpf = v3(wv(8192, 4096), KC)
    stv = wv(8192, 4096).rearrange("p (s d) -> p s d", s=4)
    sqb = v3(wv(12288, 2048, BF16), KC)
    wstf = [v3(wv(i * 2048, 2048), KC) for i in range(2)]
    w1v = [v3(wv(i * 2048, 2048, BF16), KC) for i in range(2)]
    w2st = [v3(wv(4096 + i * 2048, 2048, BF16), 4) for i in range(2)]
    ub = [v3(wv(8192 + i * 1024, 1024, BF16), 4) for i in range(2)]
    rb = [wv(10240 + i * 256, 256, BF16) for i in range(2)]
    yacc = R12

    B_R0 = [[Buf("R0_%d_%d" % (t, k)) for k in range(KC)] for t in range(NT)]
    B_Y = [[Buf("Y_%d_%d" % (t, k)) for k in range(KC)] for t in range(NT)]
    B_ident, B_identb, B_ones, B_vT, B_vrow, B_sc, B_mod, B_g, B_eps = (
        Buf(n) for n in ("ident", "identb", "ones", "vT", "vrow", "sc", "mod", "gv", "eps"))
    B_xa = [[Buf("xa%d_%d" % (i, k)) for k in range(KC)] for i in range(2)]
    B_tmpf = [Buf("tmpf%d" % k) for k in range(KC)]
    B_sqb = [Buf("sqb%d" % k) for k in range(KC)]
    B_rs1, B_rstd = Buf("rs1"), Buf("rstd")
    B_rstd2 = Buf("rstd2")
    rsel = {"n": 0}
    B_xadma = [Buf("xadma0"), Buf("xadma1")]
    B_stg = Buf("stgdma")
    B_wst = [Buf("wst0"), Buf("wst1")]
    B_wsta = [Buf("wsta0"), Buf("wsta1")]
    B_w2st = [Buf("w2st0"), Buf("w2st1")]
    B_ub = [Buf("ub0"), Buf("ub1")]
    B_rb = [Buf("rb0"), Buf("rb1")]

    psb = [stack.enter_context(nc.psum_tensor("ps%d" % i, [P, 512], F32)) for i in range(8)]
    B_ps = [Buf("ps%d" % i) for i in range(8)]
    rr = {"A": 0, "B": 0, "ev": 0}

    def bankA():
        i = rr["A"]
        rr["A"] = (i + 1) % 4
        return i

    def bankB():
        i = 4 + rr["B"]
        rr["B"] = (rr["B"] + 1) % 4
        return i

    def evac_eng():
        rr["ev"] ^= 1
        return "act" if rr["ev"] else "dve"

    def mm(out, lhsT, rhs, start, stop, reads, writes, lhs=None):
        S.op("pe", "matmul", (out,), dict(lhsT=lhsT, rhs=rhs, start=start, stop=stop), reads, writes,
             lhs=([reads[0]] if lhs is None else lhs))

    def tr(out, in_, idn, reads, writes):
        S.op("pe", "transpose", (out, in_, idn), {}, reads, writes)

    def act(out, in_, func, reads, writes, **kw):
        S.op("act", "activation", (out, in_, func), kw, reads, writes)

    def tt(eng, out, in0, in1, op, reads, writes):
        S.op(eng, "tensor_tensor", (out, in0, in1, op), {}, reads, writes)

    def stt(out, in0, scalar, in1, op0, op1, reads, writes):
        S.op("dve", "scalar_tensor_tensor", (out, in0, scalar, in1, op0, op1), {}, reads, writes)

    def ts(eng, out, in0, s1, s2, op0, op1, reads, writes):
        S.op(eng, "tensor_scalar", (out, in0, s1, s2, op0, op1), {}, reads, writes)

    def copy(eng, out, in_, reads, writes):
        if eng == "act":
            act(out, in_, AF.Copy, reads, writes)
        else:
            S.op(eng, "tensor_copy", (out, in_), {}, reads, writes)

    S.dma("sp", ident[:], identf[:, :], [], [B_ident], B_ident)
    copy("dve", identb[:], ident[:], [B_ident], [B_identb])
    S.op("pool", "memset", (onesb[:], 1.0), {}, [], [B_ones])
    S.op("pool", "memset", (onesf32[:], 1.0), {}, [], [B_onesf32])
    S.op("pool", "memset", (epsb[:], EPS), {}, [], [B_eps])

    S.dma("pool", ropeC[:], ropeC_d[:, :], [], [B_rope], B_rope)
    S.dma("pool", ropeS[:], ropeS_d[:, :], [], [B_rope], B_rope)
    S.dma("pool", m4[:], m4_d[:, :], [], [B_ec], B_ec)
    S.dma("pool", resetm[:], resetm_d[:, :], [], [B_ec], B_ec)
    B_ec2 = Buf("evenconst2")
    S.dma("sp", rdec[:], rdec_d[:, :], [], [B_ec2], B_ec2)
    S.dma("sp", iot[:], iot_d[:, :, :], [], [B_ec2], B_ec2)
    S.op("pool", "memset", (onesf[:], 1.0), {}, [], [B_ec2])
    act(lgn[:], rdec[:], AF.Exp, [B_ec2], [B_ec2])
    ts("dve", lgm[:], lgn[:], -1.0, None, ALU.mult, ALU.bypass, [B_ec2], [B_ec2])

    S.dma("sp", vrow[:], vecs.rearrange("(a p) f -> p a f", p=P), [], [B_vrow], B_vrow)
    for a in range(3):
        bk = bankA()
        tr(psb[bk][:, 0:P], vrow[:, a, :], ident[:], [B_vrow, B_ident], [B_ps[bk]])
        copy("dve", vT[:, a * P:(a + 1) * P], psb[bk][:, 0:P], [B_ps[bk]], [B_vT])
    for c in range(2):
        act(sc[:, :, c], vT[:, R_COND + c * 8:R_COND + c * 8 + 8], AF.Silu, [B_vT], [B_sc])

    copy("dve", scb[:], sc[:], [B_sc], [B_sc])
    adac = {"n": 0}

    ada_st = {}

    def ada_dma(l):
        st = ada_st.setdefault(l, {"dma": 0, "mm": 0})
        pc = st["dma"]
        if pc >= 48:
            return
        i = pc % 2
        st["dma"] += 1
        S.dma("pool", wad[:, i], w_ada[l, :, pc * P:(pc + 1) * P].rearrange("(k p) f -> p k f", p=P),
              [], [B_wad[i]], B_wad[i])

    def ada_mm(l):
        st = ada_st.setdefault(l, {"dma": 0, "mm": 0})
        pc = st["mm"]
        if pc >= 48:
            return
        if st["dma"] <= pc:
            ada_dma(l)
        i = pc % 2
        st["mm"] += 1
        bk = bankA()
        for k in range(KC):
            mm(psb[bk][:, 0:2], wad[:, i, k, :], scb[:, k, :], k == 0, k == KC - 1, [B_wad[i], B_sc], [B_ps[bk]])
        tt("dve", mod[:, l, pc, :], psb[bk][:, 0:2],
           vT[:, R_BADA + l * 48 + pc:R_BADA + l * 48 + pc + 1].to_broadcast([P, 2]), ALU.add,
           [B_ps[bk], B_vT], [B_mod])
        if st["dma"] - st["mm"] < 1:
            ada_dma(l)

    def ada_piece(l, pc):
        ada_mm(l)

    def rms_stats(src, B_src, nk, dfeat, sq=None, B_sq=None):
        if sq is None:
            sq, B_sq = sqb, B_sqb
        bk = bankA()
        for k in range(nk):
            bs_ = B_src[k] if isinstance(B_src, list) else B_src
            bq_ = B_sq[k] if isinstance(B_sq, list) else B_sq
            act(sq[:, k, :], src[:, k, :], AF.Square, [bs_], [bq_])
            mm(psb[bk][:], onesb[:], sq[:, k, :], k == 0, k == nk - 1, [B_ones, bq_], [B_ps[bk]])
        rsel["n"] ^= 1
        rr_, Br_ = (rstd, B_rstd) if rsel["n"] else (rstd2, B_rstd2)
        act(rs1[:], psb[bk][:], AF.Ln, [B_ps[bk], B_eps], [B_rs1], bias=epsb[:, 0:1], scale=1.0 / dfeat)
        act(rr_[:], rs1[:], AF.Exp, [B_rs1], [Br_], scale=-0.5)
        return rr_, Br_

    def pre_apply(xt, B_xt, rr_, Br_, gs, sh_j0, l, cidx, out3, B_out):
        for k in range(KC):
            sc_ap = gs[:, k, cidx:cidx + 1]
            bi_ap = mod[:, l, sh_j0 + k, cidx:cidx + 1]
            tt("dve", tmpf[:, k, :], xt[:, k, :], rr_[:], ALU.mult, [B_xt[k], Br_], [B_tmpf[k]])
            if k % 4 == 3:
                ts("dve", out3[:, k, :], tmpf[:, k, :], sc_ap, bi_ap, ALU.mult, ALU.add, [B_tmpf[k], B_g, B_mod], [B_out[k]])
            else:
                act(out3[:, k, :], tmpf[:, k, :], AF.Identity, [B_tmpf[k], B_g, B_mod], [B_out[k]], bias=bi_ap, scale=sc_ap)

    def prenorm(xt, B_xt, gs, sh_j0, l, cidx, out3, B_out):
        rr_, Br_ = rms_stats(xt, B_xt, KC, D)
        pre_apply(xt, B_xt, rr_, Br_, gs, sh_j0, l, cidx, out3, B_out)

    def post_apply(y3, B_y, rr_, Br_, gg, cidx, xt, B_xt):
        for k in range(KC):
            tt("dve", tmpf[:, k, :], y3[:, k, :], rr_[:], ALU.mult, [B_y[k], Br_], [B_tmpf[k]])
            stt(xt[:, k, :], tmpf[:, k, :], gg[:, k, cidx:cidx + 1], xt[:, k, :], ALU.mult, ALU.add,
                [B_tmpf[k], B_g, B_xt[k]], [B_xt[k]])

    def postnorm_res(y3, B_y, gg, cidx, xt, B_xt):
        rr_, Br_ = rms_stats(y3, B_y, KC, D)
        post_apply(y3, B_y, rr_, Br_, gg, cidx, xt, B_xt)

    def norm_pipeline(get_y, gg, pre_args, final, ahead=False):
        pend = None
        ynext = get_y(0) if ahead else None
        for t in range(NT):
            i = t % 2
            load_x(t, i)
            y3, B_y = ynext if ahead else get_y(t)
            ra, Bra = rms_stats(y3, B_y, KC, D)
            if pend is not None:
                pend()
            post_apply(y3, B_y, ra, Bra, gg, cidx_of(t), xa[i], B_xa[i])
            if ahead and t + 1 < NT:
                ynext = get_y(t + 1)
            if final:
                pend = (lambda t=t, i=i: store_y(t, i))
            else:
                store_x(t, i)
                rc_, Brc = rms_stats(xa[i], B_xa[i], KC, D)
                gs_, sh_j0, l_ = pre_args
                pend = (lambda t=t, i=i, rc_=rc_, Brc=Brc: pre_apply(
                    xa[i], B_xa[i], rc_, Brc, gs_, sh_j0, l_, cidx_of(t), R0[:, :, t * TT:(t + 1) * TT], B_R0[t]))
        pend()

    B_stg0 = [Buf("stg0_%d" % t) for t in range(NT)]

    def stage_x0(t):
        st0 = R12[:].rearrange("p k t -> p (k t)")[:, t * 4096:(t + 1) * 4096].rearrange("p (s d) -> p s d", s=4)
        S.dma("sp", st0, xin[t * TT:(t + 1) * TT, :].rearrange("(s p) d -> p s d", p=P), [], [B_stg0[t]], B_stg0[t])

    def load_x0(t, i):
        st0 = R12[:].rearrange("p k t -> p (k t)")[:, t * 4096:(t + 1) * 4096].rearrange("p (s d) -> p s d", s=4)
        for k in range(KC):
            bk = bankA()
            for s_ in range(4):
                tr(psb[bk][:, s_ * P:(s_ + 1) * P], st0[:, s_, k * P:(k + 1) * P], ident[:],
                   [B_stg0[t], B_ident], [B_ps[bk]])
            copy(evac_eng(), xa[i][:, k, :], psb[bk][:], [B_ps[bk]], [B_xa[i][k]])

    def store_y(t, i):
        for s_ in range(4):
            for hh in range(2):
                bk = bankA()
                for kk in range(4):
                    k = hh * 4 + kk
                    tr(psb[bk][:, kk * P:(kk + 1) * P], xa[i][:, k, s_ * P:(s_ + 1) * P], ident[:],
                       [B_xa[i][k], B_ident], [B_ps[bk]])
                copy(evac_eng(), stv[:, s_, hh * 512:(hh + 1) * 512], psb[bk][:], [B_ps[bk]], B_tmpf)
        S.dma("sp", ys[t * TT:(t + 1) * TT, :].rearrange("(s p) d -> p s d", p=P), stv, B_tmpf, [], B_stg)

    def load_x(t, i):
        S.dma("sp", xa[i], xs[t], [xsb[t]], B_xa[i], B_xadma[i])

    def store_x(t, i):
        S.dma("sp", xs[t], xa[i], B_xa[i], [xsb[t]], B_xadma[i])

    def layer_vecs(l):
        for c in range(2):
            stt(gvec[:, l, 0, :, c], mod[:, l, 8:16, c], 1.0, vT[:, R_NMIXPRE + l * 8:R_NMIXPRE + l * 8 + 8],
                ALU.add, ALU.mult, [B_mod, B_vT], [B_g])
            tt("dve", gvec[:, l, 1, :, c], mod[:, l, 16:24, c], vT[:, R_NMIXPOST + l * 8:R_NMIXPOST + l * 8 + 8],
               ALU.mult, [B_mod, B_vT], [B_g])
            stt(gvec[:, l, 2, :, c], mod[:, l, 32:40, c], 1.0, vT[:, R_NMLPPRE + l * 8:R_NMLPPRE + l * 8 + 8],
                ALU.add, ALU.mult, [B_mod, B_vT], [B_g])
            tt("dve", gvec[:, l, 3, :, c], mod[:, l, 40:48, c], vT[:, R_NMLPPOST + l * 8:R_NMLPPOST + l * 8 + 8],
               ALU.mult, [B_mod, B_vT], [B_g])

    def GS1(l): return gvec[:, l, 0]
    def GG1(l): return gvec[:, l, 1]
    def GS2(l): return gvec[:, l, 2]
    def GG2(l): return gvec[:, l, 3]

    def mlp(l):
        nu = 0
        for g in range(8):
            i = g % 2
            S.dma("pool", w1v[i], w_mlp1[l, :, g * 512:(g + 1) * 512].rearrange("(k p) f -> p k f", p=P),
                  [], [B_wst[i]], B_wst[i])
            S.dma("pool", w2st[i], w_mlp2[l, g * 512:(g + 1) * 512, :].rearrange("(c p) d -> p c d", p=P),
                  [], [B_w2st[i]], B_w2st[i])

            def do_u(t, ui):
                for fc in range(4):
                    bk = bankA()
                    for k in range(KC):
                        mm(psb[bk][:], w1v[i][:, k, fc * P:(fc + 1) * P], R0[:, k, t * TT:(t + 1) * TT],
                           k == 0, k == KC - 1, [B_wst[i], B_R0[t][k]], [B_ps[bk]])
                    ri = fc % 2
                    act(rb[ri], psb[bk][:], AF.Relu, [B_ps[bk]], [B_rb[ri]])
                    tt("dve", ub[ui][:, fc, :], rb[ri], rb[ri], ALU.mult, [B_rb[ri]], [B_ub[ui]])

            def do_y(t, ui):
                for dc in range(KC):
                    bk = bankB()
                    for fc in range(4):
                        mm(psb[bk][:], w2st[i][:, fc, dc * P:(dc + 1) * P], ub[ui][:, fc, :],
                           fc == 0, fc == 3, [B_w2st[i], B_ub[ui]], [B_ps[bk]])
                    dst = yacc[:, dc, t * TT:(t + 1) * TT]
                    if g == 0:
                        copy("act", dst, psb[bk][:], [B_ps[bk]], [B_Y[t][dc]])
                    else:
                        tt("dve", dst, psb[bk][:], dst, ALU.add, [B_ps[bk], B_Y[t][dc]], [B_Y[t][dc]])

            do_u(0, nu % 2)
            for t in range(NT):
                if t + 1 < NT:
                    do_u(t + 1, (nu + 1) % 2)
                do_y(t, nu % 2)
                nu += 1


    R12f = R12[:].rearrange("p k t -> p (k t)")
    OT = R12f[:, 0:10240].bitcast(BF16).rearrange("p (k t) -> p k t", k=KC)
    B_OT = [[Buf("OT%d_%d" % (t, k)) for k in range(KC)] for t in range(NT)]
    R2w = R12f[:, 10240:20480]

    def r2(off, n, dt=F32):
        a = R2w[:, off:off + n]
        return a.bitcast(BF16) if dt == BF16 else a

    SCQ = float(192 ** -0.5)

    def mla(l):
        i = l // 2
        win = v3(wv(0, 2560, BF16), KC)
        wq = v3(wv(2560, 2048, BF16), 2)
        wkv = v3(wv(4608, 2048, BF16), 2)
        kpeT = wv(6656, 1408, BF16)
        kTh = wv(8064, 1408, BF16)
        Vh = v3(wv(9472, 1408, BF16), 22)
        qnh = wv(10880, 1280, BF16)
        qrh = wv(12160, 1280, BF16)
        pT = [wv(13440 + j * 256, 256, BF16) for j in range(3)]
        qlatn = v3(r2(0, 2560, BF16), 2)
        ckvall = v3(r2(2560, 2816, BF16), 2)
        tq = v3(r2(5376, 1024), 2)
        tc = v3(r2(6400, 1024), 2)
        tk = r2(7424, 512)
        tkp = r2(7936, 512)
        tmp2 = v3(r2(8448, 1024), 2)
        stgo = r2(8448, 1024).rearrange("p (s f) -> p s f", s=4)
        cst = r2(9472, 512).rearrange("p (s f) -> p s f", s=2)
        kst = r2(9984, 128).rearrange("p (s f) -> p s f", s=2)
        ksto = r2(9472, 256).rearrange("p (s f) -> p s f", s=4)
        B_win, B_wq, B_wkv, B_kpe, B_kT, B_V, B_qn, B_qr = (Buf(n) for n in
                                                           ("win", "wq", "wkv", "kpeT", "kTh", "Vh", "qnh", "qrh"))
        B_pT = [Buf("pT%d" % j) for j in range(3)]
        pacc = r2(8448, 512)
        B_qlat = [Buf("qlat%d" % t) for t in range(NT)]
        B_ckv = [Buf("ckvall%d" % t) for t in range(NT + 1)]
        B_tq, B_tc, B_tk, B_tkp, B_tmp2, B_cst, B_kst = (Buf(n) for n in ("tq", "tc", "tk", "tkp", "tmp2", "cst", "kst"))
        B_pacc = B_tmp2

        S.dma("pool", win[:, :, 0:576], w_in_odd[i].rearrange("(k p) f -> p k f", p=P), [], [B_win], B_win)
        S.dma("pool", win[:, :, 576:608], w_in_odd[i, :, 544:576].rearrange("(k p) f -> p k f", p=P), [], [B_win], B_win)
        S.dma("pool", win[:, :, 608:640], w_in_odd[i, :, 512:544].rearrange("(k p) f -> p k f", p=P), [], [B_win], B_win)
        S.dma("pool", wq[:, :, 0:1536], w_q_b[i].rearrange("(k p) f -> p k f", p=P), [], [B_wq], B_wq)
        wqp = wq[:, :, 1536:2048].rearrange("p k (h d) -> p k h d", d=64)
        wqs = w_q_b[i].rearrange("(k p) (h d) -> p k h d", p=P, d=192)
        for kk in range(2):
            S.dma("pool", wqp[:, kk, :, 0:32], wqs[:, kk, :, 160:192], [], [B_wq], B_wq)
            S.dma("pool", wqp[:, kk, :, 32:64], wqs[:, kk, :, 128:160], [], [B_wq], B_wq)
        S.dma("pool", wkv, w_kv_b[i].rearrange("(k p) f -> p k f", p=P), [], [B_wkv], B_wkv)

        S.dma("sp", cst, cckv[i].rearrange("(s p) f -> p s f", p=P), [], [B_cst], B_cst)
        S.dma("sp", kst, ckpe[i].rearrange("(s p) f -> p s f", p=P), [], [B_kst], B_kst)
        for kc in range(2):
            bk = bankA()
            for s_ in range(2):
                tr(psb[bk][:, s_ * P:(s_ + 1) * P], cst[:, s_, kc * P:(kc + 1) * P], ident[:], [B_cst, B_ident], [B_ps[bk]])
            copy(evac_eng(), ckvall[:, kc, 0:256], psb[bk][:, 0:256], [B_ps[bk]], [B_ckv[0]])
        bk = bankA()
        for s_ in range(2):
            tr(psb[bk][0:64, s_ * P:(s_ + 1) * P], kst[:, s_, :], ident[:], [B_kst, B_ident], [B_ps[bk]])
        copy(evac_eng(), kpeT[0:64, 0:256], psb[bk][0:64, 0:256], [B_ps[bk]], [B_kpe])

        STOP = float(os.environ.get("K_STOP", "9"))
        if STOP <= 1:
            return
        for t in range(NT):
            tsl = slice(t * TT, (t + 1) * TT)
            ksl = slice(256 + t * TT, 256 + (t + 1) * TT)
            for m in range(4):
                bk = bankA()
                for k in range(KC):
                    mm(psb[bk][:], win[:, k, m * P:(m + 1) * P], R0[:, k, tsl], k == 0, k == KC - 1,
                       [B_win, B_R0[t][k]], [B_ps[bk]])
                if m < 2:
                    copy(evac_eng(), tq[:, m, :], psb[bk][:], [B_ps[bk]], [B_tq])
                else:
                    copy(evac_eng(), tc[:, m - 2, :], psb[bk][:], [B_ps[bk]], [B_tc])
            if STOP <= 1.05:
                continue
            bk1 = bankA()
            for k in range(KC):
                mm(psb[bk1][0:64, :], win[:, k, 512:576], R0[:, k, tsl], k == 0, k == KC - 1, [B_win, B_R0[t][k]], [B_ps[bk1]])
            if STOP <= 1.1:
                continue
            if t < 4:
                bk2 = bankA()
                for k in range(KC):
                    mm(psb[bk2][0:64, :], win[:, k, 576:640], R0[:, k, tsl], k == 0, k == KC - 1,
                       [B_win, B_R0[t][k]], [B_ps[bk2]])
                tt("dve", tk[0:64, :], psb[bk1][0:64, :], ropeC[0:64, tsl], ALU.mult, [B_ps[bk1], B_rope], [B_tk])
                tt("dve", tkp[0:64, :], psb[bk2][0:64, :], ropeS[0:64, tsl], ALU.mult, [B_ps[bk2], B_rope], [B_tkp])
                tt("dve", kpeT[0:64, ksl], tk[0:64, :], tkp[0:64, :], ALU.add, [B_tk, B_tkp], [B_kpe])
            else:
                copy("dve", tk[0:64, :], psb[bk1][0:64, :], [B_ps[bk1]], [B_tk])
                copy("act", kpeT[0:64, ksl], tk[0:64, :], [B_tk], [B_kpe])
            if STOP <= 1.2:
                continue
            rr_, Br_ = rms_stats(tq, B_tq, 2, 256, sq2, B_sq2)
            tt("dve", tmp2, tq, rr_[:].unsqueeze(1).to_broadcast([P, 2, TT]), ALU.mult, [B_tq, Br_], [B_tmp2])
            for m in range(2):
                ts("dve", qlatn[:, m, tsl], tmp2[:, m, :], vT[:, R_QAN + i * 2 + m:R_QAN + i * 2 + m + 1], None,
                   ALU.mult, ALU.bypass, [B_tmp2, B_vT], [B_qlat[t]])
            rr_, Br_ = rms_stats(tc, B_tc, 2, 256, sq2, B_sq2)
            tt("dve", tmp2, tc, rr_[:].unsqueeze(1).to_broadcast([P, 2, TT]), ALU.mult, [B_tc, Br_], [B_tmp2])
            for m in range(2):
                ts("dve", tc[:, m, :], tmp2[:, m, :], vT[:, R_KVAN + i * 2 + m:R_KVAN + i * 2 + m + 1], None,
                   ALU.mult, ALU.bypass, [B_tmp2, B_vT], [B_tc])
            copy("act", ckvall[:, :, ksl], tc, [B_tc], [B_ckv[t + 1]])
            if t == 4 and STOP > 1.5:
                for sub in range(4):
                    bk = bankA()
                    for m in range(2):
                        tr(psb[bk][:, m * P:(m + 1) * P], tc[:, m, sub * P:(sub + 1) * P], ident[:],
                           [B_tc, B_ident], [B_ps[bk]])
                    copy(evac_eng(), stgo[:, sub, :], psb[bk][:, 0:256], [B_ps[bk]], [B_tmp2])
                for sq_ in range(2):
                    S.dma("sp", ockv[sq_, i].rearrange("(s p) f -> p s f", p=P), stgo[:, sq_ * 2:sq_ * 2 + 2, :],
                          [B_tmp2], [], B_tmp2)
                for sub in range(4):
                    bk = bankA()
                    tr(psb[bk][:, 0:64], tk[0:64, sub * P:(sub + 1) * P], ident[0:64, 0:64], [B_tk, B_ident], [B_ps[bk]])
                    copy(evac_eng(), ksto[:, sub, :], psb[bk][:, 0:64], [B_ps[bk]], [B_cst])
                for sq_ in range(2):
                    S.dma("sp", okpe[sq_, i].rearrange("(s p) f -> p s f", p=P), ksto[:, sq_ * 2:sq_ * 2 + 2, :],
                          [B_cst], [], B_cst)

        if STOP <= 2:
            return
        qblocks = [(qb * TT, TT, list(range(0, 18))) for qb in range(4)]
        qblocks += [(2048, 256, [18, 19]), (2304, 256, [20, 21])]
        for h in range(8):
            for cb in range(6):
                c0 = cb * 512
                n = min(512, 2816 - c0)
                bk = bankA()
                for kc in range(2):
                    mm(psb[bk][:, 0:n], wkv[:, kc, h * 256:h * 256 + P], ckvall[:, kc, c0:c0 + n], kc == 0, kc == 1,
                       [B_wkv] + B_ckv, [B_ps[bk]])
                copy(evac_eng(), kTh[:, c0:c0 + n], psb[bk][:, 0:n], [B_ps[bk]], [B_kT])
            for g4 in range(6):
                bk = bankA()
                kts = list(range(g4 * 4, min(22, g4 * 4 + 4)))
                for j, kt in enumerate(kts):
                    for kc in range(2):
                        mm(psb[bk][:, j * P:(j + 1) * P], ckvall[:, kc, kt * P:(kt + 1) * P],
                           wkv[:, kc, h * 256 + P:h * 256 + 2 * P], kc == 0, kc == 1, [B_wkv] + B_ckv, [B_ps[bk]], lhs=B_ckv)
                nn = len(kts) * P
                copy(evac_eng(), Vh[:, kts[0]:kts[0] + len(kts), :].rearrange("p a b -> p (a b)"), psb[bk][:, 0:nn],
                     [B_ps[bk]], [B_V])
            for t in range(NT):
                tsl = slice(t * TT, (t + 1) * TT)
                bk = bankA()
                for kc in range(2):
                    mm(psb[bk][:], wq[:, kc, h * 192:h * 192 + P], qlatn[:, kc, tsl], kc == 0, kc == 1,
                       [B_wq, B_qlat[t]], [B_ps[bk]])
                act(qnh[:, tsl], psb[bk][:], AF.Copy, [B_ps[bk]], [B_qn], scale=SCQ)
                bk1 = bankA()
                for kc in range(2):
                    mm(psb[bk1][0:64, :], wq[:, kc, h * 192 + P:h * 192 + 192], qlatn[:, kc, tsl], kc == 0, kc == 1,
                       [B_wq, B_qlat[t]], [B_ps[bk1]])
                if t < 4:
                    bk2 = bankA()
                    for kc in range(2):
                        mm(psb[bk2][0:64, :], wq[:, kc, 1536 + h * 64:1536 + (h + 1) * 64], qlatn[:, kc, tsl],
                           kc == 0, kc == 1, [B_wq, B_qlat[t]], [B_ps[bk2]])
                    stt(tk[0:64, :], psb[bk1][0:64, :], SCQ, ropeC[0:64, tsl], ALU.mult, ALU.mult,
                        [B_ps[bk1], B_rope], [B_tk])
                    stt(tkp[0:64, :], psb[bk2][0:64, :], SCQ, ropeS[0:64, tsl], ALU.mult, ALU.mult,
                        [B_ps[bk2], B_rope], [B_tkp])
                    tt("dve", qrh[0:64, tsl], tk[0:64, :], tkp[0:64, :], ALU.add, [B_tk, B_tkp], [B_qr])
                else:
                    act(qrh[0:64, tsl], psb[bk1][0:64, :], AF.Copy, [B_ps[bk1]], [B_qr], scale=SCQ)
            npt = 0
            if STOP <= 3:
                continue
            for (q0, nq, kts) in qblocks:
                if l + 1 < nl:
                    ada_mm(l + 1)
                qsl = slice(q0, q0 + nq)
                tq_ = q0 // TT
                obk = bankB()
                sbk = bankB()
                sbanks = {}

                def smm(kt):
                    b_ = bankA()
                    sbanks[kt] = b_
                    mm(psb[b_][:, 0:nq], kTh[:, kt * P:(kt + 1) * P], qnh[:, qsl], True, False, [B_kT, B_qn], [B_ps[b_]])
                    mm(psb[b_][:, 0:nq], kpeT[0:64, kt * P:(kt + 1) * P], qrh[0:64, qsl], False, True,
                       [B_kpe, B_qr], [B_ps[b_]])

                smm(kts[0])
                if len(kts) > 1:
                    smm(kts[1])
                for j, kt in enumerate(kts):
                    if j + 2 < len(kts):
                        smm(kts[j + 2])
                    pj = npt % 3
                    npt += 1
                    b_ = sbanks[kt]
                    act(pT[pj][:, 0:nq], psb[b_][:, 0:nq], AF.Exp, [B_ps[b_]], [B_pT[pj]])
                    mm(psb[obk][:, 0:nq], Vh[:, kt, :], pT[pj][:, 0:nq], j == 0, j == len(kts) - 1,
                       [B_V, B_pT[pj]], [B_ps[obk]])
                    mm(psb[sbk][:, 0:nq], onesb[:], pT[pj][:, 0:nq], j == 0, j == len(kts) - 1,
                       [B_ones, B_pT[pj]], [B_ps[sbk]])
                act(rstd[:, 0:nq], psb[sbk][:, 0:nq], AF.Ln, [B_ps[sbk]], [B_rstd])
                act(rs1[:, 0:nq], rstd[:, 0:nq], AF.Exp, [B_rstd], [B_rs1], scale=-1.0)
                tt("dve", OT[:, h, qsl], psb[obk][:, 0:nq], rs1[:, 0:nq], ALU.mult, [B_ps[obk], B_rs1], [B_OT[tq_][h]])


    def even(l):
        i = l // 2
        wg = v3(wv(0, 4096, BF16), KC)
        QD = [wv(4096 + d * 1280, 1280, BF16) for d in range(2)]
        KI = [wv(6656 + d * 1280, 1280, BF16) for d in range(2)]
        KU = [v3(wv(9216 + d * 1280, 1280, BF16), 20) for d in range(2)]
        Vtok = v3(wv(11776, 2560, BF16), 20)
        SB = [v3(r2(d * 2560, 2560, BF16), NCH) for d in range(2)]
        qf = r2(5120, 512)
        kf = r2(5632, 512)
        qfs = [r2(5120 + j * 256, 256, BF16) for j in range(2)]
        kfs = [r2(5632 + j * 256, 256, BF16) for j in range(2)]
        t1 = r2(6144, 512)
        t2 = r2(6656, 512)
        t3 = r2(7168, 512)
        kuT = r2(7680, 256, BF16)
        t1B = r2(7936, 512)
        t2B = r2(8448, 512)
        t3B = r2(9600, 512)
        gk2kuT = r2(8960, 256, BF16)
        aTs = [r2(7936 + j * 128, 128, BF16) for j in range(2)]
        Sf = r2(8192, 512).rearrange("p (d q v) -> p d q v", d=2, q=2)
        SGt = v3(r2(8704, 512, BF16), 2)
        gk = r2(9216, 256, BF16)
        wgk = r2(9472, 128, BF16).rearrange("p (d f) -> p d f", d=2)

        def c64(a):
            return a.rearrange("p (c j) -> p c j", j=64)

        B_wg, B_V, B_qf, B_kf, B_t1, B_t2, B_t3, B_kuT, B_SGt, B_gk, B_wgk, B_D, B_rt, B_nb = (
            Buf(n) for n in ("wg", "Vtok", "qf", "kf", "t1", "t2", "t3", "kuT", "SGt", "gk", "wgk", "Dch", "rtab", "negb"))
        B_qfs, B_kfs, B_gks = [Buf("qfA"), Buf("qfB")], [Buf("kfA"), Buf("kfB")], [Buf("gkA"), Buf("gkB")]
        B_t1B, B_t2B, B_t3B = Buf("t1B"), Buf("t2B"), Buf("t3B")
        B_sq2b = Buf("sq2b")
        TS = [((t1, B_t1), (t2, B_t2), (t3, B_t3)), ((t1B, B_t1B), (t2B, B_t2B), (t3B, B_t3B))]
        kuTs = [kuT, gk2kuT]
        B_kuTs = [B_kuT, Buf("kuTB")]
        B_QD = [[Buf("QD") for t in range(NT)] for d in range(2)]
        B_KI = [[Buf("KI") for t in range(NT)] for d in range(2)]
        B_KU = [[Buf("KU") for t in range(NT)] for d in range(2)]
        B_Vt = [Buf("Vt") for t in range(NT)]
        B_SB = [[Buf("SB") for n in range(NCH)] for d in range(2)]
        B_Sf = [[Buf("Sf") for q in range(2)] for d in range(2)]
        B_aT = [Buf("aT0"), Buf("aT1")]

        ts("dve", negb[:], vT[:, R_BGK:R_BGK + 8], -1.0, None, ALU.mult, ALU.bypass, [B_vT], [B_nb])

        def load_wg(g):
            isret = g >= 2
            pg = g % 2
            wsrc = w_in_even[i]

            def wcols(c0, n):
                return wsrc[:, c0:c0 + n].rearrange("(k p) f -> p k f", p=P)

            if not isret:
                cq, ck, cv, cg = pg * 128, 256 + pg * 128, 512 + pg * 256, 1024 + pg * 256
            else:
                cq, ck, cv, cg = 1568 + pg * 128, 1824 + pg * 128, 2080 + pg * 256, 2592 + pg * 256
            S.dma("pool", wg[:, :, 0:128], wcols(cq, 128), [], [B_wg], B_wg)
            S.dma("pool", wg[:, :, 128:256], wcols(ck, 128), [], [B_wg], B_wg)
            S.dma("pool", wg[:, :, 256:512], wcols(cv, 256), [], [B_wg], B_wg)
            S.dma("pool", wg[:, :, 512:768], wcols(cg, 256), [], [B_wg], B_wg)
            if not isret:
                S.dma("pool", wg[:, :, 768:800], wcols(1536, 32), [], [B_wg], B_wg)
                S.op("dve", "memset", (wgk[0:64, :, :], 0.0), {}, [], [B_wgk])
                for d in range(2):
                    for rep in range(2):
                        S.dma("pool", wgk[rep * 32 + d * 16:rep * 32 + (d + 1) * 16, d, :],
                              w_gk2[i, d, :, pg * 128:(pg + 1) * 128], [], [B_wgk], B_wgk)
            else:
                for (dst0, csrc) in ((768, cq), (896, ck)):
                    dv_ = wg[:, :, dst0:dst0 + 128].rearrange("p k (h d) -> p k h d", d=64)
                    sv_ = wsrc[:, csrc:csrc + 128].rearrange("(k p) (h d) -> p k h d", p=P, d=64)
                    for hh in range(2):
                        S.dma("pool", dv_[:, :, hh, 0:32], sv_[:, :, hh, 32:64], [], [B_wg], B_wg)
                        S.dma("pool", dv_[:, :, hh, 32:64], sv_[:, :, hh, 0:32], [], [B_wg], B_wg)

        load_wg(0)
        for g in range(4):
            isret = g >= 2
            pg = g % 2
            if isret:
                for d in range(2):
                    lc = i * 4 + pg * 2 + d
                    xi = 1 if d == 0 else 3
                    yi = 2 if d == 0 else 0
                    act(rtab[:, d, 0, :], iot[:, xi, :], AF.Exp, [B_ec2], [B_rt], scale=lgm[:, lc:lc + 1])
                    act(rtab[:, d, 1, :], iot[:, xi, :], AF.Exp, [B_ec2], [B_rt], scale=lgn[:, lc:lc + 1])
                    act(rtab[:, d, 2, :], iot[:, yi, :], AF.Exp, [B_ec2], [B_rt], scale=lgm[:, lc:lc + 1])
                    act(Dret[:, d:d + 1], iot[:, 3, 0:1], AF.Exp, [B_ec2], [B_rt], scale=lgm[:, lc:lc + 1])
                    copy("dve", Dch[:, d, :], Dret[:, d:d + 1].to_broadcast([P, NCH]), [B_rt], [B_D])

            pbanks = {}

            def PE_PROJ(t):
                tsl = slice(t * TT, (t + 1) * TT)
                rope = isret and t < 4
                g0 = (t % 2) * 32

                def proj(c0, n=128, p0=0):
                    b_ = bankA()
                    for k in range(KC):
                        mm(psb[b_][p0:p0 + n, :], wg[:, k, c0:c0 + n], R0[:, k, tsl], k == 0, k == KC - 1,
                           [B_wg, B_R0[t][k]], [B_ps[b_]])
                    return b_

                bq = proj(0)
                bkk = proj(128)
                bqp = bkp = bgk = None
                if rope:
                    bqp = proj(768)
                    bkp = proj(896)
                if not isret:
                    bgk = proj(768, 32, g0)
                pbanks[t] = (bq, bkk, bqp, bkp, bgk)
                for half in range(2):
                    b_ = bankB()
                    for sub in range(2):
                        blk = t * 4 + half * 2 + sub
                        for k in range(KC):
                            mm(psb[b_][:, sub * 256:(sub + 1) * 256], R0[:, k, blk * P:(blk + 1) * P], wg[:, k, 256:512],
                               k == 0, k == KC - 1, [B_wg, B_R0[t][k]], [B_ps[b_]], lhs=[B_R0[t][k]])
                    blk0 = t * 4 + half * 2
                    copy(evac_eng(), Vtok[:, blk0:blk0 + 2, :].rearrange("p a b -> p (a b)"), psb[b_][:],
                         [B_ps[b_]], [B_Vt[t]])

            def EVAC(t):
                tsl = slice(t * TT, (t + 1) * TT)
                rope = isret and t < 4
                qf, kf, B_qf, B_kf = qfs[t % 2], kfs[t % 2], B_qfs[t % 2], B_kfs[t % 2]
                g0 = (t % 2) * 32
                bq, bkk, bqp, bkp, bgk = pbanks[t]
                if rope:
                    tt("dve", t1, psb[bq][:], ropeC[:, tsl], ALU.mult, [B_ps[bq], B_rope], [B_t1])
                    tt("dve", t2, psb[bqp][:], ropeS[:, tsl], ALU.mult, [B_ps[bqp], B_rope], [B_t2])
                    tt("dve", qf, t1, t2, ALU.add, [B_t1, B_t2], [B_qf])
                    stt(t1, psb[bkk][:], 0.125, ropeC[:, tsl], ALU.mult, ALU.mult, [B_ps[bkk], B_rope], [B_t1])
                    stt(t2, psb[bkp][:], 0.125, ropeS[:, tsl], ALU.mult, ALU.mult, [B_ps[bkp], B_rope], [B_t2])
                    tt("dve", kf, t1, t2, ALU.add, [B_t1, B_t2], [B_kf])
                elif isret:
                    copy("dve", qf, psb[bq][:], [B_ps[bq]], [B_qf])
                    act(kf, psb[bkk][:], AF.Copy, [B_ps[bkk]], [B_kf], scale=0.125)
                else:
                    act(qf, psb[bq][:], AF.Copy, [B_ps[bq]], [B_qf], scale=0.125)
                    copy("dve", kf, psb[bkk][:], [B_ps[bkk]], [B_kf])
                    copy("act", gk[g0:g0 + 32, :], psb[bgk][g0:g0 + 32, :], [B_ps[bgk]], [B_gks[t % 2]])

            def DECAY(t):
                tsl = slice(t * TT, (t + 1) * TT)
                qf, kf, B_qf, B_kf = qfs[t % 2], kfs[t % 2], B_qfs[t % 2], B_kfs[t % 2]
                g0 = (t % 2) * 32
                ME = ("dve", "pool")
                if not isret:
                    bgp = [bankB(), bankB()]
                    Xs, Ys, Es = [None, None], [None, None], [None, None]
                    for d in range(2):
                        mm(psb[bgp[d]][:], wgk[g0:g0 + 32, d, :], gk[g0:g0 + 32, :], True, True,
                           [B_wgk, B_gks[t % 2]], [B_ps[bgp[d]]])
                    for d in range(2):
                        a1, a2, a3 = TS[d]
                        nbc = i * 4 + d * 2 + pg
                        act(a1[0], psb[bgp[d]][:], AF.Exp, [B_ps[bgp[d]], B_nb], [a1[1]],
                            bias=negb[:, nbc:nbc + 1], scale=-1.0)
                    for d in range(2):
                        a1, a2, a3 = TS[d]
                        act(a2[0], a1[0], AF.Ln, [a1[1], B_ec2], [a2[1]], bias=onesf[:, 0:1], scale=1.0)
                    for d in range(2):
                        a1, a2, a3 = TS[d]
                        S.op("dve", "tensor_tensor_scan", (a3[0], resetm[:], a2[0], 0.0, ALU.mult, ALU.add), {},
                             [B_ec, a2[1]], [a3[1]])
                    for d in range(2):
                        a1, a2, a3 = TS[d]
                        tpb = c64(a3[0])[:, :, 63:64].to_broadcast([P, 8, 64])
                        act(Dch[:, d, t * 8:(t + 1) * 8], c64(a3[0])[:, :, 63], AF.Exp, [a3[1]], [B_D], scale=-1.0 / 16)
                        if d == 0:
                            tt("dve", c64(a1[0]), tpb, c64(a3[0]), ALU.subtract, [a3[1]], [a1[1]])
                            Xs[d], Ys[d], Es[d] = a3, a1, a2
                        else:
                            tt("dve", a2[0], a3[0], a2[0], ALU.subtract, [a3[1], a2[1]], [a2[1]])
                            tt("dve", c64(a1[0]), tpb, c64(a2[0]), ALU.subtract, [a3[1], a2[1]], [a1[1]])
                            Xs[d], Ys[d], Es[d] = a1, a2, a3
                    for d in range(2):
                        act(Es[d][0], Xs[d][0], AF.Exp, [Xs[d][1]], [Es[d][1]], scale=-1.0 / 16)
                    for d in range(2):
                        tt(ME[d], QD[d][:, tsl], qf, Es[d][0], ALU.mult, [B_qf, Es[d][1]], [B_QD[d][t]])
                    for d in range(2):
                        act(Es[d][0], Xs[d][0], AF.Exp, [Xs[d][1]], [Es[d][1]], scale=1.0 / 16)
                    for d in range(2):
                        tt(ME[d], KI[d][:, tsl], kf, Es[d][0], ALU.mult, [B_kf, Es[d][1]], [B_KI[d][t]])
                    for d in range(2):
                        act(Es[d][0], Ys[d][0], AF.Exp, [Ys[d][1]], [Es[d][1]], scale=-1.0 / 16)
                    for d in range(2):
                        tt(ME[d], kuTs[d], kf, Es[d][0], ALU.mult, [B_kf, Es[d][1]], [B_kuTs[d]])
                else:
                    for d in range(2):
                        def bc(j):
                            return rtab[:, d, j, :].unsqueeze(1).to_broadcast([P, 8, 64])
                        tt(ME[d], c64(QD[d][:, tsl]), c64(qf), bc(0), ALU.mult, [B_qf, B_rt], [B_QD[d][t]])
                        tt(ME[d], c64(KI[d][:, tsl]), c64(kf), bc(1), ALU.mult, [B_kf, B_rt], [B_KI[d][t]])
                        tt(ME[d], c64(kuTs[d]), c64(kf), bc(2), ALU.mult, [B_kf, B_rt], [B_kuTs[d]])
                for d in range(2):
                    bt = bankB()
                    pst = psb[bt][:].bitcast(BF16)
                    for sub in range(4):
                        tr(pst[:, sub * P:(sub + 1) * P], kuTs[d][:, sub * P:(sub + 1) * P], identb[:],
                           [B_kuTs[d], B_identb], [B_ps[bt]])
                    copy(evac_eng(), KU[d][:, t * 4:(t + 1) * 4, :].rearrange("p a b -> p (a b)"), pst[:, 0:512],
                         [B_ps[bt]], [B_KU[d][t]])

            PE_PROJ(0)
            EVAC(0)
            PE_PROJ(1)
            for t in range(NT):
                if t + 1 < NT:
                    EVAC(t + 1)
                if t + 2 < NT:
                    PE_PROJ(t + 2)
                DECAY(t)
            for t in range(NT):
                tsl = slice(t * TT, (t + 1) * TT)
                for hh in range(2):
                    oi = (4 if isret else 0) + pg * 2 + hh
                    bg_ = bankB()
                    for k in range(KC):
                        mm(psb[bg_][:], wg[:, k, 512 + hh * P:512 + (hh + 1) * P], R0[:, k, tsl], k == 0, k == KC - 1,
                           [B_wg, B_R0[t][k]], [B_ps[bg_]])
                    act(OT[:, oi, tsl], psb[bg_][:], AF.Silu, [B_ps[bg_]], [B_OT[t][oi]])
            S.barrier()
            if g + 1 < 4:
                load_wg(g + 1)

            st_in = sret if isret else sgla
            st_out = oret if isret else ogla
            for si, (c0, c1) in enumerate(((0, 32), (32, 36), (36, 40))):
                pp = [0, 0]
                for d in range(2):
                    cur = Sf[:, d, 0, :]
                    if si == 0:
                        S.dma("sp", cur, st_in[i, d, 2 * pg:2 * pg + 2].rearrange("h k v -> (h k) v"),
                              [], [B_Sf[d][0]], B_Sf[d][0])
                    else:
                        S.op("dve", "memset", (cur, 0.0), {}, [], [B_Sf[d][0]])
                orders = [list(range(c0, c1)), list(range(c1 - 1, c0 - 1, -1))]
                for step in range(c1 - c0):

                    for d in range(2):
                        n = orders[d][step]
                        cur = Sf[:, d, pp[d], :]
                        Bc = B_Sf[d][pp[d]]
                        nxt = Sf[:, d, 1 - pp[d], :]
                        Bn = B_Sf[d][1 - pp[d]]
                        copy("act", SB[d][:, n, :], cur, [Bc], [B_SB[d][n]])
                        blk, r0 = n // 2, (n % 2) * 64
                        tl = n // 8
                        bu = bankA()
                        for hh in range(2):
                            mm(psb[bu][hh * 64:(hh + 1) * 64, 0:P], KU[d][r0:r0 + 64, blk, hh * 64:(hh + 1) * 64],
                               Vtok[r0:r0 + 64, blk, hh * P:(hh + 1) * P], True, True,
                               [B_KU[d][tl], B_Vt[tl]], [B_ps[bu]])
                        stt(nxt, cur, Dch[:, d, n:n + 1], psb[bu][:, 0:P], ALU.mult, ALU.add, [Bc, B_D, B_ps[bu]], [Bn])
                        pp[d] = 1 - pp[d]
                if si > 0:
                    for d in range(2):
                        S.dma("sp", st_out[si - 1, i, d, 2 * pg:2 * pg + 2].rearrange("h k v -> (h k) v"),
                              Sf[:, d, pp[d], :], [B_Sf[d][pp[d]]], [], B_Sf[d][pp[d]])

            for t in range(NT):
                tsl = slice(t * TT, (t + 1) * TT)
                bo = [bankB(), bankB()]
                bj = [bankB(), bankB()]
                chs = [cn for par in range(2) for cn in range(par, 8, 2)]

                def emit_aT(idx):
                    cn = chs[idx]
                    n = t * 8 + cn
                    r0 = (n % 2) * 64
                    csl = slice(n * 64, (n + 1) * 64)
                    rc = slice(r0, r0 + 64)
                    j = idx % 2
                    for hh in range(2):
                        rh = slice(hh * 64, hh * 64 + 64)
                        ba = bankA()
                        for d in range(2):
                            mm(psb[ba][rc, d * 64:(d + 1) * 64], KI[d][rh, csl], QD[d][rh, csl], True, True,
                               [B_KI[d][t], B_QD[d][t]], [B_ps[ba]])
                        tt("dve", aTs[j][rc, hh * P:(hh + 1) * P], psb[ba][rc, 0:P], m4[rc, 0:P], ALU.mult,
                           [B_ps[ba], B_ec], [B_aT[j]])

                emit_aT(0)
                for idx in range(8):
                    if l + 1 < nl and (t * 8 + idx) % 3 == 0 and (t * 8 + idx) < 36:
                        ada_mm(l + 1)
                    if idx + 1 < 8:
                        emit_aT(idx + 1)
                    cn = chs[idx]
                    n = t * 8 + cn
                    blk, r0 = n // 2, (n % 2) * 64
                    csl = slice(n * 64, (n + 1) * 64)
                    rc = slice(r0, r0 + 64)
                    j = idx % 2
                    for hh in range(2):
                        oo = psb[bo[hh]][:, cn * 64:(cn + 1) * 64]
                        vv = Vtok[rc, blk, hh * P:(hh + 1) * P]
                        mm(oo, vv, aTs[j][rc, (hh * 2) * 64:(hh * 2 + 1) * 64], True, False,
                           [B_Vt[t], B_aT[j]], [B_ps[bo[hh]]])
                        mm(oo, vv, aTs[j][rc, (hh * 2 + 1) * 64:(hh * 2 + 2) * 64], False, True,
                           [B_Vt[t], B_aT[j]], [B_ps[bo[hh]]])
                    for hh in range(2):
                        rh = slice(hh * 64, hh * 64 + 64)
                        oj = psb[bj[hh]][:, cn * 64:(cn + 1) * 64]
                        mm(oj, SB[0][rh, n, :], QD[0][rh, csl], True, False, [B_SB[0][n], B_QD[0][t]], [B_ps[bj[hh]]])
                        mm(oj, SB[1][rh, n, :], QD[1][rh, csl], False, True, [B_SB[1][n], B_QD[1][t]], [B_ps[bj[hh]]])
                HB = [((t3, B_t3), (t1, B_t1), (t2, B_t2), (rstd, B_rstd)),
                      ((qf, B_qf), (kf, B_kf), (t3B, B_t3B), (rstd2, B_rstd2))]
                B_sqh = [B_sq2, B_sq2b]
                for hh in range(2):
                    (c_, Bc_), (o_, Bo_), _, _ = HB[hh]
                    copy("act", c_, psb[bo[hh]][:], [B_ps[bo[hh]]], [Bc_])
                for hh in range(2):
                    (c_, Bc_), (o_, Bo_), _, _ = HB[hh]
                    tt("dve", o_, c_, psb[bj[hh]][:], ALU.add, [Bc_, B_ps[bj[hh]]], [Bo_])
                bss = [bankA(), bankA()]
                for hh in range(2):
                    (c_, Bc_), (o_, Bo_), _, _ = HB[hh]
                    act(sq2[:, hh, :], o_, AF.Square, [Bo_], [B_sqh[hh]])
                    mm(psb[bss[hh]][:], onesb[:], sq2[:, hh, :], True, True, [B_ones, B_sqh[hh]], [B_ps[bss[hh]]])
                for hh in range(2):
                    (c_, Bc_), (o_, Bo_), _, (r_, Br_) = HB[hh]
                    act(c_, psb[bss[hh]][:], AF.Ln, [B_ps[bss[hh]], B_eps], [Bc_], bias=epsb[:, 0:1], scale=1.0 / 128)
                for hh in range(2):
                    (c_, Bc_), (o_, Bo_), _, (r_, Br_) = HB[hh]
                    act(r_[:], c_, AF.Exp, [Bc_], [Br_], scale=-0.5)
                for hh in range(2):
                    (c_, Bc_), (o_, Bo_), (m_, Bm_), (r_, Br_) = HB[hh]
                    tt("pool" if hh else "dve", m_, o_, r_[:], ALU.mult, [Bo_, Br_], [Bm_])
                for hh in range(2):
                    (c_, Bc_), (o_, Bo_), (m_, Bm_), (r_, Br_) = HB[hh]
                    if not isret:
                        oi = pg * 2 + hh
                        stt(OT[:, oi, tsl], m_, vT[:, R_GLAN + i:R_GLAN + i + 1], OT[:, oi, tsl],
                            ALU.mult, ALU.mult, [Bm_, B_vT, B_OT[t][oi]], [B_OT[t][oi]])
                    else:
                        oi = 4 + pg * 2 + hh
                        tt("dve", OT[:, oi, tsl], m_, OT[:, oi, tsl], ALU.mult, [Bm_, B_OT[t][oi]], [B_OT[t][oi]])
            S.barrier()

    def c1(l, w_out_l):
        wout = v3(r2(0, 4096, BF16), KC)
        y3 = v3(r2(4096, 4096), KC)
        B_wout = Buf("wout")
        B_y3 = [Buf("y3_%d" % k) for k in range(KC)]
        S.dma("pool", wout, w_out_l.rearrange("(k p) f -> p k f", p=P), [], [B_wout], B_wout)

        def get_y(t):
            tsl = slice(t * TT, (t + 1) * TT)
            for dc in range(KC):
                bk = bankA()
                for k in range(KC):
                    mm(psb[bk][:], wout[:, k, dc * P:(dc + 1) * P], OT[:, k, tsl], k == 0, k == KC - 1,
                       [B_wout, B_OT[t][k]], [B_ps[bk]])
                copy(evac_eng(), y3[:, dc, :], psb[bk][:], [B_ps[bk]], [B_y3[dc]])
            return y3, B_y3

        norm_pipeline(get_y, GG1(l), (GS2(l), 24, l), False, ahead=True)

    def cidx_of(t):
        return 0 if t < 4 else 1

    for t in range(NT):
        stage_x0(t)
    wad0 = [wv(8192 + i * 2048, 2048, BF16).rearrange("p (k f) -> p k f", k=KC) for i in range(2)]
    for pc in range(12):
        i = pc % 2
        Bw = B_tmpf[4 * i:4 * i + 4]
        S.dma("pool", wad0[i], w_ada[0, :, pc * 512:(pc + 1) * 512].rearrange("(k p) f -> p k f", p=P), [], Bw, B_wad[i])
        for jj in range(4):
            j = pc * 4 + jj
            bk = bankA()
            for k in range(KC):
                mm(psb[bk][:, 0:2], wad0[i][:, k, jj * P:(jj + 1) * P], scb[:, k, :], k == 0, k == KC - 1,
                   Bw + [B_sc], [B_ps[bk]], lhs=Bw)
            tt("dve", mod[:, 0, j, :], psb[bk][:, 0:2],
               vT[:, R_BADA + j:R_BADA + j + 1].to_broadcast([P, 2]), ALU.add, [B_ps[bk], B_vT], [B_mod])
    layer_vecs(0)
    for t in range(NT):
        i = t % 2
        load_x0(t, i)
        store_x(t, i)
        prenorm(xa[i], B_xa[i], GS1(0), 0, 0, cidx_of(t), R0[:, :, t * TT:(t + 1) * TT], B_R0[t])

    for l in range(nl):
        domix = (mix == "all") or (mix == "odd" and l % 2 == 1) or (mix == "even" and l % 2 == 0)
        if domix:
            S.barrier()
            if l % 2 == 1:
                mla(l)
                S.barrier()
                c1(l, w_out_odd[l // 2])
            else:
                even(l)
                S.barrier()
                c1(l, w_out_even[l // 2])
        else:
            for t in range(NT):
                i = t % 2
                load_x(t, i)
                prenorm(xa[i], B_xa[i], GS2(l), 24, l, cidx_of(t), R0[:, :, t * TT:(t + 1) * TT], B_R0[t])
        S.barrier()
        mlp(l)
        if l + 1 < nl:
            while ada_st.setdefault(l + 1, {"dma": 0, "mm": 0})["mm"] < 48:
                ada_mm(l + 1)
            layer_vecs(l + 1)
        S.barrier()
        def get_y2(t):
            return yacc[:, :, t * TT:(t + 1) * TT], B_Y[t]

        if l + 1 < nl:
            norm_pipeline(get_y2, GG2(l), (GS1(l + 1), 0, l + 1), False)
        else:
            norm_pipeline(get_y2, GG2(l), None, True)

    S.barrier()
    S.final_wait("sp", list(S.dma_bufs))

    with nc.Block() as block:
        @block.tensor
        def _(e):
            S.emit("pe", e)

        @block.scalar
        def _(e):
            S.emit("act", e)

        @block.vector
        def _(e):
            S.emit("dve", e)

        @block.gpsimd
        def _(e):
            S.emit("pool", e)

        @block.sync
        def _(e):
            S.emit("sp", e)
    stack.close()
    return nc


def _pack_vecs(inp, b):
    v = np.zeros((R_TOT, P), np.float32)
    v[R_BADA:R_BADA + 192] = inp["b_ada"].reshape(192, P)
    v[R_NMIXPRE:R_NMIXPRE + 32] = inp["norm_mix_pre"].reshape(32, P)
    v[R_NMIXPOST:R_NMIXPOST + 32] = inp["norm_mix_post"].reshape(32, P)
    v[R_NMLPPRE:R_NMLPPRE + 32] = inp["norm_mlp_pre"].reshape(32, P)
    v[R_NMLPPOST:R_NMLPPOST + 32] = inp["norm_mlp_post"].reshape(32, P)
    v[R_COND:R_COND + 8] = inp["c"][b].reshape(8, P)
    v[R_COND + 8:R_COND + 16] = inp["c_ctx"].reshape(8, P)
    v[R_QAN:R_QAN + 4] = inp["q_a_norm"].reshape(4, P)
    v[R_KVAN:R_KVAN + 4] = inp["kv_a_norm"].reshape(4, P)
    v[R_GLAN:R_GLAN + 2] = inp["gla_norm"].reshape(2, P)
    v[R_BGK:R_BGK + 8] = inp["b_gk2"].reshape(8, P)
    return v


def kernel(**inputs):
    inp = {k: np.asarray(v) for k, v in inputs.items()}
    ncores = int(os.environ.get("K_NCORES", "8"))
    nl = int(os.environ.get("K_NL", str(DEPTH)))
    mix = os.environ.get("K_MIX", "all")
    nc = build(nl=nl, mix=mix)
    identf = np.eye(P, dtype=np.float32)
    tpos = np.arange(LS)
    inv = 10000.0 ** (-np.arange(16, dtype=np.float64) / 16.0)
    ang = np.concatenate([(tpos // 64)[:, None] * inv[None, :], (tpos % 64)[:, None] * inv[None, :]], axis=1)
    dd = np.arange(P) % 64
    jj = np.arange(64, dtype=np.float32)
    iot = np.broadcast_to(np.stack([jj, jj + 1, 63 - jj, 64 - jj])[None], (P, 4, 64)).astype(np.float32)
    pj = (np.arange(P) % 64)[:, None]
    cc = np.arange(256)[None, :]
    ii = cc % 64
    dirb = (cc // 64) % 2
    m4 = np.where(dirb == 0, ii >= pj, ii <= pj).astype(np.float32)
    resetm = np.broadcast_to((np.arange(TT) % 64 != 0).astype(np.float32)[None], (P, TT))
    rd = inp["ret_decay"]
    rdec = np.zeros((P, 8), np.float32)
    for i_ in range(2):
        for pg_ in range(2):
            for d_ in range(2):
                for p_ in range(P):
                    rdec[p_, i_ * 4 + pg_ * 2 + d_] = rd[i_, d_, 2 * pg_ + p_ // 64]
    ropeC = np.cos(ang[:, dd % 32]).T.astype(np.float32)
    ropeS = (np.sin(ang[:, dd % 32]).T * np.where(dd < 32, -1.0, 1.0)[:, None]).astype(np.float32)
    in_maps = []
    for b in range(ncores):
        xin = np.concatenate([inp["x_sample"][b], inp["x_prompt"][2 * b], inp["x_prompt"][2 * b + 1]], axis=0)
        in_maps.append({
            "xin": np.ascontiguousarray(xin, dtype=np.float32),
            "vecs": _pack_vecs(inp, b),
            "identf": identf,
            "w_ada": inp["w_ada"], "w_mlp1": inp["w_mlp1"], "w_mlp2": inp["w_mlp2"],
            "w_in_odd": inp["w_in_odd"], "w_q_b": inp["w_q_b"], "w_kv_b": inp["w_kv_b"],
            "w_out_odd": inp["w_out_odd"], "w_out_even": inp["w_out_even"],
            "cckv": np.ascontiguousarray(inp["cache_ckv"][b]), "ckpe": np.ascontiguousarray(inp["cache_kpe"][b]),
            "ropeC": np.ascontiguousarray(ropeC), "ropeS": np.ascontiguousarray(ropeS),
            "w_in_even": inp["w_in_even"], "w_gk2": inp["w_gk2"],
            "sgla": np.ascontiguousarray(inp["state_gla"][b]), "sret": np.ascontiguousarray(inp["state_ret"][b]),
            "rdec": rdec, "iot": np.ascontiguousarray(iot), "m4": np.ascontiguousarray(m4),
            "resetm": np.ascontiguousarray(resetm),
        })
    res = run_bass_kernel_spmd(nc, in_maps, core_ids=list(range(ncores)))
    y_prompt = np.zeros((16, LP, D), np.float32)
    y_sample = np.zeros((8, LS, D), np.float32)
    new_ckv = np.zeros((16, 2, LP, 256), np.float32)
    new_kpe = np.zeros((16, 2, LP, 64), np.float32)
    new_gla = np.zeros((16, 2, 2, 4, 64, 128), np.float32)
    new_ret = np.zeros((16, 2, 2, 4, 64, 128), np.float32)
    for b in range(ncores):
        r = res.results[b]
        ysb = r["ys"]
        y_sample[b] = ysb[0:LS]
        y_prompt[2 * b] = ysb[LS:LS + LP]
        y_prompt[2 * b + 1] = ysb[LS + LP:LS + 2 * LP]
        new_ckv[2 * b:2 * b + 2] = r["ockv"]
        new_kpe[2 * b:2 * b + 2] = r["okpe"]
        if "ogla" in r:
            new_gla[2 * b:2 * b + 2] = r["ogla"]
            new_ret[2 * b:2 * b + 2] = r["oret"]
    return y_prompt, y_sample, new_ckv, new_kpe, new_gla, new_ret
```

```python
import os
from contextlib import ExitStack
import numpy as np
import concourse.bass as bass
import concourse.mybir as mybir
from concourse.bass_utils import run_bass_kernel_spmd

F32 = mybir.dt.float32
BF16 = mybir.dt.bfloat16
AF = mybir.ActivationFunctionType
ALU = mybir.AluOpType

P = 128
D = 1024
KC = 8
TT = 512
NT = 5
NTOK = NT * TT
DEPTH = 4
DFF = 4096
EPS = 1e-6
LS = 2048
LP = 256
NCH = NTOK // 64

R_BADA = 0
R_NMIXPRE = 192
R_NMIXPOST = 224
R_NMLPPRE = 256
R_NMLPPOST = 288
R_COND = 320
R_QAN = 336
R_KVAN = 340
R_GLAN = 344
R_BGK = 346
R_TOT = 384


class Buf:
    __slots__ = ("name", "w", "r", "sem", "cnt")

    def __init__(self, name):
        self.name = name
        self.w = None
        self.r = {}
        self.sem = None
        self.cnt = 0


class Sched:
    ENG = ("pe", "act", "dve", "pool", "sp")

    def __init__(self, nc, stack):
        self.nc = nc
        self.stack = stack
        self.ops = {e: [] for e in self.ENG}
        self.cnt = {e: 0 for e in self.ENG}
        self.waited = {e: {} for e in self.ENG}
        self.sems = {}
        for e in self.ENG:
            self.sems[e] = stack.enter_context(nc.semaphore("s_" + e))
        self.ndsem = 0
        self.dma_bufs = []

    def _deps(self, eng, reads, writes):
        d = {}
        dr = {}
        for b in reads:
            if b.w is not None:
                k, v = b.w
                if d.get(k, 0) < v:
                    d[k] = v
                if dr.get(k, 0) < v:
                    dr[k] = v
        for b in writes:
            if b.w is not None:
                k, v = b.w
                if d.get(k, 0) < v:
                    d[k] = v
            for k, v in b.r.items():
                if d.get(k, 0) < v:
                    d[k] = v
        out = []
        wd = self.waited[eng]
        for k, v in d.items():
            if k == eng and eng == "pe":
                continue
            if wd.get(k, 0) >= v:
                continue
            wonly = dr.get(k, 0) <= wd.get(k, 0)
            wd[k] = v
            out.append((k, v, wonly))
        return out

    def op(self, eng, meth, args, kw, reads=(), writes=()):
        deps = self._deps(eng, reads, writes)
        self.cnt[eng] += 1
        idx = self.cnt[eng]
        self.ops[eng].append((deps, (meth, args, kw), (eng, 1)))
        for b in reads:
            b.r[eng] = idx
        for b in writes:
            b.w = (eng, idx)
            b.r = {}

    def dma(self, q, out, in_, reads, writes, sb):
        fn = ("dma_start", (), dict(out=out, in_=in_))
        deps = self._deps(q, reads, writes)
        if sb.sem is None:
            sb.sem = "d%d" % self.ndsem
            self.ndsem += 1
            self.sems[sb.sem] = self.stack.enter_context(self.nc.semaphore(sb.sem))
            self.dma_bufs.append(sb)
        sb.cnt += 16
        self.ops[q].append((deps, fn, (sb.sem, 16)))
        for b in reads:
            b.r[sb.sem] = sb.cnt
        for b in writes:
            b.w = (sb.sem, sb.cnt)
            b.r = {}

    def final_wait(self, eng, bufs):
        deps = self._deps(eng, bufs, bufs)
        self.ops[eng].append((deps, None, None))

    def barrier(self):
        toks = [(k, self.cnt[k]) for k in self.ENG if self.cnt[k] > 0]
        toks += [(b.sem, b.cnt) for b in self.dma_bufs]
        for e in self.ENG:
            wd = self.waited[e]
            deps = []
            for k, v in toks:
                if k == e:
                    continue
                if wd.get(k, 0) >= v:
                    continue
                wd[k] = v
                deps.append((k, v))
            if deps:
                self.ops[e].append((deps, None, None))

    def emit(self, eng, e):
        for deps, fn, inc in self.ops[eng]:
            if fn is None or fn[0] == "dma_start" or eng in ("sp", "pool"):
                for d_ in deps:
                    e.wait_ge(self.sems[d_[0]], d_[1])
                if fn is None:
                    continue
                meth, args, kw = fn
                ins = getattr(e, meth)(*args, **kw)
            else:
                fus = None
                for j in range(len(deps) - 1, -1, -1):
                    if eng != "pe" or (len(deps[j]) > 2 and deps[j][2]):
                        fus = j
                        break
                for j, d_ in enumerate(deps):
                    if j != fus:
                        e.wait_ge(self.sems[d_[0]], d_[1])
                meth, args, kw = fn
                ins = getattr(e, meth)(*args, **kw)
                if fus is not None:
                    ins._wait_ge(self.sems[deps[fus][0]], deps[fus][1])
            ins.then_inc(self.sems[inc[0]], inc[1])
def build(nl=DEPTH, mix="all"):
    nc = bass.Bass("TRN2", target_bir_lowering=False)
    stack = ExitStack()
    S = Sched(nc, stack)

    def din(name, shape, dt=F32):
        return nc.dram_tensor(name, list(shape), dt, kind="ExternalInput").ap()

    def dout(name, shape, dt=F32):
        return nc.dram_tensor(name, list(shape), dt, kind="ExternalOutput").ap()

    xin = din("xin", [NTOK, D])
    vecs = din("vecs", [R_TOT, P])
    identf = din("identf", [P, P])
    w_ada = din("w_ada", [DEPTH, D, 6 * D])
    w_mlp1 = din("w_mlp1", [DEPTH, D, DFF])
    w_mlp2 = din("w_mlp2", [DEPTH, DFF, D])
    w_in_odd = din("w_in_odd", [2, D, 576])
    w_q_b = din("w_q_b", [2, 256, 1536])
    w_kv_b = din("w_kv_b", [2, 256, 2048])
    w_out_odd = din("w_out_odd", [2, D, D])
    w_out_even = din("w_out_even", [2, D, D])
    cckv = din("cckv", [2, 256, 256])
    ckpe = din("ckpe", [2, 256, 64])
    ropeC_d = din("ropeC", [P, LS])
    ropeS_d = din("ropeS", [P, LS])
    w_in_even = din("w_in_even", [2, D, 3104])
    w_gk2 = din("w_gk2", [2, 2, 16, 256])
    sgla = din("sgla", [2, 2, 4, 64, 128])
    sret = din("sret", [2, 2, 4, 64, 128])
    rdec_d = din("rdec", [P, 8])
    iot_d = din("iot", [P, 4, 64])
    m4_d = din("m4", [P, 256])
    resetm_d = din("resetm", [P, TT])
    ogla = dout("ogla", [2, 2, 2, 4, 64, 128])
    oret = dout("oret", [2, 2, 2, 4, 64, 128])
    ys = dout("ys", [NTOK, D])
    ockv = dout("ockv", [2, 2, LP, 256])
    okpe = dout("okpe", [2, 2, LP, 64])
    xs = nc.dram_tensor("xs", [NT, P, KC, TT], F32).ap()
    xsb = [Buf("xs%d" % t) for t in range(NT)]

    def sb(name, shape, dt):
        return stack.enter_context(nc.sbuf_tensor(name, list(shape), dt))

    R0 = sb("R0", [P, KC, NTOK], BF16)
    R12 = sb("R12", [P, KC, NTOK], F32)
    WA = sb("WA", [P, 14336], F32)
    ident = sb("ident", [P, P], F32)
    identb = sb("identb", [P, P], BF16)
    onesb = sb("onesb", [P, P], BF16)
    onesf32 = sb("onesf32", [P, P], F32)
    B_onesf32 = Buf("onesf32")
    vT = sb("vT", [P, R_TOT], F32)
    vrow = sb("vrow", [P, 3, P], F32)
    sc = sb("sc", [P, KC, 2], F32)
    scb = sb("scb", [P, KC, 2], BF16)
    wad = sb("wad", [P, 2, KC, P], BF16)
    B_wad = [Buf("wad0"), Buf("wad1")]
    mod = sb("mod", [P, DEPTH, 48, 2], F32)
    gvec = sb("gvec", [P, DEPTH, 4, KC, 2], F32)
    rs1 = sb("rs1", [P, TT], F32)
    rstd = sb("rstd", [P, TT], F32)
    rstd2 = sb("rstd2", [P, TT], F32)
    epsb = sb("epsb", [P, 1], F32)
    ropeC = sb("ropeC_s", [P, LS], BF16)
    ropeS = sb("ropeS_s", [P, LS], BF16)
    rdec = sb("rdec_s", [P, 8], F32)
    lgn = sb("lgn", [P, 8], F32)
    lgm = sb("lgm", [P, 8], F32)
    iot = sb("iot_s", [P, 4, 64], F32)
    m4 = sb("m4_s", [P, 256], BF16)
    resetm = sb("resetm_s", [P, TT], BF16)
    onesf = sb("onesf", [P, 1], F32)
    negb = sb("negb", [P, 8], F32)
    Dch = sb("Dch", [P, 2, NCH], F32)
    rtab = sb("rtab", [P, 2, 3, 64], F32)
    Dret = sb("Dret", [P, 2], F32)
    B_ec = Buf("evenconst")
    sq2 = sb("sq2", [P, 2, TT], BF16)
    B_rope, B_sq2 = Buf("rope"), Buf("sq2")

    def wv(off, n, dt=F32):
        a = WA[:, off:off + n]
        return a.bitcast(BF16) if dt == BF16 else a

    def v3(a, k):
        return a.rearrange("p (k t) -> p k t", k=k)

    xa = [v3(wv(i * 4096, 4096), KC) for i in range(2)]
    tmpf = v3(wv(8192, 4096), KC)
    stv = wv(8192, 4096).rearrange("p (s d) -> p s d", s=4)
    sqb = v3(wv(12288, 2048, BF16), KC)
    wstf = [v3(wv(i * 2048, 2048), KC) for i in range(2)]
    w1v = [v3(wv(i * 2048, 2048, BF16), KC) for i in range(2)]
    w2st = [v3(wv(4096 + i * 2048, 2048, BF16), 4) for i in range(2)]
    ub = [v3(wv(8192 + i * 1024, 1024, BF16), 4) for i in range(2)]
    rb = [wv(10240 + i * 256, 256, BF16) for i in range(2)]
    yacc = R12

    B_R0 = [[Buf("R0_%d_%d" % (t, k)) for k in range(KC)] for t in range(NT)]
    B_Y = [[Buf("Y_%d_%d" % (t, k)) for k in range(KC)] for t in range(NT)]
    B_ident, B_identb, B_ones, B_vT, B_vrow, B_sc, B_mod, B_g, B_eps = (
        Buf(n) for n in ("ident", "identb", "ones", "vT", "vrow", "sc", "mod", "gv", "eps"))
    B_xa = [[Buf("xa%d_%d" % (i, k)) for k in range(KC)] for i in range(2)]
    B_tmpf = [Buf("tmpf%d" % k) for k in range(KC)]
    B_sqb = [Buf("sqb%d" % k) for k in range(KC)]
    B_rs1, B_rstd = Buf("rs1"), Buf("rstd")
    B_rstd2 = Buf("rstd2")
    rsel = {"n": 0}
    B_xadma = [Buf("xadma0"), Buf("xadma1")]
    B_stg = Buf("stgdma")
    B_wst = [Buf("wst0"), Buf("wst1")]
    B_wsta = [Buf("wsta0"), Buf("wsta1")]
    B_w2st = [Buf("w2st0"), Buf("w2st1")]
    B_ub = [Buf("ub0"), Buf("ub1")]
    B_rb = [Buf("rb0"), Buf("rb1")]

    psb = [stack.enter_context(nc.psum_tensor("ps%d" % i, [P, 512], F32)) for i in range(8)]
    B_ps = [Buf("ps%d" % i) for i in range(8)]
    rr = {"A": 0, "B": 0, "ev": 0}

    def bankA():
        i = rr["A"]
        rr["A"] = (i + 1) % 4
        return i

    def bankB():
        i = 4 + rr["B"]
        rr["B"] = (rr["B"] + 1) % 4
        return i

    def evac_eng():
        rr["ev"] ^= 1
        return "act" if rr["ev"] else "dve"

    def mm(out, lhsT, rhs, start, stop, reads, writes):
        S.op("pe", "matmul", (out,), dict(lhsT=lhsT, rhs=rhs, start=start, stop=stop), reads, writes)

    def tr(out, in_, idn, reads, writes):
        S.op("pe", "transpose", (out, in_, idn), {}, reads, writes)

    def act(out, in_, func, reads, writes, **kw):
        S.op("act", "activation", (out, in_, func), kw, reads, writes)

    def tt(eng, out, in0, in1, op, reads, writes):
        S.op(eng, "tensor_tensor", (out, in0, in1, op), {}, reads, writes)

    def stt(out, in0, scalar, in1, op0, op1, reads, writes):
        S.op("dve", "scalar_tensor_tensor", (out, in0, scalar, in1, op0, op1), {}, reads, writes)

    def ts(eng, out, in0, s1, s2, op0, op1, reads, writes):
        S.op(eng, "tensor_scalar", (out, in0, s1, s2, op0, op1), {}, reads, writes)

    def copy(eng, out, in_, reads, writes):
        if eng == "act":
            act(out, in_, AF.Copy, reads, writes)
        else:
            S.op(eng, "tensor_copy", (out, in_), {}, reads, writes)

    S.dma("sp", ident[:], identf[:, :], [], [B_ident], B_ident)
    copy("dve", identb[:], ident[:], [B_ident], [B_identb])
    S.op("pool", "memset", (onesb[:], 1.0), {}, [], [B_ones])
    S.op("pool", "memset", (onesf32[:], 1.0), {}, [], [B_onesf32])
    S.op("pool", "memset", (epsb[:], EPS), {}, [], [B_eps])

    S.dma("pool", ropeC[:], ropeC_d[:, :], [], [B_rope], B_rope)
    S.dma("pool", ropeS[:], ropeS_d[:, :], [], [B_rope], B_rope)
    S.dma("pool", m4[:], m4_d[:, :], [], [B_ec], B_ec)
    S.dma("pool", resetm[:], resetm_d[:, :], [], [B_ec], B_ec)
    B_ec2 = Buf("evenconst2")
    S.dma("sp", rdec[:], rdec_d[:, :], [], [B_ec2], B_ec2)
    S.dma("sp", iot[:], iot_d[:, :, :], [], [B_ec2], B_ec2)
    S.op("pool", "memset", (onesf[:], 1.0), {}, [], [B_ec2])
    act(lgn[:], rdec[:], AF.Exp, [B_ec2], [B_ec2])
    ts("dve", lgm[:], lgn[:], -1.0, None, ALU.mult, ALU.bypass, [B_ec2], [B_ec2])

    S.dma("sp", vrow[:], vecs.rearrange("(a p) f -> p a f", p=P), [], [B_vrow], B_vrow)
    for a in range(3):
        bk = bankA()
        tr(psb[bk][:, 0:P], vrow[:, a, :], ident[:], [B_vrow, B_ident], [B_ps[bk]])
        copy("dve", vT[:, a * P:(a + 1) * P], psb[bk][:, 0:P], [B_ps[bk]], [B_vT])
    for c in range(2):
        act(sc[:, :, c], vT[:, R_COND + c * 8:R_COND + c * 8 + 8], AF.Silu, [B_vT], [B_sc])

    copy("dve", scb[:], sc[:], [B_sc], [B_sc])
    adac = {"n": 0}

    ada_st = {}

    def ada_dma(l):
        st = ada_st.setdefault(l, {"dma": 0, "mm": 0})
        pc = st["dma"]
        if pc >= 48:
            return
        i = pc % 2
        st["dma"] += 1
        S.dma("pool", wad[:, i], w_ada[l, :, pc * P:(pc + 1) * P].rearrange("(k p) f -> p k f", p=P),
              [], [B_wad[i]], B_wad[i])

    def ada_mm(l):
        st = ada_st.setdefault(l, {"dma": 0, "mm": 0})
        pc = st["mm"]
        if pc >= 48:
            return
        if st["dma"] <= pc:
            ada_dma(l)
        i = pc % 2
        st["mm"] += 1
        bk = bankA()
        for k in range(KC):
            mm(psb[bk][:, 0:2], wad[:, i, k, :], scb[:, k, :], k == 0, k == KC - 1, [B_wad[i], B_sc], [B_ps[bk]])
        tt("dve", mod[:, l, pc, :], psb[bk][:, 0:2],
           vT[:, R_BADA + l * 48 + pc:R_BADA + l * 48 + pc + 1].to_broadcast([P, 2]), ALU.add,
           [B_ps[bk], B_vT], [B_mod])
        if st["dma"] - st["mm"] < 1:
            ada_dma(l)

    def ada_piece(l, pc):
        ada_mm(l)

    def rms_stats(src, B_src, nk, dfeat, sq=None, B_sq=None):
        if sq is None:
            sq, B_sq = sqb, B_sqb
        bk = bankA()
        for k in range(nk):
            bs_ = B_src[k] if isinstance(B_src, list) else B_src
            bq_ = B_sq[k] if isinstance(B_sq, list) else B_sq
            act(sq[:, k, :], src[:, k, :], AF.Square, [bs_], [bq_])
            mm(psb[bk][:], onesb[:], sq[:, k, :], k == 0, k == nk - 1, [B_ones, bq_], [B_ps[bk]])
        rsel["n"] ^= 1
        rr_, Br_ = (rstd, B_rstd) if rsel["n"] else (rstd2, B_rstd2)
        act(rs1[:], psb[bk][:], AF.Ln, [B_ps[bk], B_eps], [B_rs1], bias=epsb[:, 0:1], scale=1.0 / dfeat)
        act(rr_[:], rs1[:], AF.Exp, [B_rs1], [Br_], scale=-0.5)
        return rr_, Br_

    def pre_apply(xt, B_xt, rr_, Br_, gs, sh_j0, l, cidx, out3, B_out):
        for k in range(KC):
            sc_ap = gs[:, k, cidx:cidx + 1]
            bi_ap = mod[:, l, sh_j0 + k, cidx:cidx + 1]
            tt("dve", tmpf[:, k, :], xt[:, k, :], rr_[:], ALU.mult, [B_xt[k], Br_], [B_tmpf[k]])
            if k % 4 == 3:
                ts("dve", out3[:, k, :], tmpf[:, k, :], sc_ap, bi_ap, ALU.mult, ALU.add, [B_tmpf[k], B_g, B_mod], [B_out[k]])
            else:
                act(out3[:, k, :], tmpf[:, k, :], AF.Identity, [B_tmpf[k], B_g, B_mod], [B_out[k]], bias=bi_ap, scale=sc_ap)

    def prenorm(xt, B_xt, gs, sh_j0, l, cidx, out3, B_out):
        rr_, Br_ = rms_stats(xt, B_xt, KC, D)
        pre_apply(xt, B_xt, rr_, Br_, gs, sh_j0, l, cidx, out3, B_out)

    def post_apply(y3, B_y, rr_, Br_, gg, cidx, xt, B_xt):
        for k in range(KC):
            tt("dve", tmpf[:, k, :], y3[:, k, :], rr_[:], ALU.mult, [B_y[k], Br_], [B_tmpf[k]])
            stt(xt[:, k, :], tmpf[:, k, :], gg[:, k, cidx:cidx + 1], xt[:, k, :], ALU.mult, ALU.add,
                [B_tmpf[k], B_g, B_xt[k]], [B_xt[k]])

    def postnorm_res(y3, B_y, gg, cidx, xt, B_xt):
        rr_, Br_ = rms_stats(y3, B_y, KC, D)
        post_apply(y3, B_y, rr_, Br_, gg, cidx, xt, B_xt)

    def norm_pipeline(get_y, gg, pre_args, final, ahead=False):
        pend = None
        ynext = get_y(0) if ahead else None
        for t in range(NT):
            i = t % 2
            load_x(t, i)
            y3, B_y = ynext if ahead else get_y(t)
            ra, Bra = rms_stats(y3, B_y, KC, D)
            if pend is not None:
                pend()
            post_apply(y3, B_y, ra, Bra, gg, cidx_of(t), xa[i], B_xa[i])
            if ahead and t + 1 < NT:
                ynext = get_y(t + 1)
            if final:
                pend = (lambda t=t, i=i: store_y(t, i))
            else:
                store_x(t, i)
                rc_, Brc = rms_stats(xa[i], B_xa[i], KC, D)
                gs_, sh_j0, l_ = pre_args
                pend = (lambda t=t, i=i, rc_=rc_, Brc=Brc: pre_apply(
                    xa[i], B_xa[i], rc_, Brc, gs_, sh_j0, l_, cidx_of(t), R0[:, :, t * TT:(t + 1) * TT], B_R0[t]))
        pend()

    B_stg0 = [Buf("stg0_%d" % t) for t in range(NT)]

    def stage_x0(t):
        st0 = R12[:].rearrange("p k t -> p (k t)")[:, t * 4096:(t + 1) * 4096].rearrange("p (s d) -> p s d", s=4)
        S.dma("sp", st0, xin[t * TT:(t + 1) * TT, :].rearrange("(s p) d -> p s d", p=P), [], [B_stg0[t]], B_stg0[t])

    def load_x0(t, i):
        st0 = R12[:].rearrange("p k t -> p (k t)")[:, t * 4096:(t + 1) * 4096].rearrange("p (s d) -> p s d", s=4)
        for k in range(KC):
            bk = bankA()
            for s_ in range(4):
                tr(psb[bk][:, s_ * P:(s_ + 1) * P], st0[:, s_, k * P:(k + 1) * P], ident[:],
                   [B_stg0[t], B_ident], [B_ps[bk]])
            copy(evac_eng(), xa[i][:, k, :], psb[bk][:], [B_ps[bk]], [B_xa[i][k]])

    def store_y(t, i):
        for s_ in range(4):
            for hh in range(2):
                bk = bankA()
                for kk in range(4):
                    k = hh * 4 + kk
                    tr(psb[bk][:, kk * P:(kk + 1) * P], xa[i][:, k, s_ * P:(s_ + 1) * P], ident[:],
                       [B_xa[i][k], B_ident], [B_ps[bk]])
                copy(evac_eng(), stv[:, s_, hh * 512:(hh + 1) * 512], psb[bk][:], [B_ps[bk]], B_tmpf)
        S.dma("sp", ys[t * TT:(t + 1) * TT, :].rearrange("(s p) d -> p s d", p=P), stv, B_tmpf, [], B_stg)

    def load_x(t, i):
        S.dma("sp", xa[i], xs[t], [xsb[t]], B_xa[i], B_xadma[i])

    def store_x(t, i):
        S.dma("sp", xs[t], xa[i], B_xa[i], [xsb[t]], B_xadma[i])

    def layer_vecs(l):
        for c in range(2):
            stt(gvec[:, l, 0, :, c], mod[:, l, 8:16, c], 1.0, vT[:, R_NMIXPRE + l * 8:R_NMIXPRE + l * 8 + 8],
                ALU.add, ALU.mult, [B_mod, B_vT], [B_g])
            tt("dve", gvec[:, l, 1, :, c], mod[:, l, 16:24, c], vT[:, R_NMIXPOST + l * 8:R_NMIXPOST + l * 8 + 8],
               ALU.mult, [B_mod, B_vT], [B_g])
            stt(gvec[:, l, 2, :, c], mod[:, l, 32:40, c], 1.0, vT[:, R_NMLPPRE + l * 8:R_NMLPPRE + l * 8 + 8],
                ALU.add, ALU.mult, [B_mod, B_vT], [B_g])
            tt("dve", gvec[:, l, 3, :, c], mod[:, l, 40:48, c], vT[:, R_NMLPPOST + l * 8:R_NMLPPOST + l * 8 + 8],
               ALU.mult, [B_mod, B_vT], [B_g])

    def GS1(l): return gvec[:, l, 0]
    def GG1(l): return gvec[:, l, 1]
    def GS2(l): return gvec[:, l, 2]
    def GG2(l): return gvec[:, l, 3]

    def mlp(l):
        nu = 0
        for g in range(8):
            i = g % 2
            S.dma("pool", w1v[i], w_mlp1[l, :, g * 512:(g + 1) * 512].rearrange("(k p) f -> p k f", p=P),
                  [], [B_wst[i]], B_wst[i])
            S.dma("pool", w2st[i], w_mlp2[l, g * 512:(g + 1) * 512, :].rearrange("(c p) d -> p c d", p=P),
                  [], [B_w2st[i]], B_w2st[i])

            def do_u(t, ui):
                for fc in range(4):
                    bk = bankA()
                    for k in range(KC):
                        mm(psb[bk][:], w1v[i][:, k, fc * P:(fc + 1) * P], R0[:, k, t * TT:(t + 1) * TT],
                           k == 0, k == KC - 1, [B_wst[i], B_R0[t][k]], [B_ps[bk]])
                    ri = fc % 2
                    act(rb[ri], psb[bk][:], AF.Relu, [B_ps[bk]], [B_rb[ri]])
                    tt("dve", ub[ui][:, fc, :], rb[ri], rb[ri], ALU.mult, [B_rb[ri]], [B_ub[ui]])

            def do_y(t, ui):
                for dc in range(KC):
                    bk = bankB()
                    for fc in range(4):
                        mm(psb[bk][:], w2st[i][:, fc, dc * P:(dc + 1) * P], ub[ui][:, fc, :],
                           fc == 0, fc == 3, [B_w2st[i], B_ub[ui]], [B_ps[bk]])
                    dst = yacc[:, dc, t * TT:(t + 1) * TT]
                    if g == 0:
                        copy("act", dst, psb[bk][:], [B_ps[bk]], [B_Y[t][dc]])
                    else:
                        tt("dve", dst, psb[bk][:], dst, ALU.add, [B_ps[bk], B_Y[t][dc]], [B_Y[t][dc]])

            do_u(0, nu % 2)
            for t in range(NT):
                if t + 1 < NT:
                    do_u(t + 1, (nu + 1) % 2)
                do_y(t, nu % 2)
                nu += 1


    R12f = R12[:].rearrange("p k t -> p (k t)")
    OT = R12f[:, 0:10240].bitcast(BF16).rearrange("p (k t) -> p k t", k=KC)
    B_OT = [[Buf("OT%d_%d" % (t, k)) for k in range(KC)] for t in range(NT)]
    R2w = R12f[:, 10240:20480]

    def r2(off, n, dt=F32):
        a = R2w[:, off:off + n]
        return a.bitcast(BF16) if dt == BF16 else a

    SCQ = float(192 ** -0.5)

    def mla(l):
        i = l // 2
        win = v3(wv(0, 2560, BF16), KC)
        wq = v3(wv(2560, 2048, BF16), 2)
        wkv = v3(wv(4608, 2048, BF16), 2)
        kpeT = wv(6656, 1408, BF16)
        kTh = wv(8064, 1408, BF16)
        Vh = v3(wv(9472, 1408, BF16), 22)
        qnh = wv(10880, 1280, BF16)
        qrh = wv(12160, 1280, BF16)
        pT = [wv(13440 + j * 256, 256, BF16) for j in range(3)]
        qlatn = v3(r2(0, 2560, BF16), 2)
        ckvall = v3(r2(2560, 2816, BF16), 2)
        tq = v3(r2(5376, 1024), 2)
        tc = v3(r2(6400, 1024), 2)
        tk = r2(7424, 512)
        tkp = r2(7936, 512)
        tmp2 = v3(r2(8448, 1024), 2)
        stgo = r2(8448, 1024).rearrange("p (s f) -> p s f", s=4)
        cst = r2(9472, 512).rearrange("p (s f) -> p s f", s=2)
        kst = r2(9984, 128).rearrange("p (s f) -> p s f", s=2)
        ksto = r2(9472, 256).rearrange("p (s f) -> p s f", s=4)
        B_win, B_wq, B_wkv, B_kpe, B_kT, B_V, B_qn, B_qr = (Buf(n) for n in
                                                           ("win", "wq", "wkv", "kpeT", "kTh", "Vh", "qnh", "qrh"))
        B_pT = [Buf("pT%d" % j) for j in range(3)]
        pacc = r2(8448, 512)
        B_qlat = [Buf("qlat%d" % t) for t in range(NT)]
        B_ckv = [Buf("ckvall%d" % t) for t in range(NT + 1)]
        B_tq, B_tc, B_tk, B_tkp, B_tmp2, B_cst, B_kst = (Buf(n) for n in ("tq", "tc", "tk", "tkp", "tmp2", "cst", "kst"))
        B_pacc = B_tmp2

        S.dma("pool", win[:, :, 0:576], w_in_odd[i].rearrange("(k p) f -> p k f", p=P), [], [B_win], B_win)
        S.dma("pool", win[:, :, 576:608], w_in_odd[i, :, 544:576].rearrange("(k p) f -> p k f", p=P), [], [B_win], B_win)
        S.dma("pool", win[:, :, 608:640], w_in_odd[i, :, 512:544].rearrange("(k p) f -> p k f", p=P), [], [B_win], B_win)
        S.dma("pool", wq[:, :, 0:1536], w_q_b[i].rearrange("(k p) f -> p k f", p=P), [], [B_wq], B_wq)
        wqp = wq[:, :, 1536:2048].rearrange("p k (h d) -> p k h d", d=64)
        wqs = w_q_b[i].rearrange("(k p) (h d) -> p k h d", p=P, d=192)
        for kk in range(2):
            S.dma("pool", wqp[:, kk, :, 0:32], wqs[:, kk, :, 160:192], [], [B_wq], B_wq)
            S.dma("pool", wqp[:, kk, :, 32:64], wqs[:, kk, :, 128:160], [], [B_wq], B_wq)
        S.dma("pool", wkv, w_kv_b[i].rearrange("(k p) f -> p k f", p=P), [], [B_wkv], B_wkv)

        S.dma("sp", cst, cckv[i].rearrange("(s p) f -> p s f", p=P), [], [B_cst], B_cst)
        S.dma("sp", kst, ckpe[i].rearrange("(s p) f -> p s f", p=P), [], [B_kst], B_kst)
        for kc in range(2):
            bk = bankA()
            for s_ in range(2):
                tr(psb[bk][:, s_ * P:(s_ + 1) * P], cst[:, s_, kc * P:(kc + 1) * P], ident[:], [B_cst, B_ident], [B_ps[bk]])
            copy(evac_eng(), ckvall[:, kc, 0:256], psb[bk][:, 0:256], [B_ps[bk]], [B_ckv[0]])
        bk = bankA()
        for s_ in range(2):
            tr(psb[bk][0:64, s_ * P:(s_ + 1) * P], kst[:, s_, :], ident[:], [B_kst, B_ident], [B_ps[bk]])
        copy(evac_eng(), kpeT[0:64, 0:256], psb[bk][0:64, 0:256], [B_ps[bk]], [B_kpe])

        STOP = float(os.environ.get("K_STOP", "9"))
        if STOP <= 1:
            return
        for t in range(NT):
            tsl = slice(t * TT, (t + 1) * TT)
            ksl = slice(256 + t * TT, 256 + (t + 1) * TT)
            for m in range(4):
                bk = bankA()
                for k in range(KC):
                    mm(psb[bk][:], win[:, k, m * P:(m + 1) * P], R0[:, k, tsl], k == 0, k == KC - 1,
                       [B_win, B_R0[t][k]], [B_ps[bk]])
                if m < 2:
                    copy(evac_eng(), tq[:, m, :], psb[bk][:], [B_ps[bk]], [B_tq])
                else:
                    copy(evac_eng(), tc[:, m - 2, :], psb[bk][:], [B_ps[bk]], [B_tc])
            if STOP <= 1.05:
                continue
            bk1 = bankA()
            for k in range(KC):
                mm(psb[bk1][0:64, :], win[:, k, 512:576], R0[:, k, tsl], k == 0, k == KC - 1, [B_win, B_R0[t][k]], [B_ps[bk1]])
            if STOP <= 1.1:
                continue
            if t < 4:
                bk2 = bankA()
                for k in range(KC):
                    mm(psb[bk2][0:64, :], win[:, k, 576:640], R0[:, k, tsl], k == 0, k == KC - 1,
                       [B_win, B_R0[t][k]], [B_ps[bk2]])
                tt("dve", tk[0:64, :], psb[bk1][0:64, :], ropeC[0:64, tsl], ALU.mult, [B_ps[bk1], B_rope], [B_tk])
                tt("dve", tkp[0:64, :], psb[bk2][0:64, :], ropeS[0:64, tsl], ALU.mult, [B_ps[bk2], B_rope], [B_tkp])
                tt("dve", kpeT[0:64, ksl], tk[0:64, :], tkp[0:64, :], ALU.add, [B_tk, B_tkp], [B_kpe])
            else:
                copy("dve", tk[0:64, :], psb[bk1][0:64, :], [B_ps[bk1]], [B_tk])
                copy("act", kpeT[0:64, ksl], tk[0:64, :], [B_tk], [B_kpe])
            if STOP <= 1.2:
                continue
            rr_, Br_ = rms_stats(tq, B_tq, 2, 256, sq2, B_sq2)
            tt("dve", tmp2, tq, rr_[:].unsqueeze(1).to_broadcast([P, 2, TT]), ALU.mult, [B_tq, Br_], [B_tmp2])
            for m in range(2):
                ts("dve", qlatn[:, m, tsl], tmp2[:, m, :], vT[:, R_QAN + i * 2 + m:R_QAN + i * 2 + m + 1], None,
                   ALU.mult, ALU.bypass, [B_tmp2, B_vT], [B_qlat[t]])
            rr_, Br_ = rms_stats(tc, B_tc, 2, 256, sq2, B_sq2)
            tt("dve", tmp2, tc, rr_[:].unsqueeze(1).to_broadcast([P, 2, TT]), ALU.mult, [B_tc, Br_], [B_tmp2])
            for m in range(2):
                ts("dve", tc[:, m, :], tmp2[:, m, :], vT[:, R_KVAN + i * 2 + m:R_KVAN + i * 2 + m + 1], None,
                   ALU.mult, ALU.bypass, [B_tmp2, B_vT], [B_tc])
            copy("act", ckvall[:, :, ksl], tc, [B_tc], [B_ckv[t + 1]])
            if t == 4 and STOP > 1.5:
                for sub in range(4):
                    bk = bankA()
                    for m in range(2):
                        tr(psb[bk][:, m * P:(m + 1) * P], tc[:, m, sub * P:(sub + 1) * P], ident[:],
                           [B_tc, B_ident], [B_ps[bk]])
                    copy(evac_eng(), stgo[:, sub, :], psb[bk][:, 0:256], [B_ps[bk]], [B_tmp2])
                for sq_ in range(2):
                    S.dma("sp", ockv[sq_, i].rearrange("(s p) f -> p s f", p=P), stgo[:, sq_ * 2:sq_ * 2 + 2, :],
                          [B_tmp2], [], B_tmp2)
                for sub in range(4):
                    bk = bankA()
                    tr(psb[bk][:, 0:64], tk[0:64, sub * P:(sub + 1) * P], ident[0:64, 0:64], [B_tk, B_ident], [B_ps[bk]])
                    copy(evac_eng(), ksto[:, sub, :], psb[bk][:, 0:64], [B_ps[bk]], [B_cst])
                for sq_ in range(2):
                    S.dma("sp", okpe[sq_, i].rearrange("(s p) f -> p s f", p=P), ksto[:, sq_ * 2:sq_ * 2 + 2, :],
                          [B_cst], [], B_cst)

        if STOP <= 2:
            return
        qblocks = [(qb * TT, TT, list(range(0, 18))) for qb in range(4)]
        qblocks += [(2048, 256, [18, 19]), (2304, 256, [20, 21])]
        for h in range(8):
            for cb in range(6):
                c0 = cb * 512
                n = min(512, 2816 - c0)
                bk = bankA()
                for kc in range(2):
                    mm(psb[bk][:, 0:n], wkv[:, kc, h * 256:h * 256 + P], ckvall[:, kc, c0:c0 + n], kc == 0, kc == 1,
                       [B_wkv] + B_ckv, [B_ps[bk]])
                copy(evac_eng(), kTh[:, c0:c0 + n], psb[bk][:, 0:n], [B_ps[bk]], [B_kT])
            for g4 in range(6):
                bk = bankA()
                kts = list(range(g4 * 4, min(22, g4 * 4 + 4)))
                for j, kt in enumerate(kts):
                    for kc in range(2):
                        mm(psb[bk][:, j * P:(j + 1) * P], ckvall[:, kc, kt * P:(kt + 1) * P],
                           wkv[:, kc, h * 256 + P:h * 256 + 2 * P], kc == 0, kc == 1, [B_wkv] + B_ckv, [B_ps[bk]])
                nn = len(kts) * P
                copy(evac_eng(), Vh[:, kts[0]:kts[0] + len(kts), :].rearrange("p a b -> p (a b)"), psb[bk][:, 0:nn],
                     [B_ps[bk]], [B_V])
            for t in range(NT):
                tsl = slice(t * TT, (t + 1) * TT)
                bk = bankA()
                for kc in range(2):
                    mm(psb[bk][:], wq[:, kc, h * 192:h * 192 + P], qlatn[:, kc, tsl], kc == 0, kc == 1,
                       [B_wq, B_qlat[t]], [B_ps[bk]])
                act(qnh[:, tsl], psb[bk][:], AF.Copy, [B_ps[bk]], [B_qn], scale=SCQ)
                bk1 = bankA()
                for kc in range(2):
                    mm(psb[bk1][0:64, :], wq[:, kc, h * 192 + P:h * 192 + 192], qlatn[:, kc, tsl], kc == 0, kc == 1,
                       [B_wq, B_qlat[t]], [B_ps[bk1]])
                if t < 4:
                    bk2 = bankA()
                    for kc in range(2):
                        mm(psb[bk2][0:64, :], wq[:, kc, 1536 + h * 64:1536 + (h + 1) * 64], qlatn[:, kc, tsl],
                           kc == 0, kc == 1, [B_wq, B_qlat[t]], [B_ps[bk2]])
                    stt(tk[0:64, :], psb[bk1][0:64, :], SCQ, ropeC[0:64, tsl], ALU.mult, ALU.mult,
                        [B_ps[bk1], B_rope], [B_tk])
                    stt(tkp[0:64, :], psb[bk2][0:64, :], SCQ, ropeS[0:64, tsl], ALU.mult, ALU.mult,
                        [B_ps[bk2], B_rope], [B_tkp])
                    tt("dve", qrh[0:64, tsl], tk[0:64, :], tkp[0:64, :], ALU.add, [B_tk, B_tkp], [B_qr])
                else:
                    act(qrh[0:64, tsl], psb[bk1][0:64, :], AF.Copy, [B_ps[bk1]], [B_qr], scale=SCQ)
            npt = 0
            if STOP <= 3:
                continue
            for (q0, nq, kts) in qblocks:
                if l + 1 < nl:
                    ada_mm(l + 1)
                qsl = slice(q0, q0 + nq)
                tq_ = q0 // TT
                obk = bankB()
                sbk = bankB()
                sbanks = {}

                def s1(kt):
                    b_ = bankA()
                    sbanks[kt] = b_
                    mm(psb[b_][:, 0:nq], kTh[:, kt * P:(kt + 1) * P], qnh[:, qsl], True, False, [B_kT, B_qn], [B_ps[b_]])

                def s2(kt):
                    b_ = sbanks[kt]
                    mm(psb[b_][:, 0:nq], kpeT[0:64, kt * P:(kt + 1) * P], qrh[0:64, qsl], False, True,
                       [B_kpe, B_qr], [B_ps[b_]])

                s1(kts[0])
                if len(kts) > 1:
                    s1(kts[1])
                s2(kts[0])
                if len(kts) > 1:
                    s2(kts[1])
                for j, kt in enumerate(kts):
                    nxt = kts[j + 2] if j + 2 < len(kts) else None
                    if nxt is not None:
                        s1(nxt)
                    pj = npt % 3
                    npt += 1
                    b_ = sbanks[kt]
                    act(pT[pj][:, 0:nq], psb[b_][:, 0:nq], AF.Exp, [B_ps[b_]], [B_pT[pj]])
                    mm(psb[obk][:, 0:nq], Vh[:, kt, :], pT[pj][:, 0:nq], j == 0, j == len(kts) - 1,
                       [B_V, B_pT[pj]], [B_ps[obk]])
                    if nxt is not None:
                        s2(nxt)
                    mm(psb[sbk][:, 0:nq], onesb[:], pT[pj][:, 0:nq], j == 0, j == len(kts) - 1,
                       [B_ones, B_pT[pj]], [B_ps[sbk]])
                act(rstd[:, 0:nq], psb[sbk][:, 0:nq], AF.Ln, [B_ps[sbk]], [B_rstd])
                act(rs1[:, 0:nq], rstd[:, 0:nq], AF.Exp, [B_rstd], [B_rs1], scale=-1.0)
                tt("dve", OT[:, h, qsl], psb[obk][:, 0:nq], rs1[:, 0:nq], ALU.mult, [B_ps[obk], B_rs1], [B_OT[tq_][h]])


    def even(l):
        i = l // 2
        wg = v3(wv(0, 4096, BF16), KC)
        QD = [wv(4096 + d * 1280, 1280, BF16) for d in range(2)]
        KI = [wv(6656 + d * 1280, 1280, BF16) for d in range(2)]
        KU = [v3(wv(9216 + d * 1280, 1280, BF16), 20) for d in range(2)]
        Vtok = v3(wv(11776, 2560, BF16), 20)
        SB = [v3(r2(d * 2560, 2560, BF16), NCH) for d in range(2)]
        qf = r2(5120, 512)
        kf = r2(5632, 512)
        qfs = [r2(5120 + j * 256, 256, BF16) for j in range(2)]
        kfs = [r2(5632 + j * 256, 256, BF16) for j in range(2)]
        t1 = r2(6144, 512)
        t2 = r2(6656, 512)
        t3 = r2(7168, 512)
        kuT = r2(7680, 256, BF16)
        t1B = r2(7936, 512)
        t2B = r2(8448, 512)
        t3B = r2(9600, 512)
        gk2kuT = r2(8960, 256, BF16)
        aTs = [r2(7936 + j * 128, 128, BF16) for j in range(2)]
        Sf = r2(8192, 512).rearrange("p (d q v) -> p d q v", d=2, q=2)
        SGt = v3(r2(8704, 512, BF16), 2)
        gk = r2(9216, 256, BF16)
        wgk = r2(9472, 128, BF16).rearrange("p (d f) -> p d f", d=2)

        def c64(a):
            return a.rearrange("p (c j) -> p c j", j=64)

        B_wg, B_V, B_qf, B_kf, B_t1, B_t2, B_t3, B_kuT, B_SGt, B_gk, B_wgk, B_D, B_rt, B_nb = (
            Buf(n) for n in ("wg", "Vtok", "qf", "kf", "t1", "t2", "t3", "kuT", "SGt", "gk", "wgk", "Dch", "rtab", "negb"))
        B_qfs, B_kfs, B_gks = [Buf("qfA"), Buf("qfB")], [Buf("kfA"), Buf("kfB")], [Buf("gkA"), Buf("gkB")]
        B_t1B, B_t2B, B_t3B = Buf("t1B"), Buf("t2B"), Buf("t3B")
        B_sq2b = Buf("sq2b")
        TS = [((t1, B_t1), (t2, B_t2), (t3, B_t3)), ((t1B, B_t1B), (t2B, B_t2B), (t3B, B_t3B))]
        kuTs = [kuT, gk2kuT]
        B_kuTs = [B_kuT, Buf("kuTB")]
        B_QD = [[Buf("QD") for t in range(NT)] for d in range(2)]
        B_KI = [[Buf("KI") for t in range(NT)] for d in range(2)]
        B_KU = [[Buf("KU") for t in range(NT)] for d in range(2)]
        B_Vt = [Buf("Vt") for t in range(NT)]
        B_SB = [[Buf("SB") for n in range(NCH)] for d in range(2)]
        B_Sf = [[Buf("Sf") for q in range(2)] for d in range(2)]
        B_aT = [Buf("aT0"), Buf("aT1")]

        ts("dve", negb[:], vT[:, R_BGK:R_BGK + 8], -1.0, None, ALU.mult, ALU.bypass, [B_vT], [B_nb])

        def load_wg(g):
            isret = g >= 2
            pg = g % 2
            wsrc = w_in_even[i]

            def wcols(c0, n):
                return wsrc[:, c0:c0 + n].rearrange("(k p) f -> p k f", p=P)

            if not isret:
                cq, ck, cv, cg = pg * 128, 256 + pg * 128, 512 + pg * 256, 1024 + pg * 256
            else:
                cq, ck, cv, cg = 1568 + pg * 128, 1824 + pg * 128, 2080 + pg * 256, 2592 + pg * 256
            S.dma("pool", wg[:, :, 0:128], wcols(cq, 128), [], [B_wg], B_wg)
            S.dma("pool", wg[:, :, 128:256], wcols(ck, 128), [], [B_wg], B_wg)
            S.dma("pool", wg[:, :, 256:512], wcols(cv, 256), [], [B_wg], B_wg)
            S.dma("pool", wg[:, :, 512:768], wcols(cg, 256), [], [B_wg], B_wg)
            if not isret:
                S.dma("pool", wg[:, :, 768:800], wcols(1536, 32), [], [B_wg], B_wg)
                S.op("dve", "memset", (wgk[0:64, :, :], 0.0), {}, [], [B_wgk])
                for d in range(2):
                    for rep in range(2):
                        S.dma("pool", wgk[rep * 32 + d * 16:rep * 32 + (d + 1) * 16, d, :],
                              w_gk2[i, d, :, pg * 128:(pg + 1) * 128], [], [B_wgk], B_wgk)
            else:
                for (dst0, csrc) in ((768, cq), (896, ck)):
                    dv_ = wg[:, :, dst0:dst0 + 128].rearrange("p k (h d) -> p k h d", d=64)
                    sv_ = wsrc[:, csrc:csrc + 128].rearrange("(k p) (h d) -> p k h d", p=P, d=64)
                    for hh in range(2):
                        S.dma("pool", dv_[:, :, hh, 0:32], sv_[:, :, hh, 32:64], [], [B_wg], B_wg)
                        S.dma("pool", dv_[:, :, hh, 32:64], sv_[:, :, hh, 0:32], [], [B_wg], B_wg)

        load_wg(0)
        for g in range(4):
            isret = g >= 2
            pg = g % 2
            if isret:
                for d in range(2):
                    lc = i * 4 + pg * 2 + d
                    xi = 1 if d == 0 else 3
                    yi = 2 if d == 0 else 0
                    act(rtab[:, d, 0, :], iot[:, xi, :], AF.Exp, [B_ec2], [B_rt], scale=lgm[:, lc:lc + 1])
                    act(rtab[:, d, 1, :], iot[:, xi, :], AF.Exp, [B_ec2], [B_rt], scale=lgn[:, lc:lc + 1])
                    act(rtab[:, d, 2, :], iot[:, yi, :], AF.Exp, [B_ec2], [B_rt], scale=lgm[:, lc:lc + 1])
                    act(Dret[:, d:d + 1], iot[:, 3, 0:1], AF.Exp, [B_ec2], [B_rt], scale=lgm[:, lc:lc + 1])
                    copy("dve", Dch[:, d, :], Dret[:, d:d + 1].to_broadcast([P, NCH]), [B_rt], [B_D])

            pbanks = {}

            def PE_PROJ(t):
                tsl = slice(t * TT, (t + 1) * TT)
                rope = isret and t < 4
                g0 = (t % 2) * 32

                def proj(c0, n=128, p0=0):
                    b_ = bankA()
                    for k in range(KC):
                        mm(psb[b_][p0:p0 + n, :], wg[:, k, c0:c0 + n], R0[:, k, tsl], k == 0, k == KC - 1,
                           [B_wg, B_R0[t][k]], [B_ps[b_]])
                    return b_

                bq = proj(0)
                bkk = proj(128)
                bqp = bkp = bgk = None
                if rope:
                    bqp = proj(768)
                    bkp = proj(896)
                if not isret:
                    bgk = proj(768, 32, g0)
                pbanks[t] = (bq, bkk, bqp, bkp, bgk)
                for half in range(2):
                    b_ = bankB()
                    for sub in range(2):
                        blk = t * 4 + half * 2 + sub
                        for k in range(KC):
                            mm(psb[b_][:, sub * 256:(sub + 1) * 256], R0[:, k, blk * P:(blk + 1) * P], wg[:, k, 256:512],
                               k == 0, k == KC - 1, [B_wg, B_R0[t][k]], [B_ps[b_]])
                    blk0 = t * 4 + half * 2
                    copy(evac_eng(), Vtok[:, blk0:blk0 + 2, :].rearrange("p a b -> p (a b)"), psb[b_][:],
                         [B_ps[b_]], [B_Vt[t]])

            def EVAC(t):
                tsl = slice(t * TT, (t + 1) * TT)
                rope = isret and t < 4
                qf, kf, B_qf, B_kf = qfs[t % 2], kfs[t % 2], B_qfs[t % 2], B_kfs[t % 2]
                g0 = (t % 2) * 32
                bq, bkk, bqp, bkp, bgk = pbanks[t]
                if rope:
                    tt("dve", t1, psb[bq][:], ropeC[:, tsl], ALU.mult, [B_ps[bq], B_rope], [B_t1])
                    tt("dve", t2, psb[bqp][:], ropeS[:, tsl], ALU.mult, [B_ps[bqp], B_rope], [B_t2])
                    tt("dve", qf, t1, t2, ALU.add, [B_t1, B_t2], [B_qf])
                    stt(t1, psb[bkk][:], 0.125, ropeC[:, tsl], ALU.mult, ALU.mult, [B_ps[bkk], B_rope], [B_t1])
                    stt(t2, psb[bkp][:], 0.125, ropeS[:, tsl], ALU.mult, ALU.mult, [B_ps[bkp], B_rope], [B_t2])
                    tt("dve", kf, t1, t2, ALU.add, [B_t1, B_t2], [B_kf])
                elif isret:
                    copy("dve", qf, psb[bq][:], [B_ps[bq]], [B_qf])
                    act(kf, psb[bkk][:], AF.Copy, [B_ps[bkk]], [B_kf], scale=0.125)
                else:
                    act(qf, psb[bq][:], AF.Copy, [B_ps[bq]], [B_qf], scale=0.125)
                    copy("dve", kf, psb[bkk][:], [B_ps[bkk]], [B_kf])
                    copy("act", gk[g0:g0 + 32, :], psb[bgk][g0:g0 + 32, :], [B_ps[bgk]], [B_gks[t % 2]])

            def DECAY(t):
                tsl = slice(t * TT, (t + 1) * TT)
                qf, kf, B_qf, B_kf = qfs[t % 2], kfs[t % 2], B_qfs[t % 2], B_kfs[t % 2]
                g0 = (t % 2) * 32
                ME = ("dve", "pool")
                if not isret:
                    bgp = [bankB(), bankB()]
                    Xs, Ys, Es = [None, None], [None, None], [None, None]
                    for d in range(2):
                        mm(psb[bgp[d]][:], wgk[g0:g0 + 32, d, :], gk[g0:g0 + 32, :], True, True,
                           [B_wgk, B_gks[t % 2]], [B_ps[bgp[d]]])
                    for d in range(2):
                        a1, a2, a3 = TS[d]
                        nbc = i * 4 + d * 2 + pg
                        act(a1[0], psb[bgp[d]][:], AF.Exp, [B_ps[bgp[d]], B_nb], [a1[1]],
                            bias=negb[:, nbc:nbc + 1], scale=-1.0)
                    for d in range(2):
                        a1, a2, a3 = TS[d]
                        act(a2[0], a1[0], AF.Ln, [a1[1], B_ec2], [a2[1]], bias=onesf[:, 0:1], scale=1.0)
                    for d in range(2):
                        a1, a2, a3 = TS[d]
                        S.op("dve", "tensor_tensor_scan", (a3[0], resetm[:], a2[0], 0.0, ALU.mult, ALU.add), {},
                             [B_ec, a2[1]], [a3[1]])
                    for d in range(2):
                        a1, a2, a3 = TS[d]
                        tpb = c64(a3[0])[:, :, 63:64].to_broadcast([P, 8, 64])
                        act(Dch[:, d, t * 8:(t + 1) * 8], c64(a3[0])[:, :, 63], AF.Exp, [a3[1]], [B_D], scale=-1.0 / 16)
                        if d == 0:
                            tt("dve", c64(a1[0]), tpb, c64(a3[0]), ALU.subtract, [a3[1]], [a1[1]])
                            Xs[d], Ys[d], Es[d] = a3, a1, a2
                        else:
                            tt("dve", a2[0], a3[0], a2[0], ALU.subtract, [a3[1], a2[1]], [a2[1]])
                            tt("dve", c64(a1[0]), tpb, c64(a2[0]), ALU.subtract, [a3[1], a2[1]], [a1[1]])
                            Xs[d], Ys[d], Es[d] = a1, a2, a3
                    for d in range(2):
                        act(Es[d][0], Xs[d][0], AF.Exp, [Xs[d][1]], [Es[d][1]], scale=-1.0 / 16)
                    for d in range(2):
                        tt(ME[d], QD[d][:, tsl], qf, Es[d][0], ALU.mult, [B_qf, Es[d][1]], [B_QD[d][t]])
                    for d in range(2):
                        act(Es[d][0], Xs[d][0], AF.Exp, [Xs[d][1]], [Es[d][1]], scale=1.0 / 16)
                    for d in range(2):
                        tt(ME[d], KI[d][:, tsl], kf, Es[d][0], ALU.mult, [B_kf, Es[d][1]], [B_KI[d][t]])
                    for d in range(2):
                        act(Es[d][0], Ys[d][0], AF.Exp, [Ys[d][1]], [Es[d][1]], scale=-1.0 / 16)
                    for d in range(2):
                        tt(ME[d], kuTs[d], kf, Es[d][0], ALU.mult, [B_kf, Es[d][1]], [B_kuTs[d]])
                else:
                    for d in range(2):
                        def bc(j):
                            return rtab[:, d, j, :].unsqueeze(1).to_broadcast([P, 8, 64])
                        tt(ME[d], c64(QD[d][:, tsl]), c64(qf), bc(0), ALU.mult, [B_qf, B_rt], [B_QD[d][t]])
                        tt(ME[d], c64(KI[d][:, tsl]), c64(kf), bc(1), ALU.mult, [B_kf, B_rt], [B_KI[d][t]])
                        tt(ME[d], c64(kuTs[d]), c64(kf), bc(2), ALU.mult, [B_kf, B_rt], [B_kuTs[d]])
                for d in range(2):
                    bt = bankB()
                    pst = psb[bt][:].bitcast(BF16)
                    for sub in range(4):
                        tr(pst[:, sub * P:(sub + 1) * P], kuTs[d][:, sub * P:(sub + 1) * P], identb[:],
                           [B_kuTs[d], B_identb], [B_ps[bt]])
                    copy(evac_eng(), KU[d][:, t * 4:(t + 1) * 4, :].rearrange("p a b -> p (a b)"), pst[:, 0:512],
                         [B_ps[bt]], [B_KU[d][t]])

            PE_PROJ(0)
            EVAC(0)
            PE_PROJ(1)
            for t in range(NT):
                if t + 1 < NT:
                    EVAC(t + 1)
                if t + 2 < NT:
                    PE_PROJ(t + 2)
                DECAY(t)
            for t in range(NT):
                tsl = slice(t * TT, (t + 1) * TT)
                for hh in range(2):
                    oi = (4 if isret else 0) + pg * 2 + hh
                    bg_ = bankB()
                    for k in range(KC):
                        mm(psb[bg_][:], wg[:, k, 512 + hh * P:512 + (hh + 1) * P], R0[:, k, tsl], k == 0, k == KC - 1,
                           [B_wg, B_R0[t][k]], [B_ps[bg_]])
                    act(OT[:, oi, tsl], psb[bg_][:], AF.Silu, [B_ps[bg_]], [B_OT[t][oi]])
            S.barrier()
            if g + 1 < 4:
                load_wg(g + 1)

            st_in = sret if isret else sgla
            st_out = oret if isret else ogla
            for si, (c0, c1) in enumerate(((0, 32), (32, 36), (36, 40))):
                pp = [0, 0]
                for d in range(2):
                    cur = Sf[:, d, 0, :]
                    if si == 0:
                        S.dma("sp", cur, st_in[i, d, 2 * pg:2 * pg + 2].rearrange("h k v -> (h k) v"),
                              [], [B_Sf[d][0]], B_Sf[d][0])
                    else:
                        S.op("dve", "memset", (cur, 0.0), {}, [], [B_Sf[d][0]])
                orders = [list(range(c0, c1)), list(range(c1 - 1, c0 - 1, -1))]
                for step in range(c1 - c0):

                    for d in range(2):
                        n = orders[d][step]
                        cur = Sf[:, d, pp[d], :]
                        Bc = B_Sf[d][pp[d]]
                        nxt = Sf[:, d, 1 - pp[d], :]
                        Bn = B_Sf[d][1 - pp[d]]
                        copy("act", SB[d][:, n, :], cur, [Bc], [B_SB[d][n]])
                        blk, r0 = n // 2, (n % 2) * 64
                        tl = n // 8
                        bu = bankA()
                        for hh in range(2):
                            mm(psb[bu][hh * 64:(hh + 1) * 64, 0:P], KU[d][r0:r0 + 64, blk, hh * 64:(hh + 1) * 64],
                               Vtok[r0:r0 + 64, blk, hh * P:(hh + 1) * P], True, True,
                               [B_KU[d][tl], B_Vt[tl]], [B_ps[bu]])
                        stt(nxt, cur, Dch[:, d, n:n + 1], psb[bu][:, 0:P], ALU.mult, ALU.add, [Bc, B_D, B_ps[bu]], [Bn])
                        pp[d] = 1 - pp[d]
                if si > 0:
                    for d in range(2):
                        S.dma("sp", st_out[si - 1, i, d, 2 * pg:2 * pg + 2].rearrange("h k v -> (h k) v"),
                              Sf[:, d, pp[d], :], [B_Sf[d][pp[d]]], [], B_Sf[d][pp[d]])

            for t in range(NT):
                tsl = slice(t * TT, (t + 1) * TT)
                bo = [bankB(), bankB()]
                bj = [bankB(), bankB()]
                chs = [cn for par in range(2) for cn in range(par, 8, 2)]

                def emit_aT(idx):
                    cn = chs[idx]
                    n = t * 8 + cn
                    r0 = (n % 2) * 64
                    csl = slice(n * 64, (n + 1) * 64)
                    rc = slice(r0, r0 + 64)
                    j = idx % 2
                    for hh in range(2):
                        rh = slice(hh * 64, hh * 64 + 64)
                        ba = bankA()
                        for d in range(2):
                            mm(psb[ba][rc, d * 64:(d + 1) * 64], KI[d][rh, csl], QD[d][rh, csl], True, True,
                               [B_KI[d][t], B_QD[d][t]], [B_ps[ba]])
                        tt("dve", aTs[j][rc, hh * P:(hh + 1) * P], psb[ba][rc, 0:P], m4[rc, 0:P], ALU.mult,
                           [B_ps[ba], B_ec], [B_aT[j]])

                emit_aT(0)
                for idx in range(8):
                    if l + 1 < nl and (t * 8 + idx) % 3 == 0 and (t * 8 + idx) < 36:
                        ada_mm(l + 1)
                    if idx + 1 < 8:
                        emit_aT(idx + 1)
                    cn = chs[idx]
                    n = t * 8 + cn
                    blk, r0 = n // 2, (n % 2) * 64
                    csl = slice(n * 64, (n + 1) * 64)
                    rc = slice(r0, r0 + 64)
                    j = idx % 2
                    for hh in range(2):
                        oo = psb[bo[hh]][:, cn * 64:(cn + 1) * 64]
                        vv = Vtok[rc, blk, hh * P:(hh + 1) * P]
                        mm(oo, vv, aTs[j][rc, (hh * 2) * 64:(hh * 2 + 1) * 64], True, False,
                           [B_Vt[t], B_aT[j]], [B_ps[bo[hh]]])
                        mm(oo, vv, aTs[j][rc, (hh * 2 + 1) * 64:(hh * 2 + 2) * 64], False, True,
                           [B_Vt[t], B_aT[j]], [B_ps[bo[hh]]])
                    for hh in range(2):
                        rh = slice(hh * 64, hh * 64 + 64)
                        oj = psb[bj[hh]][:, cn * 64:(cn + 1) * 64]
                        mm(oj, SB[0][rh, n, :], QD[0][rh, csl], True, False, [B_SB[0][n], B_QD[0][t]], [B_ps[bj[hh]]])
                        mm(oj, SB[1][rh, n, :], QD[1][rh, csl], False, True, [B_SB[1][n], B_QD[1][t]], [B_ps[bj[hh]]])
                HB = [((t3, B_t3), (t1, B_t1), (t2, B_t2), (rstd, B_rstd)),
                      ((qf, B_qf), (kf, B_kf), (t3B, B_t3B), (rstd2, B_rstd2))]
                B_sqh = [B_sq2, B_sq2b]
                for hh in range(2):
                    (c_, Bc_), (o_, Bo_), _, _ = HB[hh]
                    copy("act", c_, psb[bo[hh]][:], [B_ps[bo[hh]]], [Bc_])
                for hh in range(2):
                    (c_, Bc_), (o_, Bo_), _, _ = HB[hh]
                    tt("dve", o_, c_, psb[bj[hh]][:], ALU.add, [Bc_, B_ps[bj[hh]]], [Bo_])
                bss = [bankA(), bankA()]
                for hh in range(2):
                    (c_, Bc_), (o_, Bo_), _, _ = HB[hh]
                    act(sq2[:, hh, :], o_, AF.Square, [Bo_], [B_sqh[hh]])
                    mm(psb[bss[hh]][:], onesb[:], sq2[:, hh, :], True, True, [B_ones, B_sqh[hh]], [B_ps[bss[hh]]])
                for hh in range(2):
                    (c_, Bc_), (o_, Bo_), _, (r_, Br_) = HB[hh]
                    act(c_, psb[bss[hh]][:], AF.Ln, [B_ps[bss[hh]], B_eps], [Bc_], bias=epsb[:, 0:1], scale=1.0 / 128)
                for hh in range(2):
                    (c_, Bc_), (o_, Bo_), _, (r_, Br_) = HB[hh]
                    act(r_[:], c_, AF.Exp, [Bc_], [Br_], scale=-0.5)
                for hh in range(2):
                    (c_, Bc_), (o_, Bo_), (m_, Bm_), (r_, Br_) = HB[hh]
                    tt("pool" if hh else "dve", m_, o_, r_[:], ALU.mult, [Bo_, Br_], [Bm_])
                for hh in range(2):
                    (c_, Bc_), (o_, Bo_), (m_, Bm_), (r_, Br_) = HB[hh]
                    if not isret:
                        oi = pg * 2 + hh
                        stt(OT[:, oi, tsl], m_, vT[:, R_GLAN + i:R_GLAN + i + 1], OT[:, oi, tsl],
                            ALU.mult, ALU.mult, [Bm_, B_vT, B_OT[t][oi]], [B_OT[t][oi]])
                    else:
                        oi = 4 + pg * 2 + hh
                        tt("dve", OT[:, oi, tsl], m_, OT[:, oi, tsl], ALU.mult, [Bm_, B_OT[t][oi]], [B_OT[t][oi]])
            S.barrier()

    def c1(l, w_out_l):
        wout = v3(r2(0, 4096, BF16), KC)
        y3 = v3(r2(4096, 4096), KC)
        B_wout = Buf("wout")
        B_y3 = [Buf("y3_%d" % k) for k in range(KC)]
        S.dma("pool", wout, w_out_l.rearrange("(k p) f -> p k f", p=P), [], [B_wout], B_wout)

        def get_y(t):
            tsl = slice(t * TT, (t + 1) * TT)
            for dc in range(KC):
                bk = bankA()
                for k in range(KC):
                    mm(psb[bk][:], wout[:, k, dc * P:(dc + 1) * P], OT[:, k, tsl], k == 0, k == KC - 1,
                       [B_wout, B_OT[t][k]], [B_ps[bk]])
                copy(evac_eng(), y3[:, dc, :], psb[bk][:], [B_ps[bk]], [B_y3[dc]])
            return y3, B_y3

        norm_pipeline(get_y, GG1(l), (GS2(l), 24, l), False, ahead=True)

    def cidx_of(t):
        return 0 if t < 4 else 1

    for t in range(NT):
        stage_x0(t)
    wad0 = [wv(8192 + i * 2048, 2048, BF16).rearrange("p (k f) -> p k f", k=KC) for i in range(2)]
    for pc in range(12):
        i = pc % 2
        Bw = B_tmpf[4 * i:4 * i + 4]
        S.dma("pool", wad0[i], w_ada[0, :, pc * 512:(pc + 1) * 512].rearrange("(k p) f -> p k f", p=P), [], Bw, B_wad[i])
        for jj in range(4):
            j = pc * 4 + jj
            bk = bankA()
            for k in range(KC):
                mm(psb[bk][:, 0:2], wad0[i][:, k, jj * P:(jj + 1) * P], scb[:, k, :], k == 0, k == KC - 1,
                   Bw + [B_sc], [B_ps[bk]])
            tt("dve", mod[:, 0, j, :], psb[bk][:, 0:2],
               vT[:, R_BADA + j:R_BADA + j + 1].to_broadcast([P, 2]), ALU.add, [B_ps[bk], B_vT], [B_mod])
    layer_vecs(0)
    for t in range(NT):
        i = t % 2
        load_x0(t, i)
        store_x(t, i)
        prenorm(xa[i], B_xa[i], GS1(0), 0, 0, cidx_of(t), R0[:, :, t * TT:(t + 1) * TT], B_R0[t])

    for l in range(nl):
        domix = (mix == "all") or (mix == "odd" and l % 2 == 1) or (mix == "even" and l % 2 == 0)
        if domix:
            S.barrier()
            if l % 2 == 1:
                mla(l)
                S.barrier()
                c1(l, w_out_odd[l // 2])
            else:
                even(l)
                S.barrier()
                c1(l, w_out_even[l // 2])
        else:
            for t in range(NT):
                i = t % 2
                load_x(t, i)
                prenorm(xa[i], B_xa[i], GS2(l), 24, l, cidx_of(t), R0[:, :, t * TT:(t + 1) * TT], B_R0[t])
        S.barrier()
        mlp(l)
        if l + 1 < nl:
            while ada_st.setdefault(l + 1, {"dma": 0, "mm": 0})["mm"] < 48:
                ada_mm(l + 1)
            layer_vecs(l + 1)
        S.barrier()
        def get_y2(t):
            return yacc[:, :, t * TT:(t + 1) * TT], B_Y[t]

        if l + 1 < nl:
            norm_pipeline(get_y2, GG2(l), (GS1(l + 1), 0, l + 1), False)
        else:
            norm_pipeline(get_y2, GG2(l), None, True)

    S.barrier()
    S.final_wait("sp", list(S.dma_bufs))

    with nc.Block() as block:
        @block.tensor
        def _(e):
            S.emit("pe", e)

        @block.scalar
        def _(e):
            S.emit("act", e)

        @block.vector
        def _(e):
            S.emit("dve", e)

        @block.gpsimd
        def _(e):
            S.emit("pool", e)

        @block.sync
        def _(e):
            S.emit("sp", e)
    stack.close()
    return nc


def _pack_vecs(inp, b):
    v = np.zeros((R_TOT, P), np.float32)
    v[R_BADA:R_BADA + 192] = inp["b_ada"].reshape(192, P)
    v[R_NMIXPRE:R_NMIXPRE + 32] = inp["norm_mix_pre"].reshape(32, P)
    v[R_NMIXPOST:R_NMIXPOST + 32] = inp["norm_mix_post"].reshape(32, P)
    v[R_NMLPPRE:R_NMLPPRE + 32] = inp["norm_mlp_pre"].reshape(32, P)
    v[R_NMLPPOST:R_NMLPPOST + 32] = inp["norm_mlp_post"].reshape(32, P)
    v[R_COND:R_COND + 8] = inp["c"][b].reshape(8, P)
    v[R_COND + 8:R_COND + 16] = inp["c_ctx"].reshape(8, P)
    v[R_QAN:R_QAN + 4] = inp["q_a_norm"].reshape(4, P)
    v[R_KVAN:R_KVAN + 4] = inp["kv_a_norm"].reshape(4, P)
    v[R_GLAN:R_GLAN + 2] = inp["gla_norm"].reshape(2, P)
    v[R_BGK:R_BGK + 8] = inp["b_gk2"].reshape(8, P)
    return v


def kernel(**inputs):
    inp = {k: np.asarray(v) for k, v in inputs.items()}
    ncores = int(os.environ.get("K_NCORES", "8"))
    nl = int(os.environ.get("K_NL", str(DEPTH)))
    mix = os.environ.get("K_MIX", "all")
    nc = build(nl=nl, mix=mix)
    identf = np.eye(P, dtype=np.float32)
    tpos = np.arange(LS)
    inv = 10000.0 ** (-np.arange(16, dtype=np.float64) / 16.0)
    ang = np.concatenate([(tpos // 64)[:, None] * inv[None, :], (tpos % 64)[:, None] * inv[None, :]], axis=1)
    dd = np.arange(P) % 64
    jj = np.arange(64, dtype=np.float32)
    iot = np.broadcast_to(np.stack([jj, jj + 1, 63 - jj, 64 - jj])[None], (P, 4, 64)).astype(np.float32)
    pj = (np.arange(P) % 64)[:, None]
    cc = np.arange(256)[None, :]
    ii = cc % 64
    dirb = (cc // 64) % 2
    m4 = np.where(dirb == 0, ii >= pj, ii <= pj).astype(np.float32)
    resetm = np.broadcast_to((np.arange(TT) % 64 != 0).astype(np.float32)[None], (P, TT))
    rd = inp["ret_decay"]
    rdec = np.zeros((P, 8), np.float32)
    for i_ in range(2):
        for pg_ in range(2):
            for d_ in range(2):
                for p_ in range(P):
                    rdec[p_, i_ * 4 + pg_ * 2 + d_] = rd[i_, d_, 2 * pg_ + p_ // 64]
    ropeC = np.cos(ang[:, dd % 32]).T.astype(np.float32)
    ropeS = (np.sin(ang[:, dd % 32]).T * np.where(dd < 32, -1.0, 1.0)[:, None]).astype(np.float32)
    in_maps = []
    for b in range(ncores):
        xin = np.concatenate([inp["x_sample"][b], inp["x_prompt"][2 * b], inp["x_prompt"][2 * b + 1]], axis=0)
        in_maps.append({
            "xin": np.ascontiguousarray(xin, dtype=np.float32),
            "vecs": _pack_vecs(inp, b),
            "identf": identf,
            "w_ada": inp["w_ada"], "w_mlp1": inp["w_mlp1"], "w_mlp2": inp["w_mlp2"],
            "w_in_odd": inp["w_in_odd"], "w_q_b": inp["w_q_b"], "w_kv_b": inp["w_kv_b"],
            "w_out_odd": inp["w_out_odd"], "w_out_even": inp["w_out_even"],
            "cckv": np.ascontiguousarray(inp["cache_ckv"][b]), "ckpe": np.ascontiguousarray(inp["cache_kpe"][b]),
            "ropeC": np.ascontiguousarray(ropeC), "ropeS": np.ascontiguousarray(ropeS),
            "w_in_even": inp["w_in_even"], "w_gk2": inp["w_gk2"],
            "sgla": np.ascontiguousarray(inp["state_gla"][b]), "sret": np.ascontiguousarray(inp["state_ret"][b]),
            "rdec": rdec, "iot": np.ascontiguousarray(iot), "m4": np.ascontiguousarray(m4),
            "resetm": np.ascontiguousarray(resetm),
        })
    res = run_bass_kernel_spmd(nc, in_maps, core_ids=list(range(ncores)))
    y_prompt = np.zeros((16, LP, D), np.float32)
    y_sample = np.zeros((8, LS, D), np.float32)
    new_ckv = np.zeros((16, 2, LP, 256), np.float32)
    new_kpe = np.zeros((16, 2, LP, 64), np.float32)
    new_gla = np.zeros((16, 2, 2, 4, 64, 128), np.float32)
    new_ret = np.zeros((16, 2, 2, 4, 64, 128), np.float32)
    for b in range(ncores):
        r = res.results[b]
        ysb = r["ys"]
        y_sample[b] = ysb[0:LS]
        y_prompt[2 * b] = ysb[LS:LS + LP]
        y_prompt[2 * b + 1] = ysb[LS + LP:LS + 2 * LP]
        new_ckv[2 * b:2 * b + 2] = r["ockv"]
        new_kpe[2 * b:2 * b + 2] = r["okpe"]
        if "ogla" in r:
            new_gla[2 * b:2 * b + 2] = r["ogla"]
            new_ret[2 * b:2 * b + 2] = r["oret"]
    return y_prompt, y_sample, new_ckv, new_kpe, new_gla, new_ret
```

```python
import os
from contextlib import ExitStack
import numpy as np
import concourse.bass as bass
import concourse.mybir as mybir
from concourse.bass_utils import run_bass_kernel_spmd

F32 = mybir.dt.float32
BF16 = mybir.dt.bfloat16
AF = mybir.ActivationFunctionType
ALU = mybir.AluOpType

P = 128
D = 1024
KC = 8
TT = 512
NT = 5
NTOK = NT * TT
DEPTH = 4
DFF = 4096
EPS = 1e-6
LS = 2048
LP = 256
NCH = NTOK // 64

R_BADA = 0
R_NMIXPRE = 192
R_NMIXPOST = 224
R_NMLPPRE = 256
R_NMLPPOST = 288
R_COND = 320
R_QAN = 336
R_KVAN = 340
R_GLAN = 344
R_BGK = 346
R_TOT = 384


class Buf:
    __slots__ = ("name", "w", "r", "sem", "cnt")

    def __init__(self, name):
        self.name = name
        self.w = None
        self.r = {}
        self.sem = None
        self.cnt = 0


class Sched:
    ENG = ("pe", "act", "dve", "pool", "sp")

    def __init__(self, nc, stack):
        self.nc = nc
        self.stack = stack
        self.ops = {e: [] for e in self.ENG}
        self.cnt = {e: 0 for e in self.ENG}
        self.waited = {e: {} for e in self.ENG}
        self.sems = {}
        for e in self.ENG:
            self.sems[e] = stack.enter_context(nc.semaphore("s_" + e))
        self.ndsem = 0
        self.dma_bufs = []

    def _deps(self, eng, reads, writes):
        d = {}
        dr = {}
        for b in reads:
            if b.w is not None:
                k, v = b.w
                if d.get(k, 0) < v:
                    d[k] = v
                if dr.get(k, 0) < v:
                    dr[k] = v
        for b in writes:
            if b.w is not None:
                k, v = b.w
                if d.get(k, 0) < v:
                    d[k] = v
            for k, v in b.r.items():
                if d.get(k, 0) < v:
                    d[k] = v
        out = []
        wd = self.waited[eng]
        for k, v in d.items():
            if k == eng and eng == "pe":
                continue
            if wd.get(k, 0) >= v:
                continue
            wonly = dr.get(k, 0) <= wd.get(k, 0)
            wd[k] = v
            out.append((k, v, wonly))
        return out

    def op(self, eng, meth, args, kw, reads=(), writes=()):
        deps = self._deps(eng, reads, writes)
        self.cnt[eng] += 1
        idx = self.cnt[eng]
        self.ops[eng].append((deps, (meth, args, kw), (eng, 1)))
        for b in reads:
            b.r[eng] = idx
        for b in writes:
            b.w = (eng, idx)
            b.r = {}

    def dma(self, q, out, in_, reads, writes, sb):
        fn = ("dma_start", (), dict(out=out, in_=in_))
        deps = self._deps(q, reads, writes)
        if sb.sem is None:
            sb.sem = "d%d" % self.ndsem
            self.ndsem += 1
            self.sems[sb.sem] = self.stack.enter_context(self.nc.semaphore(sb.sem))
            self.dma_bufs.append(sb)
        sb.cnt += 16
        self.ops[q].append((deps, fn, (sb.sem, 16)))
        for b in reads:
            b.r[sb.sem] = sb.cnt
        for b in writes:
            b.w = (sb.sem, sb.cnt)
            b.r = {}

    def final_wait(self, eng, bufs):
        deps = self._deps(eng, bufs, bufs)
        self.ops[eng].append((deps, None, None))

    def barrier(self):
        toks = [(k, self.cnt[k]) for k in self.ENG if self.cnt[k] > 0]
        toks += [(b.sem, b.cnt) for b in self.dma_bufs]
        for e in self.ENG:
            wd = self.waited[e]
            deps = []
            for k, v in toks:
                if k == e:
                    continue
                if wd.get(k, 0) >= v:
                    continue
                wd[k] = v
                deps.append((k, v))
            if deps:
                self.ops[e].append((deps, None, None))

    def emit(self, eng, e):
        for deps, fn, inc in self.ops[eng]:
            if fn is None or fn[0] == "dma_start" or eng in ("sp", "pool"):
                for d_ in deps:
                    e.wait_ge(self.sems[d_[0]], d_[1])
                if fn is None:
                    continue
                meth, args, kw = fn
                ins = getattr(e, meth)(*args, **kw)
            else:
                fus = None
                for j in range(len(deps) - 1, -1, -1):
                    if eng != "pe" or (len(deps[j]) > 2 and deps[j][2]):
                        fus = j
                        break
                for j, d_ in enumerate(deps):
                    if j != fus:
                        e.wait_ge(self.sems[d_[0]], d_[1])
                meth, args, kw = fn
                ins = getattr(e, meth)(*args, **kw)
                if fus is not None:
                    ins._wait_ge(self.sems[deps[fus][0]], deps[fus][1])
            ins.then_inc(self.sems[inc[0]], inc[1])
def build(nl=DEPTH, mix="all"):
    nc = bass.Bass("TRN2", target_bir_lowering=False)
    stack = ExitStack()
    S = Sched(nc, stack)

    def din(name, shape, dt=F32):
        return nc.dram_tensor(name, list(shape), dt, kind="ExternalInput").ap()

    def dout(name, shape, dt=F32):
        return nc.dram_tensor(name, list(shape), dt, kind="ExternalOutput").ap()

    xin = din("xin", [NTOK, D])
    vecs = din("vecs", [R_TOT, P])
    identf = din("identf", [P, P])
    w_ada = din("w_ada", [DEPTH, D, 6 * D])
    w_mlp1 = din("w_mlp1", [DEPTH, D, DFF])
    w_mlp2 = din("w_mlp2", [DEPTH, DFF, D])
    w_in_odd = din("w_in_odd", [2, D, 576])
    w_q_b = din("w_q_b", [2, 256, 1536])
    w_kv_b = din("w_kv_b", [2, 256, 2048])
    w_out_odd = din("w_out_odd", [2, D, D])
    w_out_even = din("w_out_even", [2, D, D])
    cckv = din("cckv", [2, 256, 256])
    ckpe = din("ckpe", [2, 256, 64])
    ropeC_d = din("ropeC", [P, LS])
    ropeS_d = din("ropeS", [P, LS])
    w_in_even = din("w_in_even", [2, D, 3104])
    w_gk2 = din("w_gk2", [2, 2, 16, 256])
    sgla = din("sgla", [2, 2, 4, 64, 128])
    sret = din("sret", [2, 2, 4, 64, 128])
    rdec_d = din("rdec", [P, 8])
    iot_d = din("iot", [P, 4, 64])
    m4_d = din("m4", [P, 256])
    resetm_d = din("resetm", [P, TT])
    ogla = dout("ogla", [2, 2, 2, 4, 64, 128])
    oret = dout("oret", [2, 2, 2, 4, 64, 128])
    ys = dout("ys", [NTOK, D])
    ockv = dout("ockv", [2, 2, LP, 256])
    okpe = dout("okpe", [2, 2, LP, 64])
    xs = nc.dram_tensor("xs", [NT, P, KC, TT], F32).ap()
    xsb = [Buf("xs%d" % t) for t in range(NT)]

    def sb(name, shape, dt):
        return stack.enter_context(nc.sbuf_tensor(name, list(shape), dt))

    R0 = sb("R0", [P, KC, NTOK], BF16)
    R12 = sb("R12", [P, KC, NTOK], F32)
    WA = sb("WA", [P, 14336], F32)
    ident = sb("ident", [P, P], F32)
    identb = sb("identb", [P, P], BF16)
    onesb = sb("onesb", [P, P], BF16)
    onesf32 = sb("onesf32", [P, P], F32)
    B_onesf32 = Buf("onesf32")
    vT = sb("vT", [P, R_TOT], F32)
    vrow = sb("vrow", [P, 3, P], F32)
    sc = sb("sc", [P, KC, 2], F32)
    scb = sb("scb", [P, KC, 2], BF16)
    wad = sb("wad", [P, 2, KC, P], BF16)
    B_wad = [Buf("wad0"), Buf("wad1")]
    mod = sb("mod", [P, DEPTH, 48, 2], F32)
    gvec = sb("gvec", [P, DEPTH, 4, KC, 2], F32)
    rs1 = sb("rs1", [P, TT], F32)
    rstd = sb("rstd", [P, TT], F32)
    rstd2 = sb("rstd2", [P, TT], F32)
    epsb = sb("epsb", [P, 1], F32)
    ropeC = sb("ropeC_s", [P, LS], BF16)
    ropeS = sb("ropeS_s", [P, LS], BF16)
    rdec = sb("rdec_s", [P, 8], F32)
    lgn = sb("lgn", [P, 8], F32)
    lgm = sb("lgm", [P, 8], F32)
    iot = sb("iot_s", [P, 4, 64], F32)
    m4 = sb("m4_s", [P, 256], BF16)
    resetm = sb("resetm_s", [P, TT], BF16)
    onesf = sb("onesf", [P, 1], F32)
    negb = sb("negb", [P, 8], F32)
    Dch = sb("Dch", [P, 2, NCH], F32)
    rtab = sb("rtab", [P, 2, 3, 64], F32)
    Dret = sb("Dret", [P, 2], F32)
    B_ec = Buf("evenconst")
    sq2 = sb("sq2", [P, 2, TT], BF16)
    B_rope, B_sq2 = Buf("rope"), Buf("sq2")

    def wv(off, n, dt=F32):
        a = WA[:, off:off + n]
        return a.bitcast(BF16) if dt == BF16 else a

    def v3(a, k):
        return a.rearrange("p (k t) -> p k t", k=k)

    xa = [v3(wv(i * 4096, 4096), KC) for i in range(2)]
    tmpf = v3(wv(8192, 4096), KC)
    stv = wv(8192, 4096).rearrange("p (s d) -> p s d", s=4)
    sqb = v3(wv(12288, 2048, BF16), KC)
    wstf = [v3(wv(i * 2048, 2048), KC) for i in range(2)]
    w1v = [v3(wv(i * 2048, 2048, BF16), KC) for i in range(2)]
    w2st = [v3(wv(4096 + i * 2048, 2048, BF16), 4) for i in range(2)]
    ub = [v3(wv(8192 + i * 1024, 1024, BF16), 4) for i in range(2)]
    rb = [wv(10240 + i * 256, 256, BF16) for i in range(2)]
    yacc = R12

    B_R0 = [[Buf("R0_%d_%d" % (t, k)) for k in range(KC)] for t in range(NT)]
    B_Y = [[Buf("Y_%d_%d" % (t, k)) for k in range(KC)] for t in range(NT)]
    B_ident, B_identb, B_ones, B_vT, B_vrow, B_sc, B_mod, B_g, B_eps = (
        Buf(n) for n in ("ident", "identb", "ones", "vT", "vrow", "sc", "mod", "gv", "eps"))
    B_xa = [[Buf("xa%d_%d" % (i, k)) for k in range(KC)] for i in range(2)]
    B_tmpf = [Buf("tmpf%d" % k) for k in range(KC)]
    B_sqb = [Buf("sqb%d" % k) for k in range(KC)]
    B_rs1, B_rstd = Buf("rs1"), Buf("rstd")
    B_rstd2 = Buf("rstd2")
    rsel = {"n": 0}
    B_xadma = [Buf("xadma0"), Buf("xadma1")]
    B_stg = Buf("stgdma")
    B_wst = [Buf("wst0"), Buf("wst1")]
    B_wsta = [Buf("wsta0"), Buf("wsta1")]
    B_w2st = [Buf("w2st0"), Buf("w2st1")]
    B_ub = [Buf("ub0"), Buf("ub1")]
    B_rb = [Buf("rb0"), Buf("rb1")]

    psb = [stack.enter_context(nc.psum_tensor("ps%d" % i, [P, 512], F32)) for i in range(8)]
    B_ps = [Buf("ps%d" % i) for i in range(8)]
    rr = {"A": 0, "B": 0, "ev": 0}

    def bankA():
        i = rr["A"]
        rr["A"] = (i + 1) % 4
        return i

    def bankB():
        i = 4 + rr["B"]
        rr["B"] = (rr["B"] + 1) % 4
        return i

    def evac_eng():
        rr["ev"] ^= 1
        return "act" if rr["ev"] else "dve"

    def mm(out, lhsT, rhs, start, stop, reads, writes):
        S.op("pe", "matmul", (out,), dict(lhsT=lhsT, rhs=rhs, start=start, stop=stop), reads, writes)

    def tr(out, in_, idn, reads, writes):
        S.op("pe", "transpose", (out, in_, idn), {}, reads, writes)

    def act(out, in_, func, reads, writes, **kw):
        S.op("act", "activation", (out, in_, func), kw, reads, writes)

    def tt(eng, out, in0, in1, op, reads, writes):
        S.op(eng, "tensor_tensor", (out, in0, in1, op), {}, reads, writes)

    def stt(out, in0, scalar, in1, op0, op1, reads, writes):
        S.op("dve", "scalar_tensor_tensor", (out, in0, scalar, in1, op0, op1), {}, reads, writes)

    def ts(eng, out, in0, s1, s2, op0, op1, reads, writes):
        S.op(eng, "tensor_scalar", (out, in0, s1, s2, op0, op1), {}, reads, writes)

    def copy(eng, out, in_, reads, writes):
        if eng == "act":
            act(out, in_, AF.Copy, reads, writes)
        else:
            S.op(eng, "tensor_copy", (out, in_), {}, reads, writes)

    S.dma("sp", ident[:], identf[:, :], [], [B_ident], B_ident)
    copy("dve", identb[:], ident[:], [B_ident], [B_identb])
    S.op("pool", "memset", (onesb[:], 1.0), {}, [], [B_ones])
    S.op("pool", "memset", (onesf32[:], 1.0), {}, [], [B_onesf32])
    S.op("pool", "memset", (epsb[:], EPS), {}, [], [B_eps])

    S.dma("pool", ropeC[:], ropeC_d[:, :], [], [B_rope], B_rope)
    S.dma("pool", ropeS[:], ropeS_d[:, :], [], [B_rope], B_rope)
    S.dma("pool", m4[:], m4_d[:, :], [], [B_ec], B_ec)
    S.dma("pool", resetm[:], resetm_d[:, :], [], [B_ec], B_ec)
    B_ec2 = Buf("evenconst2")
    S.dma("sp", rdec[:], rdec_d[:, :], [], [B_ec2], B_ec2)
    S.dma("sp", iot[:], iot_d[:, :, :], [], [B_ec2], B_ec2)
    S.op("pool", "memset", (onesf[:], 1.0), {}, [], [B_ec2])
    act(lgn[:], rdec[:], AF.Exp, [B_ec2], [B_ec2])
    ts("dve", lgm[:], lgn[:], -1.0, None, ALU.mult, ALU.bypass, [B_ec2], [B_ec2])

    S.dma("sp", vrow[:], vecs.rearrange("(a p) f -> p a f", p=P), [], [B_vrow], B_vrow)
    for a in range(3):
        bk = bankA()
        tr(psb[bk][:, 0:P], vrow[:, a, :], ident[:], [B_vrow, B_ident], [B_ps[bk]])
        copy("dve", vT[:, a * P:(a + 1) * P], psb[bk][:, 0:P], [B_ps[bk]], [B_vT])
    for c in range(2):
        act(sc[:, :, c], vT[:, R_COND + c * 8:R_COND + c * 8 + 8], AF.Silu, [B_vT], [B_sc])

    copy("dve", scb[:], sc[:], [B_sc], [B_sc])
    adac = {"n": 0}

    ada_st = {}

    def ada_dma(l):
        st = ada_st.setdefault(l, {"dma": 0, "mm": 0})
        pc = st["dma"]
        if pc >= 48:
            return
        i = pc % 2
        st["dma"] += 1
        S.dma("pool", wad[:, i], w_ada[l, :, pc * P:(pc + 1) * P].rearrange("(k p) f -> p k f", p=P),
              [], [B_wad[i]], B_wad[i])

    def ada_mm(l):
        st = ada_st.setdefault(l, {"dma": 0, "mm": 0})
        pc = st["mm"]
        if pc >= 48:
            return
        if st["dma"] <= pc:
            ada_dma(l)
        i = pc % 2
        st["mm"] += 1
        bk = bankA()
        for k in range(KC):
            mm(psb[bk][:, 0:2], wad[:, i, k, :], scb[:, k, :], k == 0, k == KC - 1, [B_wad[i], B_sc], [B_ps[bk]])
        tt("dve", mod[:, l, pc, :], psb[bk][:, 0:2],
           vT[:, R_BADA + l * 48 + pc:R_BADA + l * 48 + pc + 1].to_broadcast([P, 2]), ALU.add,
           [B_ps[bk], B_vT], [B_mod])
        if st["dma"] - st["mm"] < 1:
            ada_dma(l)

    def ada_piece(l, pc):
        ada_mm(l)

    def rms_stats(src, B_src, nk, dfeat, sq=None, B_sq=None):
        if sq is None:
            sq, B_sq = sqb, B_sqb
        bk = bankA()
        for k in range(nk):
            bs_ = B_src[k] if isinstance(B_src, list) else B_src
            bq_ = B_sq[k] if isinstance(B_sq, list) else B_sq
            act(sq[:, k, :], src[:, k, :], AF.Square, [bs_], [bq_])
            mm(psb[bk][:], onesb[:], sq[:, k, :], k == 0, k == nk - 1, [B_ones, bq_], [B_ps[bk]])
        rsel["n"] ^= 1
        rr_, Br_ = (rstd, B_rstd) if rsel["n"] else (rstd2, B_rstd2)
        act(rs1[:], psb[bk][:], AF.Ln, [B_ps[bk], B_eps], [B_rs1], bias=epsb[:, 0:1], scale=1.0 / dfeat)
        act(rr_[:], rs1[:], AF.Exp, [B_rs1], [Br_], scale=-0.5)
        return rr_, Br_

    def pre_apply(xt, B_xt, rr_, Br_, gs, sh_j0, l, cidx, out3, B_out):
        for k in range(KC):
            sc_ap = gs[:, k, cidx:cidx + 1]
            bi_ap = mod[:, l, sh_j0 + k, cidx:cidx + 1]
            tt("dve", tmpf[:, k, :], xt[:, k, :], rr_[:], ALU.mult, [B_xt[k], Br_], [B_tmpf[k]])
            if k % 4 == 3:
                ts("dve", out3[:, k, :], tmpf[:, k, :], sc_ap, bi_ap, ALU.mult, ALU.add, [B_tmpf[k], B_g, B_mod], [B_out[k]])
            else:
                act(out3[:, k, :], tmpf[:, k, :], AF.Identity, [B_tmpf[k], B_g, B_mod], [B_out[k]], bias=bi_ap, scale=sc_ap)

    def prenorm(xt, B_xt, gs, sh_j0, l, cidx, out3, B_out):
        rr_, Br_ = rms_stats(xt, B_xt, KC, D)
        pre_apply(xt, B_xt, rr_, Br_, gs, sh_j0, l, cidx, out3, B_out)

    def post_apply(y3, B_y, rr_, Br_, gg, cidx, xt, B_xt):
        for k in range(KC):
            tt("dve", tmpf[:, k, :], y3[:, k, :], rr_[:], ALU.mult, [B_y[k], Br_], [B_tmpf[k]])
            stt(xt[:, k, :], tmpf[:, k, :], gg[:, k, cidx:cidx + 1], xt[:, k, :], ALU.mult, ALU.add,
                [B_tmpf[k], B_g, B_xt[k]], [B_xt[k]])

    def postnorm_res(y3, B_y, gg, cidx, xt, B_xt):
        rr_, Br_ = rms_stats(y3, B_y, KC, D)
        post_apply(y3, B_y, rr_, Br_, gg, cidx, xt, B_xt)

    def norm_pipeline(get_y, gg, pre_args, final, ahead=False):
        pend = None
        ynext = get_y(0) if ahead else None
        for t in range(NT):
            i = t % 2
            load_x(t, i)
            y3, B_y = ynext if ahead else get_y(t)
            ra, Bra = rms_stats(y3, B_y, KC, D)
            if pend is not None:
                pend()
            post_apply(y3, B_y, ra, Bra, gg, cidx_of(t), xa[i], B_xa[i])
            if ahead and t + 1 < NT:
                ynext = get_y(t + 1)
            if final:
                pend = (lambda t=t, i=i: store_y(t, i))
            else:
                store_x(t, i)
                rc_, Brc = rms_stats(xa[i], B_xa[i], KC, D)
                gs_, sh_j0, l_ = pre_args
                pend = (lambda t=t, i=i, rc_=rc_, Brc=Brc: pre_apply(
                    xa[i], B_xa[i], rc_, Brc, gs_, sh_j0, l_, cidx_of(t), R0[:, :, t * TT:(t + 1) * TT], B_R0[t]))
        pend()

    B_stg0 = [Buf("stg0_%d" % t) for t in range(NT)]

    def stage_x0(t):
        st0 = R12[:].rearrange("p k t -> p (k t)")[:, t * 4096:(t + 1) * 4096].rearrange("p (s d) -> p s d", s=4)
        S.dma("sp", st0, xin[t * TT:(t + 1) * TT, :].rearrange("(s p) d -> p s d", p=P), [], [B_stg0[t]], B_stg0[t])

    def load_x0(t, i):
        st0 = R12[:].rearrange("p k t -> p (k t)")[:, t * 4096:(t + 1) * 4096].rearrange("p (s d) -> p s d", s=4)
        for k in range(KC):
            bk = bankA()
            for s_ in range(4):
                tr(psb[bk][:, s_ * P:(s_ + 1) * P], st0[:, s_, k * P:(k + 1) * P], ident[:],
                   [B_stg0[t], B_ident], [B_ps[bk]])
            copy(evac_eng(), xa[i][:, k, :], psb[bk][:], [B_ps[bk]], [B_xa[i][k]])

    def store_y(t, i):
        for s_ in range(4):
            for hh in range(2):
                bk = bankA()
                for kk in range(4):
                    k = hh * 4 + kk
                    tr(psb[bk][:, kk * P:(kk + 1) * P], xa[i][:, k, s_ * P:(s_ + 1) * P], ident[:],
                       [B_xa[i][k], B_ident], [B_ps[bk]])
                copy(evac_eng(), stv[:, s_, hh * 512:(hh + 1) * 512], psb[bk][:], [B_ps[bk]], B_tmpf)
        S.dma("sp", ys[t * TT:(t + 1) * TT, :].rearrange("(s p) d -> p s d", p=P), stv, B_tmpf, [], B_stg)

    def load_x(t, i):
        S.dma("sp", xa[i], xs[t], [xsb[t]], B_xa[i], B_xadma[i])

    def store_x(t, i):
        S.dma("sp", xs[t], xa[i], B_xa[i], [xsb[t]], B_xadma[i])

    def layer_vecs(l):
        for c in range(2):
            stt(gvec[:, l, 0, :, c], mod[:, l, 8:16, c], 1.0, vT[:, R_NMIXPRE + l * 8:R_NMIXPRE + l * 8 + 8],
                ALU.add, ALU.mult, [B_mod, B_vT], [B_g])
            tt("dve", gvec[:, l, 1, :, c], mod[:, l, 16:24, c], vT[:, R_NMIXPOST + l * 8:R_NMIXPOST + l * 8 + 8],
               ALU.mult, [B_mod, B_vT], [B_g])
            stt(gvec[:, l, 2, :, c], mod[:, l, 32:40, c], 1.0, vT[:, R_NMLPPRE + l * 8:R_NMLPPRE + l * 8 + 8],
                ALU.add, ALU.mult, [B_mod, B_vT], [B_g])
            tt("dve", gvec[:, l, 3, :, c], mod[:, l, 40:48, c], vT[:, R_NMLPPOST + l * 8:R_NMLPPOST + l * 8 + 8],
               ALU.mult, [B_mod, B_vT], [B_g])

    def GS1(l): return gvec[:, l, 0]
    def GG1(l): return gvec[:, l, 1]
    def GS2(l): return gvec[:, l, 2]
    def GG2(l): return gvec[:, l, 3]

    def mlp(l):
        nu = 0
        for g in range(8):
            i = g % 2
            S.dma("pool", w1v[i], w_mlp1[l, :, g * 512:(g + 1) * 512].rearrange("(k p) f -> p k f", p=P),
                  [], [B_wst[i]], B_wst[i])
            S.dma("pool", w2st[i], w_mlp2[l, g * 512:(g + 1) * 512, :].rearrange("(c p) d -> p c d", p=P),
                  [], [B_w2st[i]], B_w2st[i])

            def do_u(t, ui):
                for fc in range(4):
                    bk = bankA()
                    for k in range(KC):
                        mm(psb[bk][:], w1v[i][:, k, fc * P:(fc + 1) * P], R0[:, k, t * TT:(t + 1) * TT],
                           k == 0, k == KC - 1, [B_wst[i], B_R0[t][k]], [B_ps[bk]])
                    ri = fc % 2
                    act(rb[ri], psb[bk][:], AF.Relu, [B_ps[bk]], [B_rb[ri]])
                    tt("dve", ub[ui][:, fc, :], rb[ri], rb[ri], ALU.mult, [B_rb[ri]], [B_ub[ui]])

            def do_y(t, ui):
                for dc in range(KC):
                    bk = bankB()
                    for fc in range(4):
                        mm(psb[bk][:], w2st[i][:, fc, dc * P:(dc + 1) * P], ub[ui][:, fc, :],
                           fc == 0, fc == 3, [B_w2st[i], B_ub[ui]], [B_ps[bk]])
                    dst = yacc[:, dc, t * TT:(t + 1) * TT]
                    if g == 0:
                        copy("act", dst, psb[bk][:], [B_ps[bk]], [B_Y[t][dc]])
                    else:
                        tt("dve", dst, psb[bk][:], dst, ALU.add, [B_ps[bk], B_Y[t][dc]], [B_Y[t][dc]])

            do_u(0, nu % 2)
            for t in range(NT):
                if t + 1 < NT:
                    do_u(t + 1, (nu + 1) % 2)
                do_y(t, nu % 2)
                nu += 1


    R12f = R12[:].rearrange("p k t -> p (k t)")
    OT = R12f[:, 0:10240].bitcast(BF16).rearrange("p (k t) -> p k t", k=KC)
    B_OT = [[Buf("OT%d_%d" % (t, k)) for k in range(KC)] for t in range(NT)]
    R2w = R12f[:, 10240:20480]

    def r2(off, n, dt=F32):
        a = R2w[:, off:off + n]
        return a.bitcast(BF16) if dt == BF16 else a

    SCQ = float(192 ** -0.5)

    def mla(l):
        i = l // 2
        win = v3(wv(0, 2560, BF16), KC)
        wq = v3(wv(2560, 2048, BF16), 2)
        wkv = v3(wv(4608, 2048, BF16), 2)
        kpeT = wv(6656, 1408, BF16)
        kTh = wv(8064, 1408, BF16)
        Vh = v3(wv(9472, 1408, BF16), 22)
        qnh = wv(10880, 1280, BF16)
        qrh = wv(12160, 1280, BF16)
        pT = [wv(13440 + j * 256, 256, BF16) for j in range(3)]
        qlatn = v3(r2(0, 2560, BF16), 2)
        ckvall = v3(r2(2560, 2816, BF16), 2)
        tq = v3(r2(5376, 1024), 2)
        tc = v3(r2(6400, 1024), 2)
        tk = r2(7424, 512)
        tkp = r2(7936, 512)
        tmp2 = v3(r2(8448, 1024), 2)
        stgo = r2(8448, 1024).rearrange("p (s f) -> p s f", s=4)
        cst = r2(9472, 512).rearrange("p (s f) -> p s f", s=2)
        kst = r2(9984, 128).rearrange("p (s f) -> p s f", s=2)
        ksto = r2(9472, 256).rearrange("p (s f) -> p s f", s=4)
        B_win, B_wq, B_wkv, B_kpe, B_kT, B_V, B_qn, B_qr = (Buf(n) for n in
                                                           ("win", "wq", "wkv", "kpeT", "kTh", "Vh", "qnh", "qrh"))
        B_pT = [Buf("pT%d" % j) for j in range(3)]
        B_kpz = Buf("kpzero")
        pacc = r2(8448, 512)
        B_qlat = [Buf("qlat%d" % t) for t in range(NT)]
        B_ckv = [Buf("ckvall%d" % t) for t in range(NT + 1)]
        B_tq, B_tc, B_tk, B_tkp, B_tmp2, B_cst, B_kst = (Buf(n) for n in ("tq", "tc", "tk", "tkp", "tmp2", "cst", "kst"))
        B_pacc = B_tmp2

        S.op("dve", "memset", (kpeT[64:128, :], 0.0), {}, [], [B_kpz])
        S.op("dve", "memset", (qrh[64:128, :], 0.0), {}, [], [B_kpz])
        S.dma("pool", win[:, :, 0:576], w_in_odd[i].rearrange("(k p) f -> p k f", p=P), [], [B_win], B_win)
        S.dma("pool", win[:, :, 576:608], w_in_odd[i, :, 544:576].rearrange("(k p) f -> p k f", p=P), [], [B_win], B_win)
        S.dma("pool", win[:, :, 608:640], w_in_odd[i, :, 512:544].rearrange("(k p) f -> p k f", p=P), [], [B_win], B_win)
        S.dma("pool", wq[:, :, 0:1536], w_q_b[i].rearrange("(k p) f -> p k f", p=P), [], [B_wq], B_wq)
        wqp = wq[:, :, 1536:2048].rearrange("p k (h d) -> p k h d", d=64)
        wqs = w_q_b[i].rearrange("(k p) (h d) -> p k h d", p=P, d=192)
        for kk in range(2):
            S.dma("pool", wqp[:, kk, :, 0:32], wqs[:, kk, :, 160:192], [], [B_wq], B_wq)
            S.dma("pool", wqp[:, kk, :, 32:64], wqs[:, kk, :, 128:160], [], [B_wq], B_wq)
        S.dma("pool", wkv, w_kv_b[i].rearrange("(k p) f -> p k f", p=P), [], [B_wkv], B_wkv)

        S.dma("sp", cst, cckv[i].rearrange("(s p) f -> p s f", p=P), [], [B_cst], B_cst)
        S.dma("sp", kst, ckpe[i].rearrange("(s p) f -> p s f", p=P), [], [B_kst], B_kst)
        for kc in range(2):
            bk = bankA()
            for s_ in range(2):
                tr(psb[bk][:, s_ * P:(s_ + 1) * P], cst[:, s_, kc * P:(kc + 1) * P], ident[:], [B_cst, B_ident], [B_ps[bk]])
            copy(evac_eng(), ckvall[:, kc, 0:256], psb[bk][:, 0:256], [B_ps[bk]], [B_ckv[0]])
        bk = bankA()
        for s_ in range(2):
            tr(psb[bk][0:64, s_ * P:(s_ + 1) * P], kst[:, s_, :], ident[:], [B_kst, B_ident], [B_ps[bk]])
        copy(evac_eng(), kpeT[0:64, 0:256], psb[bk][0:64, 0:256], [B_ps[bk]], [B_kpe])

        STOP = float(os.environ.get("K_STOP", "9"))
        if STOP <= 1:
            return
        for t in range(NT):
            tsl = slice(t * TT, (t + 1) * TT)
            ksl = slice(256 + t * TT, 256 + (t + 1) * TT)
            for m in range(4):
                bk = bankA()
                for k in range(KC):
                    mm(psb[bk][:], win[:, k, m * P:(m + 1) * P], R0[:, k, tsl], k == 0, k == KC - 1,
                       [B_win, B_R0[t][k]], [B_ps[bk]])
                if m < 2:
                    copy(evac_eng(), tq[:, m, :], psb[bk][:], [B_ps[bk]], [B_tq])
                else:
                    copy(evac_eng(), tc[:, m - 2, :], psb[bk][:], [B_ps[bk]], [B_tc])
            if STOP <= 1.05:
                continue
            bk1 = bankA()
            for k in range(KC):
                mm(psb[bk1][0:64, :], win[:, k, 512:576], R0[:, k, tsl], k == 0, k == KC - 1, [B_win, B_R0[t][k]], [B_ps[bk1]])
            if STOP <= 1.1:
                continue
            if t < 4:
                bk2 = bankA()
                for k in range(KC):
                    mm(psb[bk2][0:64, :], win[:, k, 576:640], R0[:, k, tsl], k == 0, k == KC - 1,
                       [B_win, B_R0[t][k]], [B_ps[bk2]])
                tt("dve", tk[0:64, :], psb[bk1][0:64, :], ropeC[0:64, tsl], ALU.mult, [B_ps[bk1], B_rope], [B_tk])
                tt("dve", tkp[0:64, :], psb[bk2][0:64, :], ropeS[0:64, tsl], ALU.mult, [B_ps[bk2], B_rope], [B_tkp])
                tt("dve", kpeT[0:64, ksl], tk[0:64, :], tkp[0:64, :], ALU.add, [B_tk, B_tkp], [B_kpe])
            else:
                copy("dve", tk[0:64, :], psb[bk1][0:64, :], [B_ps[bk1]], [B_tk])
                copy("act", kpeT[0:64, ksl], tk[0:64, :], [B_tk], [B_kpe])
            if STOP <= 1.2:
                continue
            rr_, Br_ = rms_stats(tq, B_tq, 2, 256, sq2, B_sq2)
            tt("dve", tmp2, tq, rr_[:].unsqueeze(1).to_broadcast([P, 2, TT]), ALU.mult, [B_tq, Br_], [B_tmp2])
            for m in range(2):
                ts("dve", qlatn[:, m, tsl], tmp2[:, m, :], vT[:, R_QAN + i * 2 + m:R_QAN + i * 2 + m + 1], None,
                   ALU.mult, ALU.bypass, [B_tmp2, B_vT], [B_qlat[t]])
            rr_, Br_ = rms_stats(tc, B_tc, 2, 256, sq2, B_sq2)
            tt("dve", tmp2, tc, rr_[:].unsqueeze(1).to_broadcast([P, 2, TT]), ALU.mult, [B_tc, Br_], [B_tmp2])
            for m in range(2):
                ts("dve", tc[:, m, :], tmp2[:, m, :], vT[:, R_KVAN + i * 2 + m:R_KVAN + i * 2 + m + 1], None,
                   ALU.mult, ALU.bypass, [B_tmp2, B_vT], [B_tc])
            copy("act", ckvall[:, :, ksl], tc, [B_tc], [B_ckv[t + 1]])
            if t == 4 and STOP > 1.5:
                for sub in range(4):
                    bk = bankA()
                    for m in range(2):
                        tr(psb[bk][:, m * P:(m + 1) * P], tc[:, m, sub * P:(sub + 1) * P], ident[:],
                           [B_tc, B_ident], [B_ps[bk]])
                    copy(evac_eng(), stgo[:, sub, :], psb[bk][:, 0:256], [B_ps[bk]], [B_tmp2])
                for sq_ in range(2):
                    S.dma("sp", ockv[sq_, i].rearrange("(s p) f -> p s f", p=P), stgo[:, sq_ * 2:sq_ * 2 + 2, :],
                          [B_tmp2], [], B_tmp2)
                for sub in range(4):
                    bk = bankA()
                    tr(psb[bk][:, 0:64], tk[0:64, sub * P:(sub + 1) * P], ident[0:64, 0:64], [B_tk, B_ident], [B_ps[bk]])
                    copy(evac_eng(), ksto[:, sub, :], psb[bk][:, 0:64], [B_ps[bk]], [B_cst])
                for sq_ in range(2):
                    S.dma("sp", okpe[sq_, i].rearrange("(s p) f -> p s f", p=P), ksto[:, sq_ * 2:sq_ * 2 + 2, :],
                          [B_cst], [], B_cst)

        if STOP <= 2:
            return
        qblocks = [(qb * TT, TT, list(range(0, 18))) for qb in range(4)]
        qblocks += [(2048, 256, [18, 19]), (2304, 256, [20, 21])]
        for h in range(8):
            for cb in range(6):
                c0 = cb * 512
                n = min(512, 2816 - c0)
                bk = bankA()
                for kc in range(2):
                    mm(psb[bk][:, 0:n], wkv[:, kc, h * 256:h * 256 + P], ckvall[:, kc, c0:c0 + n], kc == 0, kc == 1,
                       [B_wkv] + B_ckv, [B_ps[bk]])
                copy(evac_eng(), kTh[:, c0:c0 + n], psb[bk][:, 0:n], [B_ps[bk]], [B_kT])
            for g4 in range(6):
                bk = bankA()
                kts = list(range(g4 * 4, min(22, g4 * 4 + 4)))
                for j, kt in enumerate(kts):
                    for kc in range(2):
                        mm(psb[bk][:, j * P:(j + 1) * P], ckvall[:, kc, kt * P:(kt + 1) * P],
                           wkv[:, kc, h * 256 + P:h * 256 + 2 * P], kc == 0, kc == 1, [B_wkv] + B_ckv, [B_ps[bk]])
                nn = len(kts) * P
                copy(evac_eng(), Vh[:, kts[0]:kts[0] + len(kts), :].rearrange("p a b -> p (a b)"), psb[bk][:, 0:nn],
                     [B_ps[bk]], [B_V])
            for t in range(NT):
                tsl = slice(t * TT, (t + 1) * TT)
                bk = bankA()
                for kc in range(2):
                    mm(psb[bk][:], wq[:, kc, h * 192:h * 192 + P], qlatn[:, kc, tsl], kc == 0, kc == 1,
                       [B_wq, B_qlat[t]], [B_ps[bk]])
                act(qnh[:, tsl], psb[bk][:], AF.Copy, [B_ps[bk]], [B_qn], scale=SCQ)
                bk1 = bankA()
                for kc in range(2):
                    mm(psb[bk1][0:64, :], wq[:, kc, h * 192 + P:h * 192 + 192], qlatn[:, kc, tsl], kc == 0, kc == 1,
                       [B_wq, B_qlat[t]], [B_ps[bk1]])
                if t < 4:
                    bk2 = bankA()
                    for kc in range(2):
                        mm(psb[bk2][0:64, :], wq[:, kc, 1536 + h * 64:1536 + (h + 1) * 64], qlatn[:, kc, tsl],
                           kc == 0, kc == 1, [B_wq, B_qlat[t]], [B_ps[bk2]])
                    stt(tk[0:64, :], psb[bk1][0:64, :], SCQ, ropeC[0:64, tsl], ALU.mult, ALU.mult,
                        [B_ps[bk1], B_rope], [B_tk])
                    stt(tkp[0:64, :], psb[bk2][0:64, :], SCQ, ropeS[0:64, tsl], ALU.mult, ALU.mult,
                        [B_ps[bk2], B_rope], [B_tkp])
                    tt("dve", qrh[0:64, tsl], tk[0:64, :], tkp[0:64, :], ALU.add, [B_tk, B_tkp], [B_qr])
                else:
                    act(qrh[0:64, tsl], psb[bk1][0:64, :], AF.Copy, [B_ps[bk1]], [B_qr], scale=SCQ)
            npt = 0
            if STOP <= 3:
                continue
            for (q0, nq, kts) in qblocks:
                if l + 1 < nl:
                    ada_mm(l + 1)
                qsl = slice(q0, q0 + nq)
                tq_ = q0 // TT
                obk = bankB()
                sbk = bankB()
                sbanks = {}

                def smm(kt):
                    b_ = bankA()
                    sbanks[kt] = b_
                    mm(psb[b_][:, 0:nq], kTh[:, kt * P:(kt + 1) * P], qnh[:, qsl], True, False, [B_kT, B_qn], [B_ps[b_]])
                    mm(psb[b_][:, 0:nq], kpeT[:, kt * P:(kt + 1) * P], qrh[:, qsl], False, True,
                       [B_kpe, B_qr, B_kpz], [B_ps[b_]])

                smm(kts[0])
                if len(kts) > 1:
                    smm(kts[1])
                for j, kt in enumerate(kts):
                    if j + 2 < len(kts):
                        smm(kts[j + 2])
                    pj = npt % 3
                    npt += 1
                    b_ = sbanks[kt]
                    act(pT[pj][:, 0:nq], psb[b_][:, 0:nq], AF.Exp, [B_ps[b_]], [B_pT[pj]])
                    mm(psb[obk][:, 0:nq], Vh[:, kt, :], pT[pj][:, 0:nq], j == 0, j == len(kts) - 1,
                       [B_V, B_pT[pj]], [B_ps[obk]])
                    mm(psb[sbk][:, 0:nq], onesb[:], pT[pj][:, 0:nq], j == 0, j == len(kts) - 1,
                       [B_ones, B_pT[pj]], [B_ps[sbk]])
                act(rstd[:, 0:nq], psb[sbk][:, 0:nq], AF.Ln, [B_ps[sbk]], [B_rstd])
                act(rs1[:, 0:nq], rstd[:, 0:nq], AF.Exp, [B_rstd], [B_rs1], scale=-1.0)
                tt("dve", OT[:, h, qsl], psb[obk][:, 0:nq], rs1[:, 0:nq], ALU.mult, [B_ps[obk], B_rs1], [B_OT[tq_][h]])


    def even(l):
        i = l // 2
        wg = v3(wv(0, 4096, BF16), KC)
        QD = [wv(4096 + d * 1280, 1280, BF16) for d in range(2)]
        KI = [wv(6656 + d * 1280, 1280, BF16) for d in range(2)]
        KU = [v3(wv(9216 + d * 1280, 1280, BF16), 20) for d in range(2)]
        Vtok = v3(wv(11776, 2560, BF16), 20)
        SB = [v3(r2(d * 2560, 2560, BF16), NCH) for d in range(2)]
        qf = r2(5120, 512)
        kf = r2(5632, 512)
        qfs = [r2(5120 + j * 256, 256, BF16) for j in range(2)]
        kfs = [r2(5632 + j * 256, 256, BF16) for j in range(2)]
        t1 = r2(6144, 512)
        t2 = r2(6656, 512)
        t3 = r2(7168, 512)
        kuT = r2(7680, 256, BF16)
        t1B = r2(7936, 512)
        t2B = r2(8448, 512)
        t3B = r2(9600, 512)
        gk2kuT = r2(8960, 256, BF16)
        aTs = [r2(7936 + j * 128, 128, BF16) for j in range(2)]
        Sf = r2(8192, 512).rearrange("p (d q v) -> p d q v", d=2, q=2)
        SGt = v3(r2(8704, 512, BF16), 2)
        gk = r2(9216, 256, BF16)
        wgk = r2(9472, 128, BF16).rearrange("p (d f) -> p d f", d=2)

        def c64(a):
            return a.rearrange("p (c j) -> p c j", j=64)

        B_wg, B_V, B_qf, B_kf, B_t1, B_t2, B_t3, B_kuT, B_SGt, B_gk, B_wgk, B_D, B_rt, B_nb = (
            Buf(n) for n in ("wg", "Vtok", "qf", "kf", "t1", "t2", "t3", "kuT", "SGt", "gk", "wgk", "Dch", "rtab", "negb"))
        B_qfs, B_kfs, B_gks = [Buf("qfA"), Buf("qfB")], [Buf("kfA"), Buf("kfB")], [Buf("gkA"), Buf("gkB")]
        B_t1B, B_t2B, B_t3B = Buf("t1B"), Buf("t2B"), Buf("t3B")
        B_sq2b = Buf("sq2b")
        TS = [((t1, B_t1), (t2, B_t2), (t3, B_t3)), ((t1B, B_t1B), (t2B, B_t2B), (t3B, B_t3B))]
        kuTs = [kuT, gk2kuT]
        B_kuTs = [B_kuT, Buf("kuTB")]
        B_QD = [[Buf("QD") for t in range(NT)] for d in range(2)]
        B_KI = [[Buf("KI") for t in range(NT)] for d in range(2)]
        B_KU = [[Buf("KU") for t in range(NT)] for d in range(2)]
        B_Vt = [Buf("Vt") for t in range(NT)]
        B_SB = [[Buf("SB") for n in range(NCH)] for d in range(2)]
        B_Sf = [[Buf("Sf") for q in range(2)] for d in range(2)]
        B_aT = [Buf("aT0"), Buf("aT1")]

        ts("dve", negb[:], vT[:, R_BGK:R_BGK + 8], -1.0, None, ALU.mult, ALU.bypass, [B_vT], [B_nb])

        def load_wg(g):
            isret = g >= 2
            pg = g % 2
            wsrc = w_in_even[i]

            def wcols(c0, n):
                return wsrc[:, c0:c0 + n].rearrange("(k p) f -> p k f", p=P)

            if not isret:
                cq, ck, cv, cg = pg * 128, 256 + pg * 128, 512 + pg * 256, 1024 + pg * 256
            else:
                cq, ck, cv, cg = 1568 + pg * 128, 1824 + pg * 128, 2080 + pg * 256, 2592 + pg * 256
            S.dma("pool", wg[:, :, 0:128], wcols(cq, 128), [], [B_wg], B_wg)
            S.dma("pool", wg[:, :, 128:256], wcols(ck, 128), [], [B_wg], B_wg)
            S.dma("pool", wg[:, :, 256:512], wcols(cv, 256), [], [B_wg], B_wg)
            S.dma("pool", wg[:, :, 512:768], wcols(cg, 256), [], [B_wg], B_wg)
            if not isret:
                S.dma("pool", wg[:, :, 768:800], wcols(1536, 32), [], [B_wg], B_wg)
                S.op("dve", "memset", (wgk[0:64, :, :], 0.0), {}, [], [B_wgk])
                for d in range(2):
                    for rep in range(2):
                        S.dma("pool", wgk[rep * 32 + d * 16:rep * 32 + (d + 1) * 16, d, :],
                              w_gk2[i, d, :, pg * 128:(pg + 1) * 128], [], [B_wgk], B_wgk)
            else:
                for (dst0, csrc) in ((768, cq), (896, ck)):
                    dv_ = wg[:, :, dst0:dst0 + 128].rearrange("p k (h d) -> p k h d", d=64)
                    sv_ = wsrc[:, csrc:csrc + 128].rearrange("(k p) (h d) -> p k h d", p=P, d=64)
                    for hh in range(2):
                        S.dma("pool", dv_[:, :, hh, 0:32], sv_[:, :, hh, 32:64], [], [B_wg], B_wg)
                        S.dma("pool", dv_[:, :, hh, 32:64], sv_[:, :, hh, 0:32], [], [B_wg], B_wg)

        load_wg(0)
        for g in range(4):
            isret = g >= 2
            pg = g % 2
            if isret:
                for d in range(2):
                    lc = i * 4 + pg * 2 + d
                    xi = 1 if d == 0 else 3
                    yi = 2 if d == 0 else 0
                    act(rtab[:, d, 0, :], iot[:, xi, :], AF.Exp, [B_ec2], [B_rt], scale=lgm[:, lc:lc + 1])
                    act(rtab[:, d, 1, :], iot[:, xi, :], AF.Exp, [B_ec2], [B_rt], scale=lgn[:, lc:lc + 1])
                    act(rtab[:, d, 2, :], iot[:, yi, :], AF.Exp, [B_ec2], [B_rt], scale=lgm[:, lc:lc + 1])
                    act(Dret[:, d:d + 1], iot[:, 3, 0:1], AF.Exp, [B_ec2], [B_rt], scale=lgm[:, lc:lc + 1])
                    copy("dve", Dch[:, d, :], Dret[:, d:d + 1].to_broadcast([P, NCH]), [B_rt], [B_D])

            pbanks = {}

            def PE_PROJ(t):
                tsl = slice(t * TT, (t + 1) * TT)
                rope = isret and t < 4
                g0 = (t % 2) * 32

                def proj(c0, n=128, p0=0):
                    b_ = bankA()
                    for k in range(KC):
                        mm(psb[b_][p0:p0 + n, :], wg[:, k, c0:c0 + n], R0[:, k, tsl], k == 0, k == KC - 1,
                           [B_wg, B_R0[t][k]], [B_ps[b_]])
                    return b_

                bq = proj(0)
                bkk = proj(128)
                bqp = bkp = bgk = None
                if rope:
                    bqp = proj(768)
                    bkp = proj(896)
                if not isret:
                    bgk = proj(768, 32, g0)
                pbanks[t] = (bq, bkk, bqp, bkp, bgk)
                for half in range(2):
                    b_ = bankB()
                    for sub in range(2):
                        blk = t * 4 + half * 2 + sub
                        for k in range(KC):
                            mm(psb[b_][:, sub * 256:(sub + 1) * 256], R0[:, k, blk * P:(blk + 1) * P], wg[:, k, 256:512],
                               k == 0, k == KC - 1, [B_wg, B_R0[t][k]], [B_ps[b_]])
                    blk0 = t * 4 + half * 2
                    copy(evac_eng(), Vtok[:, blk0:blk0 + 2, :].rearrange("p a b -> p (a b)"), psb[b_][:],
                         [B_ps[b_]], [B_Vt[t]])

            def EVAC(t):
                tsl = slice(t * TT, (t + 1) * TT)
                rope = isret and t < 4
                qf, kf, B_qf, B_kf = qfs[t % 2], kfs[t % 2], B_qfs[t % 2], B_kfs[t % 2]
                g0 = (t % 2) * 32
                bq, bkk, bqp, bkp, bgk = pbanks[t]
                if rope:
                    tt("dve", t1, psb[bq][:], ropeC[:, tsl], ALU.mult, [B_ps[bq], B_rope], [B_t1])
                    tt("dve", t2, psb[bqp][:], ropeS[:, tsl], ALU.mult, [B_ps[bqp], B_rope], [B_t2])
                    tt("dve", qf, t1, t2, ALU.add, [B_t1, B_t2], [B_qf])
                    stt(t1, psb[bkk][:], 0.125, ropeC[:, tsl], ALU.mult, ALU.mult, [B_ps[bkk], B_rope], [B_t1])
                    stt(t2, psb[bkp][:], 0.125, ropeS[:, tsl], ALU.mult, ALU.mult, [B_ps[bkp], B_rope], [B_t2])
                    tt("dve", kf, t1, t2, ALU.add, [B_t1, B_t2], [B_kf])
                elif isret:
                    copy("dve", qf, psb[bq][:], [B_ps[bq]], [B_qf])
                    act(kf, psb[bkk][:], AF.Copy, [B_ps[bkk]], [B_kf], scale=0.125)
                else:
                    act(qf, psb[bq][:], AF.Copy, [B_ps[bq]], [B_qf], scale=0.125)
                    copy("dve", kf, psb[bkk][:], [B_ps[bkk]], [B_kf])
                    copy("act", gk[g0:g0 + 32, :], psb[bgk][g0:g0 + 32, :], [B_ps[bgk]], [B_gks[t % 2]])

            def DECAY(t):
                tsl = slice(t * TT, (t + 1) * TT)
                qf, kf, B_qf, B_kf = qfs[t % 2], kfs[t % 2], B_qfs[t % 2], B_kfs[t % 2]
                g0 = (t % 2) * 32
                ME = ("dve", "pool")
                if not isret:
                    bgp = [bankB(), bankB()]
                    Xs, Ys, Es = [None, None], [None, None], [None, None]
                    for d in range(2):
                        mm(psb[bgp[d]][:], wgk[g0:g0 + 32, d, :], gk[g0:g0 + 32, :], True, True,
                           [B_wgk, B_gks[t % 2]], [B_ps[bgp[d]]])
                    for d in range(2):
                        a1, a2, a3 = TS[d]
                        nbc = i * 4 + d * 2 + pg
                        act(a1[0], psb[bgp[d]][:], AF.Exp, [B_ps[bgp[d]], B_nb], [a1[1]],
                            bias=negb[:, nbc:nbc + 1], scale=-1.0)
                    for d in range(2):
                        a1, a2, a3 = TS[d]
                        act(a2[0], a1[0], AF.Ln, [a1[1], B_ec2], [a2[1]], bias=onesf[:, 0:1], scale=1.0)
                    for d in range(2):
                        a1, a2, a3 = TS[d]
                        S.op("dve", "tensor_tensor_scan", (a3[0], resetm[:], a2[0], 0.0, ALU.mult, ALU.add), {},
                             [B_ec, a2[1]], [a3[1]])
                    for d in range(2):
                        a1, a2, a3 = TS[d]
                        tpb = c64(a3[0])[:, :, 63:64].to_broadcast([P, 8, 64])
                        act(Dch[:, d, t * 8:(t + 1) * 8], c64(a3[0])[:, :, 63], AF.Exp, [a3[1]], [B_D], scale=-1.0 / 16)
                        if d == 0:
                            tt("dve", c64(a1[0]), tpb, c64(a3[0]), ALU.subtract, [a3[1]], [a1[1]])
                            Xs[d], Ys[d], Es[d] = a3, a1, a2
                        else:
                            tt("dve", a2[0], a3[0], a2[0], ALU.subtract, [a3[1], a2[1]], [a2[1]])
                            tt("dve", c64(a1[0]), tpb, c64(a2[0]), ALU.subtract, [a3[1], a2[1]], [a1[1]])
                            Xs[d], Ys[d], Es[d] = a1, a2, a3
                    for d in range(2):
                        act(Es[d][0], Xs[d][0], AF.Exp, [Xs[d][1]], [Es[d][1]], scale=-1.0 / 16)
                    for d in range(2):
                        tt(ME[d], QD[d][:, tsl], qf, Es[d][0], ALU.mult, [B_qf, Es[d][1]], [B_QD[d][t]])
                    for d in range(2):
                        act(Es[d][0], Xs[d][0], AF.Exp, [Xs[d][1]], [Es[d][1]], scale=1.0 / 16)
                    for d in range(2):
                        tt(ME[d], KI[d][:, tsl], kf, Es[d][0], ALU.mult, [B_kf, Es[d][1]], [B_KI[d][t]])
                    for d in range(2):
                        act(Es[d][0], Ys[d][0], AF.Exp, [Ys[d][1]], [Es[d][1]], scale=-1.0 / 16)
                    for d in range(2):
                        tt(ME[d], kuTs[d], kf, Es[d][0], ALU.mult, [B_kf, Es[d][1]], [B_kuTs[d]])
                else:
                    for d in range(2):
                        def bc(j):
                            return rtab[:, d, j, :].unsqueeze(1).to_broadcast([P, 8, 64])
                        tt(ME[d], c64(QD[d][:, tsl]), c64(qf), bc(0), ALU.mult, [B_qf, B_rt], [B_QD[d][t]])
                        tt(ME[d], c64(KI[d][:, tsl]), c64(kf), bc(1), ALU.mult, [B_kf, B_rt], [B_KI[d][t]])
                        tt(ME[d], c64(kuTs[d]), c64(kf), bc(2), ALU.mult, [B_kf, B_rt], [B_kuTs[d]])
                for d in range(2):
                    bt = bankB()
                    pst = psb[bt][:].bitcast(BF16)
                    for sub in range(4):
                        tr(pst[:, sub * P:(sub + 1) * P], kuTs[d][:, sub * P:(sub + 1) * P], identb[:],
                           [B_kuTs[d], B_identb], [B_ps[bt]])
                    copy(evac_eng(), KU[d][:, t * 4:(t + 1) * 4, :].rearrange("p a b -> p (a b)"), pst[:, 0:512],
                         [B_ps[bt]], [B_KU[d][t]])

            PE_PROJ(0)
            EVAC(0)
            PE_PROJ(1)
            for t in range(NT):
                if t + 1 < NT:
                    EVAC(t + 1)
                if t + 2 < NT:
                    PE_PROJ(t + 2)
                DECAY(t)
            for t in range(NT):
                tsl = slice(t * TT, (t + 1) * TT)
                for hh in range(2):
                    oi = (4 if isret else 0) + pg * 2 + hh
                    bg_ = bankB()
                    for k in range(KC):
                        mm(psb[bg_][:], wg[:, k, 512 + hh * P:512 + (hh + 1) * P], R0[:, k, tsl], k == 0, k == KC - 1,
                           [B_wg, B_R0[t][k]], [B_ps[bg_]])
                    act(OT[:, oi, tsl], psb[bg_][:], AF.Silu, [B_ps[bg_]], [B_OT[t][oi]])
            S.barrier()
            if g + 1 < 4:
                load_wg(g + 1)

            st_in = sret if isret else sgla
            st_out = oret if isret else ogla
            for si, (c0, c1) in enumerate(((0, 32), (32, 36), (36, 40))):
                pp = [0, 0]
                for d in range(2):
                    cur = Sf[:, d, 0, :]
                    if si == 0:
                        S.dma("sp", cur, st_in[i, d, 2 * pg:2 * pg + 2].rearrange("h k v -> (h k) v"),
                              [], [B_Sf[d][0]], B_Sf[d][0])
                    else:
                        S.op("dve", "memset", (cur, 0.0), {}, [], [B_Sf[d][0]])
                orders = [list(range(c0, c1)), list(range(c1 - 1, c0 - 1, -1))]
                for step in range(c1 - c0):

                    for d in range(2):
                        n = orders[d][step]
                        cur = Sf[:, d, pp[d], :]
                        Bc = B_Sf[d][pp[d]]
                        nxt = Sf[:, d, 1 - pp[d], :]
                        Bn = B_Sf[d][1 - pp[d]]
                        copy("act", SB[d][:, n, :], cur, [Bc], [B_SB[d][n]])
                        blk, r0 = n // 2, (n % 2) * 64
                        tl = n // 8
                        bu = bankA()
                        for hh in range(2):
                            mm(psb[bu][hh * 64:(hh + 1) * 64, 0:P], KU[d][r0:r0 + 64, blk, hh * 64:(hh + 1) * 64],
                               Vtok[r0:r0 + 64, blk, hh * P:(hh + 1) * P], True, True,
                               [B_KU[d][tl], B_Vt[tl]], [B_ps[bu]])
                        stt(nxt, cur, Dch[:, d, n:n + 1], psb[bu][:, 0:P], ALU.mult, ALU.add, [Bc, B_D, B_ps[bu]], [Bn])
                        pp[d] = 1 - pp[d]
                if si > 0:
                    for d in range(2):
                        S.dma("sp", st_out[si - 1, i, d, 2 * pg:2 * pg + 2].rearrange("h k v -> (h k) v"),
                              Sf[:, d, pp[d], :], [B_Sf[d][pp[d]]], [], B_Sf[d][pp[d]])

            for t in range(NT):
                tsl = slice(t * TT, (t + 1) * TT)
                bo = [bankB(), bankB()]
                bj = [bankB(), bankB()]
                chs = [cn for par in range(2) for cn in range(par, 8, 2)]

                def emit_aT(idx):
                    cn = chs[idx]
                    n = t * 8 + cn
                    r0 = (n % 2) * 64
                    csl = slice(n * 64, (n + 1) * 64)
                    rc = slice(r0, r0 + 64)
                    j = idx % 2
                    for hh in range(2):
                        rh = slice(hh * 64, hh * 64 + 64)
                        ba = bankA()
                        for d in range(2):
                            mm(psb[ba][rc, d * 64:(d + 1) * 64], KI[d][rh, csl], QD[d][rh, csl], True, True,
                               [B_KI[d][t], B_QD[d][t]], [B_ps[ba]])
                        tt("dve", aTs[j][rc, hh * P:(hh + 1) * P], psb[ba][rc, 0:P], m4[rc, 0:P], ALU.mult,
                           [B_ps[ba], B_ec], [B_aT[j]])

                emit_aT(0)
                for idx in range(8):
                    if l + 1 < nl and (t * 8 + idx) % 3 == 0 and (t * 8 + idx) < 36:
                        ada_mm(l + 1)
                    if idx + 1 < 8:
                        emit_aT(idx + 1)
                    cn = chs[idx]
                    n = t * 8 + cn
                    blk, r0 = n // 2, (n % 2) * 64
                    csl = slice(n * 64, (n + 1) * 64)
                    rc = slice(r0, r0 + 64)
                    j = idx % 2
                    for hh in range(2):
                        oo = psb[bo[hh]][:, cn * 64:(cn + 1) * 64]
                        vv = Vtok[rc, blk, hh * P:(hh + 1) * P]
                        mm(oo, vv, aTs[j][rc, (hh * 2) * 64:(hh * 2 + 1) * 64], True, False,
                           [B_Vt[t], B_aT[j]], [B_ps[bo[hh]]])
                        mm(oo, vv, aTs[j][rc, (hh * 2 + 1) * 64:(hh * 2 + 2) * 64], False, True,
                           [B_Vt[t], B_aT[j]], [B_ps[bo[hh]]])
                    for hh in range(2):
                        rh = slice(hh * 64, hh * 64 + 64)
                        oj = psb[bj[hh]][:, cn * 64:(cn + 1) * 64]
                        mm(oj, SB[0][rh, n, :], QD[0][rh, csl], True, False, [B_SB[0][n], B_QD[0][t]], [B_ps[bj[hh]]])
                        mm(oj, SB[1][rh, n, :], QD[1][rh, csl], False, True, [B_SB[1][n], B_QD[1][t]], [B_ps[bj[hh]]])
                HB = [((t3, B_t3), (t1, B_t1), (t2, B_t2), (rstd, B_rstd)),
                      ((qf, B_qf), (kf, B_kf), (t3B, B_t3B), (rstd2, B_rstd2))]
                B_sqh = [B_sq2, B_sq2b]
                for hh in range(2):
                    (c_, Bc_), (o_, Bo_), _, _ = HB[hh]
                    copy("act", c_, psb[bo[hh]][:], [B_ps[bo[hh]]], [Bc_])
                for hh in range(2):
                    (c_, Bc_), (o_, Bo_), _, _ = HB[hh]
                    tt("dve", o_, c_, psb[bj[hh]][:], ALU.add, [Bc_, B_ps[bj[hh]]], [Bo_])
                bss = [bankA(), bankA()]
                for hh in range(2):
                    (c_, Bc_), (o_, Bo_), _, _ = HB[hh]
                    act(sq2[:, hh, :], o_, AF.Square, [Bo_], [B_sqh[hh]])
                    mm(psb[bss[hh]][:], onesb[:], sq2[:, hh, :], True, True, [B_ones, B_sqh[hh]], [B_ps[bss[hh]]])
                for hh in range(2):
                    (c_, Bc_), (o_, Bo_), _, (r_, Br_) = HB[hh]
                    act(c_, psb[bss[hh]][:], AF.Ln, [B_ps[bss[hh]], B_eps], [Bc_], bias=epsb[:, 0:1], scale=1.0 / 128)
                for hh in range(2):
                    (c_, Bc_), (o_, Bo_), _, (r_, Br_) = HB[hh]
                    act(r_[:], c_, AF.Exp, [Bc_], [Br_], scale=-0.5)
                for hh in range(2):
                    (c_, Bc_), (o_, Bo_), (m_, Bm_), (r_, Br_) = HB[hh]
                    tt("pool" if hh else "dve", m_, o_, r_[:], ALU.mult, [Bo_, Br_], [Bm_])
                for hh in range(2):
                    (c_, Bc_), (o_, Bo_), (m_, Bm_), (r_, Br_) = HB[hh]
                    if not isret:
                        oi = pg * 2 + hh
                        stt(OT[:, oi, tsl], m_, vT[:, R_GLAN + i:R_GLAN + i + 1], OT[:, oi, tsl],
                            ALU.mult, ALU.mult, [Bm_, B_vT, B_OT[t][oi]], [B_OT[t][oi]])
                    else:
                        oi = 4 + pg * 2 + hh
                        tt("dve", OT[:, oi, tsl], m_, OT[:, oi, tsl], ALU.mult, [Bm_, B_OT[t][oi]], [B_OT[t][oi]])
            S.barrier()

    def c1(l, w_out_l):
        wout = v3(r2(0, 4096, BF16), KC)
        y3 = v3(r2(4096, 4096), KC)
        B_wout = Buf("wout")
        B_y3 = [Buf("y3_%d" % k) for k in range(KC)]
        S.dma("pool", wout, w_out_l.rearrange("(k p) f -> p k f", p=P), [], [B_wout], B_wout)

        def get_y(t):
            tsl = slice(t * TT, (t + 1) * TT)
            for dc in range(KC):
                bk = bankA()
                for k in range(KC):
                    mm(psb[bk][:], wout[:, k, dc * P:(dc + 1) * P], OT[:, k, tsl], k == 0, k == KC - 1,
                       [B_wout, B_OT[t][k]], [B_ps[bk]])
                copy(evac_eng(), y3[:, dc, :], psb[bk][:], [B_ps[bk]], [B_y3[dc]])
            return y3, B_y3

        norm_pipeline(get_y, GG1(l), (GS2(l), 24, l), False, ahead=True)

    def cidx_of(t):
        return 0 if t < 4 else 1

    for t in range(NT):
        stage_x0(t)
    wad0 = [wv(8192 + i * 2048, 2048, BF16).rearrange("p (k f) -> p k f", k=KC) for i in range(2)]
    for pc in range(12):
        i = pc % 2
        Bw = B_tmpf[4 * i:4 * i + 4]
        S.dma("pool", wad0[i], w_ada[0, :, pc * 512:(pc + 1) * 512].rearrange("(k p) f -> p k f", p=P), [], Bw, B_wad[i])
        for jj in range(4):
            j = pc * 4 + jj
            bk = bankA()
            for k in range(KC):
                mm(psb[bk][:, 0:2], wad0[i][:, k, jj * P:(jj + 1) * P], scb[:, k, :], k == 0, k == KC - 1,
                   Bw + [B_sc], [B_ps[bk]])
            tt("dve", mod[:, 0, j, :], psb[bk][:, 0:2],
               vT[:, R_BADA + j:R_BADA + j + 1].to_broadcast([P, 2]), ALU.add, [B_ps[bk], B_vT], [B_mod])
    layer_vecs(0)
    for t in range(NT):
        i = t % 2
        load_x0(t, i)
        store_x(t, i)
        prenorm(xa[i], B_xa[i], GS1(0), 0, 0, cidx_of(t), R0[:, :, t * TT:(t + 1) * TT], B_R0[t])

    for l in range(nl):
        domix = (mix == "all") or (mix == "odd" and l % 2 == 1) or (mix == "even" and l % 2 == 0)
        if domix:
            S.barrier()
            if l % 2 == 1:
                mla(l)
                S.barrier()
                c1(l, w_out_odd[l // 2])
            else:
                even(l)
                S.barrier()
                c1(l, w_out_even[l // 2])
        else:
            for t in range(NT):
                i = t % 2
                load_x(t, i)
                prenorm(xa[i], B_xa[i], GS2(l), 24, l, cidx_of(t), R0[:, :, t * TT:(t + 1) * TT], B_R0[t])
        S.barrier()
        mlp(l)
        if l + 1 < nl:
            while ada_st.setdefault(l + 1, {"dma": 0, "mm": 0})["mm"] < 48:
                ada_mm(l + 1)
            layer_vecs(l + 1)
        S.barrier()
        def get_y2(t):
            return yacc[:, :, t * TT:(t + 1) * TT], B_Y[t]

        if l + 1 < nl:
            norm_pipeline(get_y2, GG2(l), (GS1(l + 1), 0, l + 1), False)
        else:
            norm_pipeline(get_y2, GG2(l), None, True)

    S.barrier()
    S.final_wait("sp", list(S.dma_bufs))

    with nc.Block() as block:
        @block.tensor
        def _(e):
            S.emit("pe", e)

        @block.scalar
        def _(e):
            S.emit("act", e)

        @block.vector
        def _(e):
            S.emit("dve", e)

        @block.gpsimd
        def _(e):
            S.emit("pool", e)

        @block.sync
        def _(e):
            S.emit("sp", e)
    stack.close()
    return nc


def _pack_vecs(inp, b):
    v = np.zeros((R_TOT, P), np.float32)
    v[R_BADA:R_BADA + 192] = inp["b_ada"].reshape(192, P)
    v[R_NMIXPRE:R_NMIXPRE + 32] = inp["norm_mix_pre"].reshape(32, P)
    v[R_NMIXPOST:R_NMIXPOST + 32] = inp["norm_mix_post"].reshape(32, P)
    v[R_NMLPPRE:R_NMLPPRE + 32] = inp["norm_mlp_pre"].reshape(32, P)
    v[R_NMLPPOST:R_NMLPPOST + 32] = inp["norm_mlp_post"].reshape(32, P)
    v[R_COND:R_COND + 8] = inp["c"][b].reshape(8, P)
    v[R_COND + 8:R_COND + 16] = inp["c_ctx"].reshape(8, P)
    v[R_QAN:R_QAN + 4] = inp["q_a_norm"].reshape(4, P)
    v[R_KVAN:R_KVAN + 4] = inp["kv_a_norm"].reshape(4, P)
    v[R_GLAN:R_GLAN + 2] = inp["gla_norm"].reshape(2, P)
    v[R_BGK:R_BGK + 8] = inp["b_gk2"].reshape(8, P)
    return v


def kernel(**inputs):
    inp = {k: np.asarray(v) for k, v in inputs.items()}
    ncores = int(os.environ.get("K_NCORES", "8"))
    nl = int(os.environ.get("K_NL", str(DEPTH)))
    mix = os.environ.get("K_MIX", "all")
    nc = build(nl=nl, mix=mix)
    identf = np.eye(P, dtype=np.float32)
    tpos = np.arange(LS)
    inv = 10000.0 ** (-np.arange(16, dtype=np.float64) / 16.0)
    ang = np.concatenate([(tpos // 64)[:, None] * inv[None, :], (tpos % 64)[:, None] * inv[None, :]], axis=1)
    dd = np.arange(P) % 64
    jj = np.arange(64, dtype=np.float32)
    iot = np.broadcast_to(np.stack([jj, jj + 1, 63 - jj, 64 - jj])[None], (P, 4, 64)).astype(np.float32)
    pj = (np.arange(P) % 64)[:, None]
    cc = np.arange(256)[None, :]
    ii = cc % 64
    dirb = (cc // 64) % 2
    m4 = np.where(dirb == 0, ii >= pj, ii <= pj).astype(np.float32)
    resetm = np.broadcast_to((np.arange(TT) % 64 != 0).astype(np.float32)[None], (P, TT))
    rd = inp["ret_decay"]
    rdec = np.zeros((P, 8), np.float32)
    for i_ in range(2):
        for pg_ in range(2):
            for d_ in range(2):
                for p_ in range(P):
                    rdec[p_, i_ * 4 + pg_ * 2 + d_] = rd[i_, d_, 2 * pg_ + p_ // 64]
    ropeC = np.cos(ang[:, dd % 32]).T.astype(np.float32)
    ropeS = (np.sin(ang[:, dd % 32]).T * np.where(dd < 32, -1.0, 1.0)[:, None]).astype(np.float32)
    in_maps = []
    for b in range(ncores):
        xin = np.concatenate([inp["x_sample"][b], inp["x_prompt"][2 * b], inp["x_prompt"][2 * b + 1]], axis=0)
        in_maps.append({
            "xin": np.ascontiguousarray(xin, dtype=np.float32),
            "vecs": _pack_vecs(inp, b),
            "identf": identf,
            "w_ada": inp["w_ada"], "w_mlp1": inp["w_mlp1"], "w_mlp2": inp["w_mlp2"],
            "w_in_odd": inp["w_in_odd"], "w_q_b": inp["w_q_b"], "w_kv_b": inp["w_kv_b"],
            "w_out_odd": inp["w_out_odd"], "w_out_even": inp["w_out_even"],
            "cckv": np.ascontiguousarray(inp["cache_ckv"][b]), "ckpe": np.ascontiguousarray(inp["cache_kpe"][b]),
            "ropeC": np.ascontiguousarray(ropeC), "ropeS": np.ascontiguousarray(ropeS),
            "w_in_even": inp["w_in_even"], "w_gk2": inp["w_gk2"],
            "sgla": np.ascontiguousarray(inp["state_gla"][b]), "sret": np.ascontiguousarray(inp["state_ret"][b]),
            "rdec": rdec, "iot": np.ascontiguousarray(iot), "m4": np.ascontiguousarray(m4),
            "resetm": np.ascontiguousarray(resetm),
        })
    res = run_bass_kernel_spmd(nc, in_maps, core_ids=list(range(ncores)))
    y_prompt = np.zeros((16, LP, D), np.float32)
    y_sample = np.zeros((8, LS, D), np.float32)
    new_ckv = np.zeros((16, 2, LP, 256), np.float32)
    new_kpe = np.zeros((16, 2, LP, 64), np.float32)
    new_gla = np.zeros((16, 2, 2, 4, 64, 128), np.float32)
    new_ret = np.zeros((16, 2, 2, 4, 64, 128), np.float32)
    for b in range(ncores):
        r = res.results[b]
        ysb = r["ys"]
        y_sample[b] = ysb[0:LS]
        y_prompt[2 * b] = ysb[LS:LS + LP]
        y_prompt[2 * b + 1] = ysb[LS + LP:LS + 2 * LP]
        new_ckv[2 * b:2 * b + 2] = r["ockv"]
        new_kpe[2 * b:2 * b + 2] = r["okpe"]
        if "ogla" in r:
            new_gla[2 * b:2 * b + 2] = r["ogla"]
            new_ret[2 * b:2 * b + 2] = r["oret"]
    return y_prompt, y_sample, new_ckv, new_kpe, new_gla, new_ret
```

```python
import os
from contextlib import ExitStack
import numpy as np
import concourse.bass as bass
import concourse.mybir as mybir
from concourse.bass_utils import run_bass_kernel_spmd

F32 = mybir.dt.float32
BF16 = mybir.dt.bfloat16
AF = mybir.ActivationFunctionType
ALU = mybir.AluOpType

P = 128
D = 1024
KC = 8
TT = 512
NT = 5
NTOK = NT * TT
DEPTH = 4
DFF = 4096
EPS = 1e-6
LS = 2048
LP = 256
NCH = NTOK // 64

R_BADA = 0
R_NMIXPRE = 192
R_NMIXPOST = 224
R_NMLPPRE = 256
R_NMLPPOST = 288
R_COND = 320
R_QAN = 336
R_KVAN = 340
R_GLAN = 344
R_BGK = 346
R_TOT = 384


class Buf:
    __slots__ = ("name", "w", "r", "sem", "cnt")

    def __init__(self, name):
        self.name = name
        self.w = None
        self.r = {}
        self.sem = None
        self.cnt = 0


class Sched:
    ENG = ("pe", "act", "dve", "pool", "sp")

    def __init__(self, nc, stack):
        self.nc = nc
        self.stack = stack
        self.ops = {e: [] for e in self.ENG}
        self.cnt = {e: 0 for e in self.ENG}
        self.waited = {e: {} for e in self.ENG}
        self.sems = {}
        for e in self.ENG:
            self.sems[e] = stack.enter_context(nc.semaphore("s_" + e))
        self.ndsem = 0
        self.dma_bufs = []

    def _deps(self, eng, reads, writes):
        d = {}
        dr = {}
        for b in reads:
            if b.w is not None:
                k, v = b.w
                if d.get(k, 0) < v:
                    d[k] = v
                if dr.get(k, 0) < v:
                    dr[k] = v
        for b in writes:
            if b.w is not None:
                k, v = b.w
                if d.get(k, 0) < v:
                    d[k] = v
            for k, v in b.r.items():
                if d.get(k, 0) < v:
                    d[k] = v
        out = []
        wd = self.waited[eng]
        for k, v in d.items():
            if k == eng and eng == "pe":
                continue
            if wd.get(k, 0) >= v:
                continue
            wonly = dr.get(k, 0) <= wd.get(k, 0)
            wd[k] = v
            out.append((k, v, wonly))
        return out

    def op(self, eng, meth, args, kw, reads=(), writes=()):
        deps = self._deps(eng, reads, writes)
        self.cnt[eng] += 1
        idx = self.cnt[eng]
        self.ops[eng].append((deps, (meth, args, kw), (eng, 1)))
        for b in reads:
            b.r[eng] = idx
        for b in writes:
            b.w = (eng, idx)
            b.r = {}

    def dma(self, q, out, in_, reads, writes, sb):
        fn = ("dma_start", (), dict(out=out, in_=in_))
        deps = self._deps(q, reads, writes)
        if sb.sem is None:
            sb.sem = "d%d" % self.ndsem
            self.ndsem += 1
            self.sems[sb.sem] = self.stack.enter_context(self.nc.semaphore(sb.sem))
            self.dma_bufs.append(sb)
        sb.cnt += 16
        self.ops[q].append((deps, fn, (sb.sem, 16)))
        for b in reads:
            b.r[sb.sem] = sb.cnt
        for b in writes:
            b.w = (sb.sem, sb.cnt)
            b.r = {}

    def final_wait(self, eng, bufs):
        deps = self._deps(eng, bufs, bufs)
        self.ops[eng].append((deps, None, None))

    def barrier(self):
        toks = [(k, self.cnt[k]) for k in self.ENG if self.cnt[k] > 0]
        toks += [(b.sem, b.cnt) for b in self.dma_bufs]
        for e in self.ENG:
            wd = self.waited[e]
            deps = []
            for k, v in toks:
                if k == e:
                    continue
                if wd.get(k, 0) >= v:
                    continue
                wd[k] = v
                deps.append((k, v))
            if deps:
                self.ops[e].append((deps, None, None))

    def emit(self, eng, e):
        for deps, fn, inc in self.ops[eng]:
            if fn is None or fn[0] == "dma_start" or eng in ("sp", "pool"):
                for d_ in deps:
                    e.wait_ge(self.sems[d_[0]], d_[1])
                if fn is None:
                    continue
                meth, args, kw = fn
                ins = getattr(e, meth)(*args, **kw)
            else:
                fus = None
                for j in range(len(deps) - 1, -1, -1):
                    if eng != "pe" or (len(deps[j]) > 2 and deps[j][2]):
                        fus = j
                        break
                for j, d_ in enumerate(deps):
                    if j != fus:
                        e.wait_ge(self.sems[d_[0]], d_[1])
                meth, args, kw = fn
                ins = getattr(e, meth)(*args, **kw)
                if fus is not None:
                    ins._wait_ge(self.sems[deps[fus][0]], deps[fus][1])
            ins.then_inc(self.sems[inc[0]], inc[1])
def build(nl=DEPTH, mix="all"):
    nc = bass.Bass("TRN2", target_bir_lowering=False)
    stack = ExitStack()
    S = Sched(nc, stack)

    def din(name, shape, dt=F32):
        return nc.dram_tensor(name, list(shape), dt, kind="ExternalInput").ap()

    def dout(name, shape, dt=F32):
        return nc.dram_tensor(name, list(shape), dt, kind="ExternalOutput").ap()

    xin = din("xin", [NTOK, D])
    vecs = din("vecs", [R_TOT, P])
    identf = din("identf", [P, P])
    w_ada = din("w_ada", [DEPTH, D, 6 * D])
    w_mlp1 = din("w_mlp1", [DEPTH, D, DFF])
    w_mlp2 = din("w_mlp2", [DEPTH, DFF, D])
    w_in_odd = din("w_in_odd", [2, D, 576])
    w_q_b = din("w_q_b", [2, 256, 1536])
    w_kv_b = din("w_kv_b", [2, 256, 2048])
    w_out_odd = din("w_out_odd", [2, D, D])
    w_out_even = din("w_out_even", [2, D, D])
    cckv = din("cckv", [2, 256, 256])
    ckpe = din("ckpe", [2, 256, 64])
    ropeC_d = din("ropeC", [P, LS])
    ropeS_d = din("ropeS", [P, LS])
    w_in_even = din("w_in_even", [2, D, 3104])
    w_gk2 = din("w_gk2", [2, 2, 16, 256])
    sgla = din("sgla", [2, 2, 4, 64, 128])
    sret = din("sret", [2, 2, 4, 64, 128])
    rdec_d = din("rdec", [P, 8])
    iot_d = din("iot", [P, 4, 64])
    m4_d = din("m4", [P, 256])
    resetm_d = din("resetm", [P, TT])
    ogla = dout("ogla", [2, 2, 2, 4, 64, 128])
    oret = dout("oret", [2, 2, 2, 4, 64, 128])
    ys = dout("ys", [NTOK, D])
    ockv = dout("ockv", [2, 2, LP, 256])
    okpe = dout("okpe", [2, 2, LP, 64])
    xs = nc.dram_tensor("xs", [NT, P, KC, TT], F32).ap()
    xsb = [Buf("xs%d" % t) for t in range(NT)]

    def sb(name, shape, dt):
        return stack.enter_context(nc.sbuf_tensor(name, list(shape), dt))

    R0 = sb("R0", [P, KC, NTOK], BF16)
    R12 = sb("R12", [P, KC, NTOK], F32)
    WA = sb("WA", [P, 14336], F32)
    ident = sb("ident", [P, P], F32)
    identb = sb("identb", [P, P], BF16)
    onesb = sb("onesb", [P, P], BF16)
    onesf32 = sb("onesf32", [P, P], F32)
    B_onesf32 = Buf("onesf32")
    vT = sb("vT", [P, R_TOT], F32)
    vrow = sb("vrow", [P, 3, P], F32)
    sc = sb("sc", [P, KC, 2], F32)
    scb = sb("scb", [P, KC, 2], BF16)
    wad = sb("wad", [P, 2, KC, P], BF16)
    B_wad = [Buf("wad0"), Buf("wad1")]
    mod = sb("mod", [P, DEPTH, 48, 2], F32)
    gvec = sb("gvec", [P, DEPTH, 4, KC, 2], F32)
    rs1 = sb("rs1", [P, TT], F32)
    rstd = sb("rstd", [P, TT], F32)
    rstd2 = sb("rstd2", [P, TT], F32)
    epsb = sb("epsb", [P, 1], F32)
    ropeC = sb("ropeC_s", [P, LS], BF16)
    ropeS = sb("ropeS_s", [P, LS], BF16)
    rdec = sb("rdec_s", [P, 8], F32)
    lgn = sb("lgn", [P, 8], F32)
    lgm = sb("lgm", [P, 8], F32)
    iot = sb("iot_s", [P, 4, 64], F32)
    m4 = sb("m4_s", [P, 256], BF16)
    resetm = sb("resetm_s", [P, TT], BF16)
    onesf = sb("onesf", [P, 1], F32)
    negb = sb("negb", [P, 8], F32)
    Dch = sb("Dch", [P, 2, NCH], F32)
    rtab = sb("rtab", [P, 2, 3, 64], F32)
    Dret = sb("Dret", [P, 2], F32)
    B_ec = Buf("evenconst")
    sq2 = sb("sq2", [P, 2, TT], BF16)
    B_rope, B_sq2 = Buf("rope"), Buf("sq2")

    def wv(off, n, dt=F32):
        a = WA[:, off:off + n]
        return a.bitcast(BF16) if dt == BF16 else a

    def v3(a, k):
        return a.rearrange("p (k t) -> p k t", k=k)

    xa = [v3(wv(i * 4096, 4096), KC) for i in range(2)]
    tmpf = v3(wv(8192, 4096), KC)
    stv = wv(8192, 4096).rearrange("p (s d) -> p s d", s=4)
    sqb = v3(wv(12288, 2048, BF16), KC)
    wstf = [v3(wv(i * 2048, 2048), KC) for i in range(2)]
    w1v = [v3(wv(i * 2048, 2048, BF16), KC) for i in range(2)]
    w2st = [v3(wv(4096 + i * 2048, 2048, BF16), 4) for i in range(2)]
    ub = [v3(wv(8192 + i * 1024, 1024, BF16), 4) for i in range(2)]
    rb = [wv(10240 + i * 256, 256, BF16) for i in range(2)]
    yacc = R12

    B_R0 = [[Buf("R0_%d_%d" % (t, k)) for k in range(KC)] for t in range(NT)]
    B_Y = [[Buf("Y_%d_%d" % (t, k)) for k in range(KC)] for t in range(NT)]
    B_ident, B_identb, B_ones, B_vT, B_vrow, B_sc, B_mod, B_g, B_eps = (
        Buf(n) for n in ("ident", "identb", "ones", "vT", "vrow", "sc", "mod", "gv", "eps"))
    B_xa = [[Buf("xa%d_%d" % (i, k)) for k in range(KC)] for i in range(2)]
    B_tmpf = [Buf("tmpf%d" % k) for k in range(KC)]
    B_sqb = [Buf("sqb%d" % k) for k in range(KC)]
    B_rs1, B_rstd = Buf("rs1"), Buf("rstd")
    B_rstd2 = Buf("rstd2")
    rsel = {"n": 0}
    B_xadma = [Buf("xadma0"), Buf("xadma1")]
    B_stg = Buf("stgdma")
    B_wst = [Buf("wst0"), Buf("wst1")]
    B_wsta = [Buf("wsta0"), Buf("wsta1")]
    B_w2st = [Buf("w2st0"), Buf("w2st1")]
    B_ub = [Buf("ub0"), Buf("ub1")]
    B_rb = [Buf("rb0"), Buf("rb1")]

    psb = [stack.enter_context(nc.psum_tensor("ps%d" % i, [P, 512], F32)) for i in range(8)]
    B_ps = [Buf("ps%d" % i) for i in range(8)]
    rr = {"A": 0, "B": 0, "ev": 0}

    def bankA():
        i = rr["A"]
        rr["A"] = (i + 1) % 4
        return i

    def bankB():
        i = 4 + rr["B"]
        rr["B"] = (rr["B"] + 1) % 4
        return i

    def evac_eng():
        rr["ev"] ^= 1
        return "act" if rr["ev"] else "dve"

    def mm(out, lhsT, rhs, start, stop, reads, writes):
        S.op("pe", "matmul", (out,), dict(lhsT=lhsT, rhs=rhs, start=start, stop=stop), reads, writes)

    def tr(out, in_, idn, reads, writes):
        S.op("pe", "transpose", (out, in_, idn), {}, reads, writes)

    def act(out, in_, func, reads, writes, **kw):
        S.op("act", "activation", (out, in_, func), kw, reads, writes)

    def tt(eng, out, in0, in1, op, reads, writes):
        S.op(eng, "tensor_tensor", (out, in0, in1, op), {}, reads, writes)

    def stt(out, in0, scalar, in1, op0, op1, reads, writes):
        S.op("dve", "scalar_tensor_tensor", (out, in0, scalar, in1, op0, op1), {}, reads, writes)

    def ts(eng, out, in0, s1, s2, op0, op1, reads, writes):
        S.op(eng, "tensor_scalar", (out, in0, s1, s2, op0, op1), {}, reads, writes)

    def copy(eng, out, in_, reads, writes):
        if eng == "act":
            act(out, in_, AF.Copy, reads, writes)
        else:
            S.op(eng, "tensor_copy", (out, in_), {}, reads, writes)

    S.dma("sp", ident[:], identf[:, :], [], [B_ident], B_ident)
    copy("dve", identb[:], ident[:], [B_ident], [B_identb])
    S.op("pool", "memset", (onesb[:], 1.0), {}, [], [B_ones])
    S.op("pool", "memset", (onesf32[:], 1.0), {}, [], [B_onesf32])
    S.op("pool", "memset", (epsb[:], EPS), {}, [], [B_eps])

    S.dma("pool", ropeC[:], ropeC_d[:, :], [], [B_rope], B_rope)
    S.dma("pool", ropeS[:], ropeS_d[:, :], [], [B_rope], B_rope)
    S.dma("pool", m4[:], m4_d[:, :], [], [B_ec], B_ec)
    S.dma("pool", resetm[:], resetm_d[:, :], [], [B_ec], B_ec)
    B_ec2 = Buf("evenconst2")
    S.dma("sp", rdec[:], rdec_d[:, :], [], [B_ec2], B_ec2)
    S.dma("sp", iot[:], iot_d[:, :, :], [], [B_ec2], B_ec2)
    S.op("pool", "memset", (onesf[:], 1.0), {}, [], [B_ec2])
    act(lgn[:], rdec[:], AF.Exp, [B_ec2], [B_ec2])
    ts("dve", lgm[:], lgn[:], -1.0, None, ALU.mult, ALU.bypass, [B_ec2], [B_ec2])

    S.dma("sp", vrow[:], vecs.rearrange("(a p) f -> p a f", p=P), [], [B_vrow], B_vrow)
    for a in range(3):
        bk = bankA()
        tr(psb[bk][:, 0:P], vrow[:, a, :], ident[:], [B_vrow, B_ident], [B_ps[bk]])
        copy("dve", vT[:, a * P:(a + 1) * P], psb[bk][:, 0:P], [B_ps[bk]], [B_vT])
    for c in range(2):
        act(sc[:, :, c], vT[:, R_COND + c * 8:R_COND + c * 8 + 8], AF.Silu, [B_vT], [B_sc])

    copy("dve", scb[:], sc[:], [B_sc], [B_sc])
    adac = {"n": 0}

    ada_st = {}

    def ada_dma(l):
        st = ada_st.setdefault(l, {"dma": 0, "mm": 0})
        pc = st["dma"]
        if pc >= 48:
            return
        i = pc % 2
        st["dma"] += 1
        S.dma("pool", wad[:, i], w_ada[l, :, pc * P:(pc + 1) * P].rearrange("(k p) f -> p k f", p=P),
              [], [B_wad[i]], B_wad[i])

    def ada_mm(l):
        st = ada_st.setdefault(l, {"dma": 0, "mm": 0})
        pc = st["mm"]
        if pc >= 48:
            return
        if st["dma"] <= pc:
            ada_dma(l)
        i = pc % 2
        st["mm"] += 1
        bk = bankA()
        for k in range(KC):
            mm(psb[bk][:, 0:2], wad[:, i, k, :], scb[:, k, :], k == 0, k == KC - 1, [B_wad[i], B_sc], [B_ps[bk]])
        tt("dve", mod[:, l, pc, :], psb[bk][:, 0:2],
           vT[:, R_BADA + l * 48 + pc:R_BADA + l * 48 + pc + 1].to_broadcast([P, 2]), ALU.add,
           [B_ps[bk], B_vT], [B_mod])
        if st["dma"] - st["mm"] < 1:
            ada_dma(l)

    def ada_piece(l, pc):
        ada_mm(l)

    def rms_stats(src, B_src, nk, dfeat, sq=None, B_sq=None):
        if sq is None:
            sq, B_sq = sqb, B_sqb
        bk = bankA()
        for k in range(nk):
            bs_ = B_src[k] if isinstance(B_src, list) else B_src
            bq_ = B_sq[k] if isinstance(B_sq, list) else B_sq
            act(sq[:, k, :], src[:, k, :], AF.Square, [bs_], [bq_])
            mm(psb[bk][:], onesb[:], sq[:, k, :], k == 0, k == nk - 1, [B_ones, bq_], [B_ps[bk]])
        rsel["n"] ^= 1
        rr_, Br_ = (rstd, B_rstd) if rsel["n"] else (rstd2, B_rstd2)
        act(rs1[:], psb[bk][:], AF.Ln, [B_ps[bk], B_eps], [B_rs1], bias=epsb[:, 0:1], scale=1.0 / dfeat)
        act(rr_[:], rs1[:], AF.Exp, [B_rs1], [Br_], scale=-0.5)
        return rr_, Br_

    def pre_apply(xt, B_xt, rr_, Br_, gs, sh_j0, l, cidx, out3, B_out):
        all_act = (sh_j0 == 0)
        for k in range(KC):
            sc_ap = gs[:, k, cidx:cidx + 1]
            bi_ap = mod[:, l, sh_j0 + k, cidx:cidx + 1]
            tt("dve", tmpf[:, k, :], xt[:, k, :], rr_[:], ALU.mult, [B_xt[k], Br_], [B_tmpf[k]])
            if k % 4 == 3 and not all_act:
                ts("dve", out3[:, k, :], tmpf[:, k, :], sc_ap, bi_ap, ALU.mult, ALU.add, [B_tmpf[k], B_g, B_mod], [B_out[k]])
            else:
                act(out3[:, k, :], tmpf[:, k, :], AF.Identity, [B_tmpf[k], B_g, B_mod], [B_out[k]], bias=bi_ap, scale=sc_ap)

    def prenorm(xt, B_xt, gs, sh_j0, l, cidx, out3, B_out):
        rr_, Br_ = rms_stats(xt, B_xt, KC, D)
        pre_apply(xt, B_xt, rr_, Br_, gs, sh_j0, l, cidx, out3, B_out)

    def post_apply(y3, B_y, rr_, Br_, gg, cidx, xt, B_xt):
        for k in range(KC):
            tt("dve", tmpf[:, k, :], y3[:, k, :], rr_[:], ALU.mult, [B_y[k], Br_], [B_tmpf[k]])
            stt(xt[:, k, :], tmpf[:, k, :], gg[:, k, cidx:cidx + 1], xt[:, k, :], ALU.mult, ALU.add,
                [B_tmpf[k], B_g, B_xt[k]], [B_xt[k]])

    def postnorm_res(y3, B_y, gg, cidx, xt, B_xt):
        rr_, Br_ = rms_stats(y3, B_y, KC, D)
        post_apply(y3, B_y, rr_, Br_, gg, cidx, xt, B_xt)

    def norm_pipeline(get_y, gg, pre_args, final, ahead=False):
        pend = None
        ynext = get_y(0) if ahead else None
        for t in range(NT):
            i = t % 2
            load_x(t, i)
            y3, B_y = ynext if ahead else get_y(t)
            ra, Bra = rms_stats(y3, B_y, KC, D)
            if pend is not None:
                pend()
            post_apply(y3, B_y, ra, Bra, gg, cidx_of(t), xa[i], B_xa[i])
            if ahead and t + 1 < NT:
                ynext = get_y(t + 1)
            if final:
                pend = (lambda t=t, i=i: store_y(t, i))
            else:
                store_x(t, i)
                rc_, Brc = rms_stats(xa[i], B_xa[i], KC, D)
                gs_, sh_j0, l_ = pre_args
                pend = (lambda t=t, i=i, rc_=rc_, Brc=Brc: pre_apply(
                    xa[i], B_xa[i], rc_, Brc, gs_, sh_j0, l_, cidx_of(t), R0[:, :, t * TT:(t + 1) * TT], B_R0[t]))
        pend()

    B_stg0 = [Buf("stg0_%d" % t) for t in range(NT)]

    def stage_x0(t):
        st0 = R12[:].rearrange("p k t -> p (k t)")[:, t * 4096:(t + 1) * 4096].rearrange("p (s d) -> p s d", s=4)
        S.dma("sp", st0, xin[t * TT:(t + 1) * TT, :].rearrange("(s p) d -> p s d", p=P), [], [B_stg0[t]], B_stg0[t])

    def load_x0(t, i):
        st0 = R12[:].rearrange("p k t -> p (k t)")[:, t * 4096:(t + 1) * 4096].rearrange("p (s d) -> p s d", s=4)
        for k in range(KC):
            bk = bankA()
            for s_ in range(4):
                tr(psb[bk][:, s_ * P:(s_ + 1) * P], st0[:, s_, k * P:(k + 1) * P], ident[:],
                   [B_stg0[t], B_ident], [B_ps[bk]])
            copy(evac_eng(), xa[i][:, k, :], psb[bk][:], [B_ps[bk]], [B_xa[i][k]])

    def store_y(t, i):
        for s_ in range(4):
            for hh in range(2):
                bk = bankA()
                for kk in range(4):
                    k = hh * 4 + kk
                    tr(psb[bk][:, kk * P:(kk + 1) * P], xa[i][:, k, s_ * P:(s_ + 1) * P], ident[:],
                       [B_xa[i][k], B_ident], [B_ps[bk]])
                copy(evac_eng(), stv[:, s_, hh * 512:(hh + 1) * 512], psb[bk][:], [B_ps[bk]], B_tmpf)
        S.dma("sp", ys[t * TT:(t + 1) * TT, :].rearrange("(s p) d -> p s d", p=P), stv, B_tmpf, [], B_stg)

    def load_x(t, i):
        S.dma("sp", xa[i], xs[t], [xsb[t]], B_xa[i], B_xadma[i])

    def store_x(t, i):
        S.dma("sp", xs[t], xa[i], B_xa[i], [xsb[t]], B_xadma[i])

    def layer_vecs(l):
        for c in range(2):
            stt(gvec[:, l, 0, :, c], mod[:, l, 8:16, c], 1.0, vT[:, R_NMIXPRE + l * 8:R_NMIXPRE + l * 8 + 8],
                ALU.add, ALU.mult, [B_mod, B_vT], [B_g])
            tt("dve", gvec[:, l, 1, :, c], mod[:, l, 16:24, c], vT[:, R_NMIXPOST + l * 8:R_NMIXPOST + l * 8 + 8],
               ALU.mult, [B_mod, B_vT], [B_g])
            stt(gvec[:, l, 2, :, c], mod[:, l, 32:40, c], 1.0, vT[:, R_NMLPPRE + l * 8:R_NMLPPRE + l * 8 + 8],
                ALU.add, ALU.mult, [B_mod, B_vT], [B_g])
            tt("dve", gvec[:, l, 3, :, c], mod[:, l, 40:48, c], vT[:, R_NMLPPOST + l * 8:R_NMLPPOST + l * 8 + 8],
               ALU.mult, [B_mod, B_vT], [B_g])

    def GS1(l): return gvec[:, l, 0]
    def GG1(l): return gvec[:, l, 1]
    def GS2(l): return gvec[:, l, 2]
    def GG2(l): return gvec[:, l, 3]

    def mlp(l):
        nu = 0
        for g in range(8):
            i = g % 2
            S.dma("pool", w1v[i], w_mlp1[l, :, g * 512:(g + 1) * 512].rearrange("(k p) f -> p k f", p=P),
                  [], [B_wst[i]], B_wst[i])
            S.dma("pool", w2st[i], w_mlp2[l, g * 512:(g + 1) * 512, :].rearrange("(c p) d -> p c d", p=P),
                  [], [B_w2st[i]], B_w2st[i])

            def do_u(t, ui):
                for fc in range(4):
                    bk = bankA()
                    for k in range(KC):
                        mm(psb[bk][:], w1v[i][:, k, fc * P:(fc + 1) * P], R0[:, k, t * TT:(t + 1) * TT],
                           k == 0, k == KC - 1, [B_wst[i], B_R0[t][k]], [B_ps[bk]])
                    ri = fc % 2
                    act(rb[ri], psb[bk][:], AF.Relu, [B_ps[bk]], [B_rb[ri]])
                    tt("dve", ub[ui][:, fc, :], rb[ri], rb[ri], ALU.mult, [B_rb[ri]], [B_ub[ui]])

            def do_y(t, ui):
                for dc in range(KC):
                    bk = bankB()
                    for fc in range(4):
                        mm(psb[bk][:], w2st[i][:, fc, dc * P:(dc + 1) * P], ub[ui][:, fc, :],
                           fc == 0, fc == 3, [B_w2st[i], B_ub[ui]], [B_ps[bk]])
                    dst = yacc[:, dc, t * TT:(t + 1) * TT]
                    if g == 0:
                        copy("act", dst, psb[bk][:], [B_ps[bk]], [B_Y[t][dc]])
                    else:
                        tt("dve", dst, psb[bk][:], dst, ALU.add, [B_ps[bk], B_Y[t][dc]], [B_Y[t][dc]])

            do_u(0, nu % 2)
            for t in range(NT):
                if t + 1 < NT:
                    do_u(t + 1, (nu + 1) % 2)
                do_y(t, nu % 2)
                nu += 1


    R12f = R12[:].rearrange("p k t -> p (k t)")
    OT = R12f[:, 0:10240].bitcast(BF16).rearrange("p (k t) -> p k t", k=KC)
    B_OT = [[Buf("OT%d_%d" % (t, k)) for k in range(KC)] for t in range(NT)]
    R2w = R12f[:, 10240:20480]

    def r2(off, n, dt=F32):
        a = R2w[:, off:off + n]
        return a.bitcast(BF16) if dt == BF16 else a

    SCQ = float(192 ** -0.5)

    def mla(l):
        i = l // 2
        win = v3(wv(0, 2560, BF16), KC)
        wq = v3(wv(2560, 2048, BF16), 2)
        wkv = v3(wv(4608, 2048, BF16), 2)
        kpeT = wv(6656, 1408, BF16)
        kTh = wv(8064, 1408, BF16)
        Vh = v3(wv(9472, 1408, BF16), 22)
        qnh = wv(10880, 1280, BF16)
        qrh = wv(12160, 1280, BF16)
        pT = [wv(13440 + j * 256, 256, BF16) for j in range(3)]
        qlatn = v3(r2(0, 2560, BF16), 2)
        ckvall = v3(r2(2560, 2816, BF16), 2)
        tq = v3(r2(5376, 1024), 2)
        tc = v3(r2(6400, 1024), 2)
        tk = r2(7424, 512)
        tkp = r2(7936, 512)
        tmp2 = v3(r2(8448, 1024), 2)
        stgo = r2(8448, 1024).rearrange("p (s f) -> p s f", s=4)
        cst = r2(9472, 512).rearrange("p (s f) -> p s f", s=2)
        kst = r2(9984, 128).rearrange("p (s f) -> p s f", s=2)
        ksto = r2(9472, 256).rearrange("p (s f) -> p s f", s=4)
        B_win, B_wq, B_wkv, B_kpe, B_kT, B_V, B_qn, B_qr = (Buf(n) for n in
                                                           ("win", "wq", "wkv", "kpeT", "kTh", "Vh", "qnh", "qrh"))
        B_pT = [Buf("pT%d" % j) for j in range(3)]
        pacc = r2(8448, 512)
        B_qlat = [Buf("qlat%d" % t) for t in range(NT)]
        B_ckv = [Buf("ckvall%d" % t) for t in range(NT + 1)]
        B_tq, B_tc, B_tk, B_tkp, B_tmp2, B_cst, B_kst = (Buf(n) for n in ("tq", "tc", "tk", "tkp", "tmp2", "cst", "kst"))
        B_pacc = B_tmp2

        S.dma("pool", win[:, :, 0:576], w_in_odd[i].rearrange("(k p) f -> p k f", p=P), [], [B_win], B_win)
        S.dma("pool", win[:, :, 576:608], w_in_odd[i, :, 544:576].rearrange("(k p) f -> p k f", p=P), [], [B_win], B_win)
        S.dma("pool", win[:, :, 608:640], w_in_odd[i, :, 512:544].rearrange("(k p) f -> p k f", p=P), [], [B_win], B_win)
        S.dma("pool", wq[:, :, 0:1536], w_q_b[i].rearrange("(k p) f -> p k f", p=P), [], [B_wq], B_wq)
        wqp = wq[:, :, 1536:2048].rearrange("p k (h d) -> p k h d", d=64)
        wqs = w_q_b[i].rearrange("(k p) (h d) -> p k h d", p=P, d=192)
        for kk in range(2):
            S.dma("pool", wqp[:, kk, :, 0:32], wqs[:, kk, :, 160:192], [], [B_wq], B_wq)
            S.dma("pool", wqp[:, kk, :, 32:64], wqs[:, kk, :, 128:160], [], [B_wq], B_wq)
        S.dma("pool", wkv, w_kv_b[i].rearrange("(k p) f -> p k f", p=P), [], [B_wkv], B_wkv)

        S.dma("sp", cst, cckv[i].rearrange("(s p) f -> p s f", p=P), [], [B_cst], B_cst)
        S.dma("sp", kst, ckpe[i].rearrange("(s p) f -> p s f", p=P), [], [B_kst], B_kst)
        for kc in range(2):
            bk = bankA()
            for s_ in range(2):
                tr(psb[bk][:, s_ * P:(s_ + 1) * P], cst[:, s_, kc * P:(kc + 1) * P], ident[:], [B_cst, B_ident], [B_ps[bk]])
            copy(evac_eng(), ckvall[:, kc, 0:256], psb[bk][:, 0:256], [B_ps[bk]], [B_ckv[0]])
        bk = bankA()
        for s_ in range(2):
            tr(psb[bk][0:64, s_ * P:(s_ + 1) * P], kst[:, s_, :], ident[:], [B_kst, B_ident], [B_ps[bk]])
        copy(evac_eng(), kpeT[0:64, 0:256], psb[bk][0:64, 0:256], [B_ps[bk]], [B_kpe])

        STOP = float(os.environ.get("K_STOP", "9"))
        if STOP <= 1:
            return
        for t in range(NT):
            tsl = slice(t * TT, (t + 1) * TT)
            ksl = slice(256 + t * TT, 256 + (t + 1) * TT)
            for m in range(4):
                bk = bankA()
                for k in range(KC):
                    mm(psb[bk][:], win[:, k, m * P:(m + 1) * P], R0[:, k, tsl], k == 0, k == KC - 1,
                       [B_win, B_R0[t][k]], [B_ps[bk]])
                if m < 2:
                    copy(evac_eng(), tq[:, m, :], psb[bk][:], [B_ps[bk]], [B_tq])
                else:
                    copy(evac_eng(), tc[:, m - 2, :], psb[bk][:], [B_ps[bk]], [B_tc])
            if STOP <= 1.05:
                continue
            bk1 = bankA()
            for k in range(KC):
                mm(psb[bk1][0:64, :], win[:, k, 512:576], R0[:, k, tsl], k == 0, k == KC - 1, [B_win, B_R0[t][k]], [B_ps[bk1]])
            if STOP <= 1.1:
                continue
            if t < 4:
                bk2 = bankA()
                for k in range(KC):
                    mm(psb[bk2][0:64, :], win[:, k, 576:640], R0[:, k, tsl], k == 0, k == KC - 1,
                       [B_win, B_R0[t][k]], [B_ps[bk2]])
                tt("dve", tk[0:64, :], psb[bk1][0:64, :], ropeC[0:64, tsl], ALU.mult, [B_ps[bk1], B_rope], [B_tk])
                tt("dve", tkp[0:64, :], psb[bk2][0:64, :], ropeS[0:64, tsl], ALU.mult, [B_ps[bk2], B_rope], [B_tkp])
                tt("dve", kpeT[0:64, ksl], tk[0:64, :], tkp[0:64, :], ALU.add, [B_tk, B_tkp], [B_kpe])
            else:
                copy("dve", tk[0:64, :], psb[bk1][0:64, :], [B_ps[bk1]], [B_tk])
                copy("act", kpeT[0:64, ksl], tk[0:64, :], [B_tk], [B_kpe])
            if STOP <= 1.2:
                continue
            rr_, Br_ = rms_stats(tq, B_tq, 2, 256, sq2, B_sq2)
            tt("dve", tmp2, tq, rr_[:].unsqueeze(1).to_broadcast([P, 2, TT]), ALU.mult, [B_tq, Br_], [B_tmp2])
            for m in range(2):
                ts("dve", qlatn[:, m, tsl], tmp2[:, m, :], vT[:, R_QAN + i * 2 + m:R_QAN + i * 2 + m + 1], None,
                   ALU.mult, ALU.bypass, [B_tmp2, B_vT], [B_qlat[t]])
            rr_, Br_ = rms_stats(tc, B_tc, 2, 256, sq2, B_sq2)
            tt("dve", tmp2, tc, rr_[:].unsqueeze(1).to_broadcast([P, 2, TT]), ALU.mult, [B_tc, Br_], [B_tmp2])
            for m in range(2):
                ts("dve", tc[:, m, :], tmp2[:, m, :], vT[:, R_KVAN + i * 2 + m:R_KVAN + i * 2 + m + 1], None,
                   ALU.mult, ALU.bypass, [B_tmp2, B_vT], [B_tc])
            copy("act", ckvall[:, :, ksl], tc, [B_tc], [B_ckv[t + 1]])
            if t == 4 and STOP > 1.5:
                for sub in range(4):
                    bk = bankA()
                    for m in range(2):
                        tr(psb[bk][:, m * P:(m + 1) * P], tc[:, m, sub * P:(sub + 1) * P], ident[:],
                           [B_tc, B_ident], [B_ps[bk]])
                    copy(evac_eng(), stgo[:, sub, :], psb[bk][:, 0:256], [B_ps[bk]], [B_tmp2])
                for sq_ in range(2):
                    S.dma("sp", ockv[sq_, i].rearrange("(s p) f -> p s f", p=P), stgo[:, sq_ * 2:sq_ * 2 + 2, :],
                          [B_tmp2], [], B_tmp2)
                for sub in range(4):
                    bk = bankA()
                    tr(psb[bk][:, 0:64], tk[0:64, sub * P:(sub + 1) * P], ident[0:64, 0:64], [B_tk, B_ident], [B_ps[bk]])
                    copy(evac_eng(), ksto[:, sub, :], psb[bk][:, 0:64], [B_ps[bk]], [B_cst])
                for sq_ in range(2):
                    S.dma("sp", okpe[sq_, i].rearrange("(s p) f -> p s f", p=P), ksto[:, sq_ * 2:sq_ * 2 + 2, :],
                          [B_cst], [], B_cst)

        if STOP <= 2:
            return
        qblocks = [(qb * TT, TT, list(range(0, 18))) for qb in range(4)]
        qblocks += [(2048, 256, [18, 19]), (2304, 256, [20, 21])]
        for h in range(8):
            for cb in range(6):
                c0 = cb * 512
                n = min(512, 2816 - c0)
                bk = bankA()
                for kc in range(2):
                    mm(psb[bk][:, 0:n], wkv[:, kc, h * 256:h * 256 + P], ckvall[:, kc, c0:c0 + n], kc == 0, kc == 1,
                       [B_wkv] + B_ckv, [B_ps[bk]])
                copy(evac_eng(), kTh[:, c0:c0 + n], psb[bk][:, 0:n], [B_ps[bk]], [B_kT])
            for g4 in range(6):
                bk = bankA()
                kts = list(range(g4 * 4, min(22, g4 * 4 + 4)))
                for j, kt in enumerate(kts):
                    for kc in range(2):
                        mm(psb[bk][:, j * P:(j + 1) * P], ckvall[:, kc, kt * P:(kt + 1) * P],
                           wkv[:, kc, h * 256 + P:h * 256 + 2 * P], kc == 0, kc == 1, [B_wkv] + B_ckv, [B_ps[bk]])
                nn = len(kts) * P
                copy(evac_eng(), Vh[:, kts[0]:kts[0] + len(kts), :].rearrange("p a b -> p (a b)"), psb[bk][:, 0:nn],
                     [B_ps[bk]], [B_V])
            for t in range(NT):
                tsl = slice(t * TT, (t + 1) * TT)
                bk = bankA()
                for kc in range(2):
                    mm(psb[bk][:], wq[:, kc, h * 192:h * 192 + P], qlatn[:, kc, tsl], kc == 0, kc == 1,
                       [B_wq, B_qlat[t]], [B_ps[bk]])
                act(qnh[:, tsl], psb[bk][:], AF.Copy, [B_ps[bk]], [B_qn], scale=SCQ)
                bk1 = bankA()
                for kc in range(2):
                    mm(psb[bk1][0:64, :], wq[:, kc, h * 192 + P:h * 192 + 192], qlatn[:, kc, tsl], kc == 0, kc == 1,
                       [B_wq, B_qlat[t]], [B_ps[bk1]])
                if t < 4:
                    bk2 = bankA()
                    for kc in range(2):
                        mm(psb[bk2][0:64, :], wq[:, kc, 1536 + h * 64:1536 + (h + 1) * 64], qlatn[:, kc, tsl],
                           kc == 0, kc == 1, [B_wq, B_qlat[t]], [B_ps[bk2]])
                    stt(tk[0:64, :], psb[bk1][0:64, :], SCQ, ropeC[0:64, tsl], ALU.mult, ALU.mult,
                        [B_ps[bk1], B_rope], [B_tk])
                    stt(tkp[0:64, :], psb[bk2][0:64, :], SCQ, ropeS[0:64, tsl], ALU.mult, ALU.mult,
                        [B_ps[bk2], B_rope], [B_tkp])
                    tt("dve", qrh[0:64, tsl], tk[0:64, :], tkp[0:64, :], ALU.add, [B_tk, B_tkp], [B_qr])
                else:
                    act(qrh[0:64, tsl], psb[bk1][0:64, :], AF.Copy, [B_ps[bk1]], [B_qr], scale=SCQ)
            npt = 0
            if STOP <= 3:
                continue
            for (q0, nq, kts) in qblocks:
                if l + 1 < nl:
                    ada_mm(l + 1)
                qsl = slice(q0, q0 + nq)
                tq_ = q0 // TT
                obk = bankB()
                sbk = bankB()
                sbanks = {}

                def smm(kt):
                    b_ = bankA()
                    sbanks[kt] = b_
                    mm(psb[b_][:, 0:nq], kTh[:, kt * P:(kt + 1) * P], qnh[:, qsl], True, False, [B_kT, B_qn], [B_ps[b_]])
                    mm(psb[b_][:, 0:nq], kpeT[0:64, kt * P:(kt + 1) * P], qrh[0:64, qsl], False, True,
                       [B_kpe, B_qr], [B_ps[b_]])

                smm(kts[0])
                if len(kts) > 1:
                    smm(kts[1])
                for j, kt in enumerate(kts):
                    if j + 2 < len(kts):
                        smm(kts[j + 2])
                    pj = npt % 3
                    npt += 1
                    b_ = sbanks[kt]
                    act(pT[pj][:, 0:nq], psb[b_][:, 0:nq], AF.Exp, [B_ps[b_]], [B_pT[pj]])
                    mm(psb[obk][:, 0:nq], Vh[:, kt, :], pT[pj][:, 0:nq], j == 0, j == len(kts) - 1,
                       [B_V, B_pT[pj]], [B_ps[obk]])
                    mm(psb[sbk][:, 0:nq], onesb[:], pT[pj][:, 0:nq], j == 0, j == len(kts) - 1,
                       [B_ones, B_pT[pj]], [B_ps[sbk]])
                act(rstd[:, 0:nq], psb[sbk][:, 0:nq], AF.Ln, [B_ps[sbk]], [B_rstd])
                act(rs1[:, 0:nq], rstd[:, 0:nq], AF.Exp, [B_rstd], [B_rs1], scale=-1.0)
                tt("dve", OT[:, h, qsl], psb[obk][:, 0:nq], rs1[:, 0:nq], ALU.mult, [B_ps[obk], B_rs1], [B_OT[tq_][h]])


    def even(l):
        i = l // 2
        wg = v3(wv(0, 4096, BF16), KC)
        QD = [wv(4096 + d * 1280, 1280, BF16) for d in range(2)]
        KI = [wv(6656 + d * 1280, 1280, BF16) for d in range(2)]
        KU = [v3(wv(9216 + d * 1280, 1280, BF16), 20) for d in range(2)]
        Vtok = v3(wv(11776, 2560, BF16), 20)
        SB = [v3(r2(d * 2560, 2560, BF16), NCH) for d in range(2)]
        qf = r2(5120, 512)
        kf = r2(5632, 512)
        qfs = [r2(5120 + j * 256, 256, BF16) for j in range(2)]
        kfs = [r2(5632 + j * 256, 256, BF16) for j in range(2)]
        t1 = r2(6144, 512)
        t2 = r2(6656, 512)
        t3 = r2(7168, 512)
        kuT = r2(7680, 256, BF16)
        t1B = r2(7936, 512)
        t2B = r2(8448, 512)
        t3B = r2(9600, 512)
        gk2kuT = r2(8960, 256, BF16)
        aTs = [r2(7936 + j * 128, 128, BF16) for j in range(2)]
        Sf = r2(8192, 512).rearrange("p (d q v) -> p d q v", d=2, q=2)
        SGt = v3(r2(8704, 512, BF16), 2)
        gk = r2(9216, 256, BF16)
        wgk = r2(9472, 128, BF16).rearrange("p (d f) -> p d f", d=2)

        def c64(a):
            return a.rearrange("p (c j) -> p c j", j=64)

        B_wg, B_V, B_qf, B_kf, B_t1, B_t2, B_t3, B_kuT, B_SGt, B_gk, B_wgk, B_D, B_rt, B_nb = (
            Buf(n) for n in ("wg", "Vtok", "qf", "kf", "t1", "t2", "t3", "kuT", "SGt", "gk", "wgk", "Dch", "rtab", "negb"))
        B_qfs, B_kfs, B_gks = [Buf("qfA"), Buf("qfB")], [Buf("kfA"), Buf("kfB")], [Buf("gkA"), Buf("gkB")]
        B_t1B, B_t2B, B_t3B = Buf("t1B"), Buf("t2B"), Buf("t3B")
        B_sq2b = Buf("sq2b")
        TS = [((t1, B_t1), (t2, B_t2), (t3, B_t3)), ((t1B, B_t1B), (t2B, B_t2B), (t3B, B_t3B))]
        kuTs = [kuT, gk2kuT]
        B_kuTs = [B_kuT, Buf("kuTB")]
        B_QD = [[Buf("QD") for t in range(NT)] for d in range(2)]
        B_KI = [[Buf("KI") for t in range(NT)] for d in range(2)]
        B_KU = [[Buf("KU") for t in range(NT)] for d in range(2)]
        B_Vt = [Buf("Vt") for t in range(NT)]
        B_SB = [[Buf("SB") for n in range(NCH)] for d in range(2)]
        B_Sf = [[Buf("Sf") for q in range(2)] for d in range(2)]
        B_aT = [Buf("aT0"), Buf("aT1")]

        ts("dve", negb[:], vT[:, R_BGK:R_BGK + 8], -1.0, None, ALU.mult, ALU.bypass, [B_vT], [B_nb])

        def load_wg(g):
            isret = g >= 2
            pg = g % 2
            wsrc = w_in_even[i]

            def wcols(c0, n):
                return wsrc[:, c0:c0 + n].rearrange("(k p) f -> p k f", p=P)

            if not isret:
                cq, ck, cv, cg = pg * 128, 256 + pg * 128, 512 + pg * 256, 1024 + pg * 256
            else:
                cq, ck, cv, cg = 1568 + pg * 128, 1824 + pg * 128, 2080 + pg * 256, 2592 + pg * 256
            S.dma("pool", wg[:, :, 0:128], wcols(cq, 128), [], [B_wg], B_wg)
            S.dma("pool", wg[:, :, 128:256], wcols(ck, 128), [], [B_wg], B_wg)
            S.dma("pool", wg[:, :, 256:512], wcols(cv, 256), [], [B_wg], B_wg)
            S.dma("pool", wg[:, :, 512:768], wcols(cg, 256), [], [B_wg], B_wg)
            if not isret:
                S.dma("pool", wg[:, :, 768:800], wcols(1536, 32), [], [B_wg], B_wg)
                S.op("dve", "memset", (wgk[0:64, :, :], 0.0), {}, [], [B_wgk])
                for d in range(2):
                    for rep in range(2):
                        S.dma("pool", wgk[rep * 32 + d * 16:rep * 32 + (d + 1) * 16, d, :],
                              w_gk2[i, d, :, pg * 128:(pg + 1) * 128], [], [B_wgk], B_wgk)
            else:
                for (dst0, csrc) in ((768, cq), (896, ck)):
                    dv_ = wg[:, :, dst0:dst0 + 128].rearrange("p k (h d) -> p k h d", d=64)
                    sv_ = wsrc[:, csrc:csrc + 128].rearrange("(k p) (h d) -> p k h d", p=P, d=64)
                    for hh in range(2):
                        S.dma("pool", dv_[:, :, hh, 0:32], sv_[:, :, hh, 32:64], [], [B_wg], B_wg)
                        S.dma("pool", dv_[:, :, hh, 32:64], sv_[:, :, hh, 0:32], [], [B_wg], B_wg)

        load_wg(0)
        for g in range(4):
            isret = g >= 2
            pg = g % 2
            if isret:
                for d in range(2):
                    lc = i * 4 + pg * 2 + d
                    xi = 1 if d == 0 else 3
                    yi = 2 if d == 0 else 0
                    act(rtab[:, d, 0, :], iot[:, xi, :], AF.Exp, [B_ec2], [B_rt], scale=lgm[:, lc:lc + 1])
                    act(rtab[:, d, 1, :], iot[:, xi, :], AF.Exp, [B_ec2], [B_rt], scale=lgn[:, lc:lc + 1])
                    act(rtab[:, d, 2, :], iot[:, yi, :], AF.Exp, [B_ec2], [B_rt], scale=lgm[:, lc:lc + 1])
                    act(Dret[:, d:d + 1], iot[:, 3, 0:1], AF.Exp, [B_ec2], [B_rt], scale=lgm[:, lc:lc + 1])
                    copy("dve", Dch[:, d, :], Dret[:, d:d + 1].to_broadcast([P, NCH]), [B_rt], [B_D])

            pbanks = {}

            def PE_PROJ(t):
                tsl = slice(t * TT, (t + 1) * TT)
                rope = isret and t < 4
                g0 = (t % 2) * 32

                def proj(c0, n=128, p0=0):
                    b_ = bankA()
                    for k in range(KC):
                        mm(psb[b_][p0:p0 + n, :], wg[:, k, c0:c0 + n], R0[:, k, tsl], k == 0, k == KC - 1,
                           [B_wg, B_R0[t][k]], [B_ps[b_]])
                    return b_

                bq = proj(0)
                bkk = proj(128)
                bqp = bkp = bgk = None
                if rope:
                    bqp = proj(768)
                    bkp = proj(896)
                if not isret:
                    bgk = proj(768, 32, g0)
                pbanks[t] = (bq, bkk, bqp, bkp, bgk)
                for half in range(2):
                    b_ = bankB()
                    for sub in range(2):
                        blk = t * 4 + half * 2 + sub
                        for k in range(KC):
                            mm(psb[b_][:, sub * 256:(sub + 1) * 256], R0[:, k, blk * P:(blk + 1) * P], wg[:, k, 256:512],
                               k == 0, k == KC - 1, [B_wg, B_R0[t][k]], [B_ps[b_]])
                    blk0 = t * 4 + half * 2
                    copy(evac_eng(), Vtok[:, blk0:blk0 + 2, :].rearrange("p a b -> p (a b)"), psb[b_][:],
                         [B_ps[b_]], [B_Vt[t]])

            def EVAC(t):
                tsl = slice(t * TT, (t + 1) * TT)
                rope = isret and t < 4
                qf, kf, B_qf, B_kf = qfs[t % 2], kfs[t % 2], B_qfs[t % 2], B_kfs[t % 2]
                g0 = (t % 2) * 32
                bq, bkk, bqp, bkp, bgk = pbanks[t]
                if rope:
                    tt("dve", t1, psb[bq][:], ropeC[:, tsl], ALU.mult, [B_ps[bq], B_rope], [B_t1])
                    tt("dve", t2, psb[bqp][:], ropeS[:, tsl], ALU.mult, [B_ps[bqp], B_rope], [B_t2])
                    tt("dve", qf, t1, t2, ALU.add, [B_t1, B_t2], [B_qf])
                    stt(t1, psb[bkk][:], 0.125, ropeC[:, tsl], ALU.mult, ALU.mult, [B_ps[bkk], B_rope], [B_t1])
                    stt(t2, psb[bkp][:], 0.125, ropeS[:, tsl], ALU.mult, ALU.mult, [B_ps[bkp], B_rope], [B_t2])
                    tt("dve", kf, t1, t2, ALU.add, [B_t1, B_t2], [B_kf])
                elif isret:
                    copy("dve", qf, psb[bq][:], [B_ps[bq]], [B_qf])
                    act(kf, psb[bkk][:], AF.Copy, [B_ps[bkk]], [B_kf], scale=0.125)
                else:
                    act(qf, psb[bq][:], AF.Copy, [B_ps[bq]], [B_qf], scale=0.125)
                    copy("dve", kf, psb[bkk][:], [B_ps[bkk]], [B_kf])
                    copy("act", gk[g0:g0 + 32, :], psb[bgk][g0:g0 + 32, :], [B_ps[bgk]], [B_gks[t % 2]])

            def DECAY(t):
                tsl = slice(t * TT, (t + 1) * TT)
                qf, kf, B_qf, B_kf = qfs[t % 2], kfs[t % 2], B_qfs[t % 2], B_kfs[t % 2]
                g0 = (t % 2) * 32
                ME = ("dve", "pool")
                if not isret:
                    bgp = [bankB(), bankB()]
                    Xs, Ys, Es = [None, None], [None, None], [None, None]
                    for d in range(2):
                        mm(psb[bgp[d]][:], wgk[g0:g0 + 32, d, :], gk[g0:g0 + 32, :], True, True,
                           [B_wgk, B_gks[t % 2]], [B_ps[bgp[d]]])
                    for d in range(2):
                        a1, a2, a3 = TS[d]
                        nbc = i * 4 + d * 2 + pg
                        act(a1[0], psb[bgp[d]][:], AF.Exp, [B_ps[bgp[d]], B_nb], [a1[1]],
                            bias=negb[:, nbc:nbc + 1], scale=-1.0)
                    for d in range(2):
                        a1, a2, a3 = TS[d]
                        act(a2[0], a1[0], AF.Ln, [a1[1], B_ec2], [a2[1]], bias=onesf[:, 0:1], scale=1.0)
                    for d in range(2):
                        a1, a2, a3 = TS[d]
                        S.op("dve", "tensor_tensor_scan", (a3[0], resetm[:], a2[0], 0.0, ALU.mult, ALU.add), {},
                             [B_ec, a2[1]], [a3[1]])
                    for d in range(2):
                        a1, a2, a3 = TS[d]
                        tpb = c64(a3[0])[:, :, 63:64].to_broadcast([P, 8, 64])
                        act(Dch[:, d, t * 8:(t + 1) * 8], c64(a3[0])[:, :, 63], AF.Exp, [a3[1]], [B_D], scale=-1.0 / 16)
                        if d == 0:
                            tt("dve", c64(a1[0]), tpb, c64(a3[0]), ALU.subtract, [a3[1]], [a1[1]])
                            Xs[d], Ys[d], Es[d] = a3, a1, a2
                        else:
                            tt("dve", a2[0], a3[0], a2[0], ALU.subtract, [a3[1], a2[1]], [a2[1]])
                            tt("dve", c64(a1[0]), tpb, c64(a2[0]), ALU.subtract, [a3[1], a2[1]], [a1[1]])
                            Xs[d], Ys[d], Es[d] = a1, a2, a3
                    for d in range(2):
                        act(Es[d][0], Xs[d][0], AF.Exp, [Xs[d][1]], [Es[d][1]], scale=-1.0 / 16)
                    for d in range(2):
                        tt(ME[d], QD[d][:, tsl], qf, Es[d][0], ALU.mult, [B_qf, Es[d][1]], [B_QD[d][t]])
                    for d in range(2):
                        act(Es[d][0], Xs[d][0], AF.Exp, [Xs[d][1]], [Es[d][1]], scale=1.0 / 16)
                    for d in range(2):
                        tt(ME[d], KI[d][:, tsl], kf, Es[d][0], ALU.mult, [B_kf, Es[d][1]], [B_KI[d][t]])
                    for d in range(2):
                        act(Es[d][0], Ys[d][0], AF.Exp, [Ys[d][1]], [Es[d][1]], scale=-1.0 / 16)
                    for d in range(2):
                        tt(ME[d], kuTs[d], kf, Es[d][0], ALU.mult, [B_kf, Es[d][1]], [B_kuTs[d]])
                else:
                    for d in range(2):
                        def bc(j):
                            return rtab[:, d, j, :].unsqueeze(1).to_broadcast([P, 8, 64])
                        tt(ME[d], c64(QD[d][:, tsl]), c64(qf), bc(0), ALU.mult, [B_qf, B_rt], [B_QD[d][t]])
                        tt(ME[d], c64(KI[d][:, tsl]), c64(kf), bc(1), ALU.mult, [B_kf, B_rt], [B_KI[d][t]])
                        tt(ME[d], c64(kuTs[d]), c64(kf), bc(2), ALU.mult, [B_kf, B_rt], [B_kuTs[d]])
                for d in range(2):
                    bt = bankB()
                    pst = psb[bt][:].bitcast(BF16)
                    for sub in range(4):
                        tr(pst[:, sub * P:(sub + 1) * P], kuTs[d][:, sub * P:(sub + 1) * P], identb[:],
                           [B_kuTs[d], B_identb], [B_ps[bt]])
                    copy(evac_eng(), KU[d][:, t * 4:(t + 1) * 4, :].rearrange("p a b -> p (a b)"), pst[:, 0:512],
                         [B_ps[bt]], [B_KU[d][t]])

            PE_PROJ(0)
            EVAC(0)
            PE_PROJ(1)
            for t in range(NT):
                if t + 1 < NT:
                    EVAC(t + 1)
                if t + 2 < NT:
                    PE_PROJ(t + 2)
                DECAY(t)
            for t in range(NT):
                tsl = slice(t * TT, (t + 1) * TT)
                for hh in range(2):
                    oi = (4 if isret else 0) + pg * 2 + hh
                    bg_ = bankB()
                    for k in range(KC):
                        mm(psb[bg_][:], wg[:, k, 512 + hh * P:512 + (hh + 1) * P], R0[:, k, tsl], k == 0, k == KC - 1,
                           [B_wg, B_R0[t][k]], [B_ps[bg_]])
                    act(OT[:, oi, tsl], psb[bg_][:], AF.Silu, [B_ps[bg_]], [B_OT[t][oi]])
            S.barrier()
            if g + 1 < 4:
                load_wg(g + 1)

            st_in = sret if isret else sgla
            st_out = oret if isret else ogla
            for si, (c0, c1) in enumerate(((0, 32), (32, 36), (36, 40))):
                pp = [0, 0]
                for d in range(2):
                    cur = Sf[:, d, 0, :]
                    if si == 0:
                        S.dma("sp", cur, st_in[i, d, 2 * pg:2 * pg + 2].rearrange("h k v -> (h k) v"),
                              [], [B_Sf[d][0]], B_Sf[d][0])
                    else:
                        S.op("dve", "memset", (cur, 0.0), {}, [], [B_Sf[d][0]])
                orders = [list(range(c0, c1)), list(range(c1 - 1, c0 - 1, -1))]
                for step in range(c1 - c0):

                    for d in range(2):
                        n = orders[d][step]
                        cur = Sf[:, d, pp[d], :]
                        Bc = B_Sf[d][pp[d]]
                        nxt = Sf[:, d, 1 - pp[d], :]
                        Bn = B_Sf[d][1 - pp[d]]
                        copy("act", SB[d][:, n, :], cur, [Bc], [B_SB[d][n]])
                        blk, r0 = n // 2, (n % 2) * 64
                        tl = n // 8
                        bu = bankA()
                        for hh in range(2):
                            mm(psb[bu][hh * 64:(hh + 1) * 64, 0:P], KU[d][r0:r0 + 64, blk, hh * 64:(hh + 1) * 64],
                               Vtok[r0:r0 + 64, blk, hh * P:(hh + 1) * P], True, True,
                               [B_KU[d][tl], B_Vt[tl]], [B_ps[bu]])
                        stt(nxt, cur, Dch[:, d, n:n + 1], psb[bu][:, 0:P], ALU.mult, ALU.add, [Bc, B_D, B_ps[bu]], [Bn])
                        pp[d] = 1 - pp[d]
                if si > 0:
                    for d in range(2):
                        S.dma("sp", st_out[si - 1, i, d, 2 * pg:2 * pg + 2].rearrange("h k v -> (h k) v"),
                              Sf[:, d, pp[d], :], [B_Sf[d][pp[d]]], [], B_Sf[d][pp[d]])

            for t in range(NT):
                tsl = slice(t * TT, (t + 1) * TT)
                bo = [bankB(), bankB()]
                bj = [bankB(), bankB()]
                chs = [cn for par in range(2) for cn in range(par, 8, 2)]

                def emit_aT(idx):
                    cn = chs[idx]
                    n = t * 8 + cn
                    r0 = (n % 2) * 64
                    csl = slice(n * 64, (n + 1) * 64)
                    rc = slice(r0, r0 + 64)
                    j = idx % 2
                    for hh in range(2):
                        rh = slice(hh * 64, hh * 64 + 64)
                        ba = bankA()
                        for d in range(2):
                            mm(psb[ba][rc, d * 64:(d + 1) * 64], KI[d][rh, csl], QD[d][rh, csl], True, True,
                               [B_KI[d][t], B_QD[d][t]], [B_ps[ba]])
                        tt("dve", aTs[j][rc, hh * P:(hh + 1) * P], psb[ba][rc, 0:P], m4[rc, 0:P], ALU.mult,
                           [B_ps[ba], B_ec], [B_aT[j]])

                emit_aT(0)
                for idx in range(8):
                    if l + 1 < nl and (t * 8 + idx) % 3 == 0 and (t * 8 + idx) < 36:
                        ada_mm(l + 1)
                    if idx + 1 < 8:
                        emit_aT(idx + 1)
                    cn = chs[idx]
                    n = t * 8 + cn
                    blk, r0 = n // 2, (n % 2) * 64
                    csl = slice(n * 64, (n + 1) * 64)
                    rc = slice(r0, r0 + 64)
                    j = idx % 2
                    for hh in range(2):
                        oo = psb[bo[hh]][:, cn * 64:(cn + 1) * 64]
                        vv = Vtok[rc, blk, hh * P:(hh + 1) * P]
                        mm(oo, vv, aTs[j][rc, (hh * 2) * 64:(hh * 2 + 1) * 64], True, False,
                           [B_Vt[t], B_aT[j]], [B_ps[bo[hh]]])
                        mm(oo, vv, aTs[j][rc, (hh * 2 + 1) * 64:(hh * 2 + 2) * 64], False, True,
                           [B_Vt[t], B_aT[j]], [B_ps[bo[hh]]])
                    for hh in range(2):
                        rh = slice(hh * 64, hh * 64 + 64)
                        oj = psb[bj[hh]][:, cn * 64:(cn + 1) * 64]
                        mm(oj, SB[0][rh, n, :], QD[0][rh, csl], True, False, [B_SB[0][n], B_QD[0][t]], [B_ps[bj[hh]]])
                        mm(oj, SB[1][rh, n, :], QD[1][rh, csl], False, True, [B_SB[1][n], B_QD[1][t]], [B_ps[bj[hh]]])
                HB = [((t3, B_t3), (t1, B_t1), (t2, B_t2), (rstd, B_rstd)),
                      ((qf, B_qf), (kf, B_kf), (t3B, B_t3B), (rstd2, B_rstd2))]
                B_sqh = [B_sq2, B_sq2b]
                for hh in range(2):
                    (c_, Bc_), (o_, Bo_), _, _ = HB[hh]
                    copy("act", c_, psb[bo[hh]][:], [B_ps[bo[hh]]], [Bc_])
                for hh in range(2):
                    (c_, Bc_), (o_, Bo_), _, _ = HB[hh]
                    tt("dve", o_, c_, psb[bj[hh]][:], ALU.add, [Bc_, B_ps[bj[hh]]], [Bo_])
                bss = [bankA(), bankA()]
                for hh in range(2):
                    (c_, Bc_), (o_, Bo_), _, _ = HB[hh]
                    act(sq2[:, hh, :], o_, AF.Square, [Bo_], [B_sqh[hh]])
                    mm(psb[bss[hh]][:], onesb[:], sq2[:, hh, :], True, True, [B_ones, B_sqh[hh]], [B_ps[bss[hh]]])
                for hh in range(2):
                    (c_, Bc_), (o_, Bo_), _, (r_, Br_) = HB[hh]
                    act(c_, psb[bss[hh]][:], AF.Ln, [B_ps[bss[hh]], B_eps], [Bc_], bias=epsb[:, 0:1], scale=1.0 / 128)
                for hh in range(2):
                    (c_, Bc_), (o_, Bo_), _, (r_, Br_) = HB[hh]
                    act(r_[:], c_, AF.Exp, [Bc_], [Br_], scale=-0.5)
                for hh in range(2):
                    (c_, Bc_), (o_, Bo_), (m_, Bm_), (r_, Br_) = HB[hh]
                    tt("pool" if hh else "dve", m_, o_, r_[:], ALU.mult, [Bo_, Br_], [Bm_])
                for hh in range(2):
                    (c_, Bc_), (o_, Bo_), (m_, Bm_), (r_, Br_) = HB[hh]
                    if not isret:
                        oi = pg * 2 + hh
                        stt(OT[:, oi, tsl], m_, vT[:, R_GLAN + i:R_GLAN + i + 1], OT[:, oi, tsl],
                            ALU.mult, ALU.mult, [Bm_, B_vT, B_OT[t][oi]], [B_OT[t][oi]])
                    else:
                        oi = 4 + pg * 2 + hh
                        tt("dve", OT[:, oi, tsl], m_, OT[:, oi, tsl], ALU.mult, [Bm_, B_OT[t][oi]], [B_OT[t][oi]])
            S.barrier()

    def c1(l, w_out_l):
        wout = v3(r2(0, 4096, BF16), KC)
        y3 = v3(r2(4096, 4096), KC)
        B_wout = Buf("wout")
        B_y3 = [Buf("y3_%d" % k) for k in range(KC)]
        S.dma("pool", wout, w_out_l.rearrange("(k p) f -> p k f", p=P), [], [B_wout], B_wout)

        def get_y(t):
            tsl = slice(t * TT, (t + 1) * TT)
            for dc in range(KC):
                bk = bankA()
                for k in range(KC):
                    mm(psb[bk][:], wout[:, k, dc * P:(dc + 1) * P], OT[:, k, tsl], k == 0, k == KC - 1,
                       [B_wout, B_OT[t][k]], [B_ps[bk]])
                copy("dve" if dc % 4 == 3 else "act", y3[:, dc, :], psb[bk][:], [B_ps[bk]], [B_y3[dc]])
            return y3, B_y3

        norm_pipeline(get_y, GG1(l), (GS2(l), 24, l), False, ahead=True)

    def cidx_of(t):
        return 0 if t < 4 else 1

    for t in range(NT):
        stage_x0(t)
    wad0 = [wv(8192 + i * 2048, 2048, BF16).rearrange("p (k f) -> p k f", k=KC) for i in range(2)]
    for pc in range(12):
        i = pc % 2
        Bw = B_tmpf[4 * i:4 * i + 4]
        S.dma("pool", wad0[i], w_ada[0, :, pc * 512:(pc + 1) * 512].rearrange("(k p) f -> p k f", p=P), [], Bw, B_wad[i])
        for jj in range(4):
            j = pc * 4 + jj
            bk = bankA()
            for k in range(KC):
                mm(psb[bk][:, 0:2], wad0[i][:, k, jj * P:(jj + 1) * P], scb[:, k, :], k == 0, k == KC - 1,
                   Bw + [B_sc], [B_ps[bk]])
            tt("dve", mod[:, 0, j, :], psb[bk][:, 0:2],
               vT[:, R_BADA + j:R_BADA + j + 1].to_broadcast([P, 2]), ALU.add, [B_ps[bk], B_vT], [B_mod])
    layer_vecs(0)
    for t in range(NT):
        i = t % 2
        load_x0(t, i)
        store_x(t, i)
        prenorm(xa[i], B_xa[i], GS1(0), 0, 0, cidx_of(t), R0[:, :, t * TT:(t + 1) * TT], B_R0[t])

    for l in range(nl):
        domix = (mix == "all") or (mix == "odd" and l % 2 == 1) or (mix == "even" and l % 2 == 0)
        if domix:
            S.barrier()
            if l % 2 == 1:
                mla(l)
                S.barrier()
                c1(l, w_out_odd[l // 2])
            else:
                even(l)
                S.barrier()
                c1(l, w_out_even[l // 2])
        else:
            for t in range(NT):
                i = t % 2
                load_x(t, i)
                prenorm(xa[i], B_xa[i], GS2(l), 24, l, cidx_of(t), R0[:, :, t * TT:(t + 1) * TT], B_R0[t])
        S.barrier()
        mlp(l)
        if l + 1 < nl:
            while ada_st.setdefault(l + 1, {"dma": 0, "mm": 0})["mm"] < 48:
                ada_mm(l + 1)
            layer_vecs(l + 1)
        S.barrier()
        def get_y2(t):
            return yacc[:, :, t * TT:(t + 1) * TT], B_Y[t]

        if l + 1 < nl:
            norm_pipeline(get_y2, GG2(l), (GS1(l + 1), 0, l + 1), False)
        else:
            norm_pipeline(get_y2, GG2(l), None, True)

    S.barrier()
    S.final_wait("sp", list(S.dma_bufs))

    with nc.Block() as block:
        @block.tensor
        def _(e):
            S.emit("pe", e)

        @block.scalar
        def _(e):
            S.emit("act", e)

        @block.vector
        def _(e):
            S.emit("dve", e)

        @block.gpsimd
        def _(e):
            S.emit("pool", e)

        @block.sync
        def _(e):
            S.emit("sp", e)
    stack.close()
    return nc


def _pack_vecs(inp, b):
    v = np.zeros((R_TOT, P), np.float32)
    v[R_BADA:R_BADA + 192] = inp["b_ada"].reshape(192, P)
    v[R_NMIXPRE:R_NMIXPRE + 32] = inp["norm_mix_pre"].reshape(32, P)
    v[R_NMIXPOST:R_NMIXPOST + 32] = inp["norm_mix_post"].reshape(32, P)
    v[R_NMLPPRE:R_NMLPPRE + 32] = inp["norm_mlp_pre"].reshape(32, P)
    v[R_NMLPPOST:R_NMLPPOST + 32] = inp["norm_mlp_post"].reshape(32, P)
    v[R_COND:R_COND + 8] = inp["c"][b].reshape(8, P)
    v[R_COND + 8:R_COND + 16] = inp["c_ctx"].reshape(8, P)
    v[R_QAN:R_QAN + 4] = inp["q_a_norm"].reshape(4, P)
    v[R_KVAN:R_KVAN + 4] = inp["kv_a_norm"].reshape(4, P)
    v[R_GLAN:R_GLAN + 2] = inp["gla_norm"].reshape(2, P)
    v[R_BGK:R_BGK + 8] = inp["b_gk2"].reshape(8, P)
    return v


def kernel(**inputs):
    inp = {k: np.asarray(v) for k, v in inputs.items()}
    ncores = int(os.environ.get("K_NCORES", "8"))
    nl = int(os.environ.get("K_NL", str(DEPTH)))
    mix = os.environ.get("K_MIX", "all")
    nc = build(nl=nl, mix=mix)
    identf = np.eye(P, dtype=np.float32)
    tpos = np.arange(LS)
    inv = 10000.0 ** (-np.arange(16, dtype=np.float64) / 16.0)
    ang = np.concatenate([(tpos // 64)[:, None] * inv[None, :], (tpos % 64)[:, None] * inv[None, :]], axis=1)
    dd = np.arange(P) % 64
    jj = np.arange(64, dtype=np.float32)
    iot = np.broadcast_to(np.stack([jj, jj + 1, 63 - jj, 64 - jj])[None], (P, 4, 64)).astype(np.float32)
    pj = (np.arange(P) % 64)[:, None]
    cc = np.arange(256)[None, :]
    ii = cc % 64
    dirb = (cc // 64) % 2
    m4 = np.where(dirb == 0, ii >= pj, ii <= pj).astype(np.float32)
    resetm = np.broadcast_to((np.arange(TT) % 64 != 0).astype(np.float32)[None], (P, TT))
    rd = inp["ret_decay"]
    rdec = np.zeros((P, 8), np.float32)
    for i_ in range(2):
        for pg_ in range(2):
            for d_ in range(2):
                for p_ in range(P):
                    rdec[p_, i_ * 4 + pg_ * 2 + d_] = rd[i_, d_, 2 * pg_ + p_ // 64]
    ropeC = np.cos(ang[:, dd % 32]).T.astype(np.float32)
    ropeS = (np.sin(ang[:, dd % 32]).T * np.where(dd < 32, -1.0, 1.0)[:, None]).astype(np.float32)
    in_maps = []
    for b in range(ncores):
        xin = np.concatenate([inp["x_sample"][b], inp["x_prompt"][2 * b], inp["x_prompt"][2 * b + 1]], axis=0)
        in_maps.append({
            "xin": np.ascontiguousarray(xin, dtype=np.float32),
            "vecs": _pack_vecs(inp, b),
            "identf": identf,
            "w_ada": inp["w_ada"], "w_mlp1": inp["w_mlp1"], "w_mlp2": inp["w_mlp2"],
            "w_in_odd": inp["w_in_odd"], "w_q_b": inp["w_q_b"], "w_kv_b": inp["w_kv_b"],
            "w_out_odd": inp["w_out_odd"], "w_out_even": inp["w_out_even"],
            "cckv": np.ascontiguousarray(inp["cache_ckv"][b]), "ckpe": np.ascontiguousarray(inp["cache_kpe"][b]),
            "ropeC": np.ascontiguousarray(ropeC), "ropeS": np.ascontiguousarray(ropeS),
            "w_in_even": inp["w_in_even"], "w_gk2": inp["w_gk2"],
            "sgla": np.ascontiguousarray(inp["state_gla"][b]), "sret": np.ascontiguousarray(inp["state_ret"][b]),
            "rdec": rdec, "iot": np.ascontiguousarray(iot), "m4": np.ascontiguousarray(m4),
            "resetm": np.ascontiguousarray(resetm),
        })
    res = run_bass_kernel_spmd(nc, in_maps, core_ids=list(range(ncores)))
    y_prompt = np.zeros((16, LP, D), np.float32)
    y_sample = np.zeros((8, LS, D), np.float32)
    new_ckv = np.zeros((16, 2, LP, 256), np.float32)
    new_kpe = np.zeros((16, 2, LP, 64), np.float32)
    new_gla = np.zeros((16, 2, 2, 4, 64, 128), np.float32)
    new_ret = np.zeros((16, 2, 2, 4, 64, 128), np.float32)
    for b in range(ncores):
        r = res.results[b]
        ysb = r["ys"]
        y_sample[b] = ysb[0:LS]
        y_prompt[2 * b] = ysb[LS:LS + LP]
        y_prompt[2 * b + 1] = ysb[LS + LP:LS + 2 * LP]
        new_ckv[2 * b:2 * b + 2] = r["ockv"]
        new_kpe[2 * b:2 * b + 2] = r["okpe"]
        if "ogla" in r:
            new_gla[2 * b:2 * b + 2] = r["ogla"]
            new_ret[2 * b:2 * b + 2] = r["oret"]
    return y_prompt, y_sample, new_ckv, new_kpe, new_gla, new_ret
```
